# Optimizing a Trainium2 kernel written in Bass

```python
import math
import jax, jax.numpy as jnp
from jax import lax
import numpy as np

D_MODEL = 1024
BATCH = 8
SEQ = 4096
DEPTH = 2

N_MIXERS = 2
EXPAND = 2
D_INNER = EXPAND * D_MODEL
NORM_EPS = 1e-6
S5_GROUP = 16
S5_GROUPS = D_INNER // S5_GROUP
S5_STATE = 64
S5_CHUNK = 128
DT_MIN = 1e-3
DT_MAX = 1e-1
GDN_HEADS = 8
GDN_DK = D_MODEL // GDN_HEADS
GDN_DV = D_INNER // GDN_HEADS
GDN_CONV = 4
GDN_CHUNK = 64
GDN_QK = GDN_HEADS * GDN_DK
GDN_CONV_CH = 2 * GDN_QK + D_INNER
GDN_PROJ = GDN_CONV_CH + D_INNER + 2 * GDN_HEADS
N_S5 = (DEPTH + 1) // 2
N_GDN = DEPTH // 2

kernel_name = "hybrid_s5_gated_deltanet_adaln"

F32 = jnp.float32


def rms_norm(x, w):
    xf = x.astype(F32)
    y = xf * lax.rsqrt(jnp.mean(xf * xf, axis=-1, keepdims=True) + NORM_EPS) * w.astype(F32)
    return y.astype(x.dtype)


def l2norm(x):
    return x * lax.rsqrt(jnp.sum(x * x, axis=-1, keepdims=True) + NORM_EPS)


def _diag_combine(e1, e2):
    a1r, a1i, b1r, b1i = e1
    a2r, a2i, b2r, b2i = e2
    return (a2r * a1r - a2i * a1i,
            a2r * a1i + a2i * a1r,
            a2r * b1r - a2i * b1i + b2r,
            a2r * b1i + a2i * b1r + b2i)


def s5_mixer(h, w_in, lam_re, lam_im, log_dt, b_re, b_im, c_re, c_im, d_skip, w_glu, w_out):
    bsz, seqlen, _ = h.shape
    u, z = jnp.split(h @ w_in, 2, axis=-1)
    u = u.astype(F32)
    lam_re = lam_re.astype(F32); lam_im = lam_im.astype(F32)
    b_re = b_re.astype(F32); b_im = b_im.astype(F32)
    c_re = c_re.astype(F32); c_im = c_im.astype(F32)
    dt = jnp.exp(log_dt.astype(F32))[:, None]
    mag = jnp.exp(lam_re * dt)
    ab_re = mag * jnp.cos(lam_im * dt)
    ab_im = mag * jnp.sin(lam_im * dt)
    den = lam_re * lam_re + lam_im * lam_im
    nr = ab_re - 1.0
    ni = ab_im
    q_re = (nr * lam_re + ni * lam_im) / den
    q_im = (ni * lam_re - nr * lam_im) / den
    bb_re = q_re[..., None] * b_re - q_im[..., None] * b_im
    bb_im = q_re[..., None] * b_im + q_im[..., None] * b_re

    n_chunks = seqlen // S5_CHUNK
    u_chunks = u.reshape(bsz, n_chunks, S5_CHUNK, S5_GROUPS, S5_GROUP).transpose(1, 0, 2, 3, 4)
    a_re = jnp.broadcast_to(ab_re, (bsz, S5_CHUNK, S5_GROUPS, S5_STATE))
    a_im = jnp.broadcast_to(ab_im, (bsz, S5_CHUNK, S5_GROUPS, S5_STATE))

    def chunk_step(carry, u_c):
        s_re, s_im = carry
        bu_re = jnp.einsum('bcgm,gpm->bcgp', u_c, bb_re)
        bu_im = jnp.einsum('bcgm,gpm->bcgp', u_c, bb_im)
        bu_re = bu_re.at[:, 0].add(ab_re * s_re - ab_im * s_im)
        bu_im = bu_im.at[:, 0].add(ab_re * s_im + ab_im * s_re)
        _, _, x_re, x_im = lax.associative_scan(_diag_combine, (a_re, a_im, bu_re, bu_im), axis=1)
        y_c = (jnp.einsum('bcgp,gmp->bcgm', x_re, c_re)
               - jnp.einsum('bcgp,gmp->bcgm', x_im, c_im))
        return (x_re[:, -1], x_im[:, -1]), y_c

    init = (jnp.zeros((bsz, S5_GROUPS, S5_STATE), F32), jnp.zeros((bsz, S5_GROUPS, S5_STATE), F32))
    _, y = lax.scan(chunk_step, init, u_chunks)
    y = y.transpose(1, 0, 2, 3, 4).reshape(bsz, seqlen, D_INNER) + d_skip.astype(F32) * u
    y = jax.nn.gelu(y)
    y = y * jax.nn.sigmoid(y @ w_glu.astype(F32))
    y = y.astype(h.dtype) * jax.nn.silu(z)
    return y @ w_out


def _chunk_gated_delta_rule(q, k, v, beta, g):
    bsz, seqlen, nh, dk = q.shape
    dv = v.shape[-1]
    C = GDN_CHUNK
    nc = seqlen // C

    def to_chunks(t):
        return t.reshape((bsz, nc, C, nh) + t.shape[3:]).swapaxes(2, 3)

    q, k, v, beta, g = (to_chunks(t) for t in (q, k, v, beta, g))
    gc = jnp.cumsum(g, axis=-1)
    causal = jnp.tril(jnp.ones((C, C), bool))
    strict = jnp.tril(jnp.ones((C, C), bool), -1)
    decay = jnp.exp(jnp.where(causal, gc[..., :, None] - gc[..., None, :], -jnp.inf))
    kk = jnp.einsum('bnhid,bnhjd->bnhij', k, k)
    a_mat = jnp.where(strict, beta[..., None] * kk * decay, 0.0)
    lhs = a_mat + jnp.eye(C, dtype=F32)
    rhs_w = (beta * jnp.exp(gc))[..., None] * k
    rhs_u = beta[..., None] * v
    w = lax.linalg.triangular_solve(lhs, rhs_w, left_side=True, lower=True, unit_diagonal=True)
    u = lax.linalg.triangular_solve(lhs, rhs_u, left_side=True, lower=True, unit_diagonal=True)
    qk = jnp.einsum('bnhid,bnhjd->bnhij', q, k) * decay
    q_dec = q * jnp.exp(gc)[..., None]
    g_last = gc[..., -1]
    k_dec = k * jnp.exp(g_last[..., None] - gc)[..., None]

    def step(state, xs):
        q_c, w_c, u_c, qk_c, k_c, gl_c = xs
        v_new = u_c - jnp.einsum('bhcd,bhde->bhce', w_c, state)
        o_c = (jnp.einsum('bhcd,bhde->bhce', q_c, state)
               + jnp.einsum('bhij,bhje->bhie', qk_c, v_new))
        state = (jnp.exp(gl_c)[..., None, None] * state
                 + jnp.einsum('bhcd,bhce->bhde', k_c, v_new))
        return state, o_c

    xs = tuple(jnp.moveaxis(t, 1, 0) for t in (q_dec, w, u, qk, k_dec, g_last))
    init = jnp.zeros((bsz, nh, dk, dv), F32)
    _, o = lax.scan(step, init, xs)
    return o.transpose(1, 0, 3, 2, 4).reshape(bsz, seqlen, nh, dv)


def gdn_mixer(h, w_in, conv_w, a_log, dt_bias, norm_w, w_out):
    bsz, seqlen, _ = h.shape
    proj = h @ w_in
    qkv, z, b_logit, a_logit = jnp.split(
        proj, [GDN_CONV_CH, GDN_CONV_CH + D_INNER, GDN_CONV_CH + D_INNER + GDN_HEADS], axis=-1)
    qkv = lax.conv_general_dilated(
        qkv.astype(F32), conv_w.astype(F32)[:, None, :], (1,), [(GDN_CONV - 1, 0)],
        dimension_numbers=('NWC', 'WIO', 'NWC'), feature_group_count=GDN_CONV_CH)
    qkv = jax.nn.silu(qkv)
    q, k, v = jnp.split(qkv, [GDN_QK, 2 * GDN_QK], axis=-1)
    q = l2norm(q.reshape(bsz, seqlen, GDN_HEADS, GDN_DK)) * (GDN_DK ** -0.5)
    k = l2norm(k.reshape(bsz, seqlen, GDN_HEADS, GDN_DK))
    v = v.reshape(bsz, seqlen, GDN_HEADS, GDN_DV)
    beta = jax.nn.sigmoid(b_logit.astype(F32))
    g = -jnp.exp(a_log.astype(F32)) * jax.nn.softplus(a_logit.astype(F32) + dt_bias.astype(F32))
    o = _chunk_gated_delta_rule(q, k, v, beta, g)
    o = o * lax.rsqrt(jnp.mean(o * o, axis=-1, keepdims=True) + NORM_EPS) * norm_w.astype(F32)
    o = o.reshape(bsz, seqlen, D_INNER).astype(h.dtype) * jax.nn.silu(z)
    return o @ w_out


def setup_inputs(seed: int = 0) -> dict:
    key = jax.random.key(seed)
    ks = iter(jax.random.split(key, 32))

    def nrm(shape, scale):
        return scale * jax.random.normal(next(ks), shape, F32)

    s5_lambda_im = (math.pi * jnp.broadcast_to(jnp.arange(S5_STATE, dtype=F32), (N_S5, S5_GROUPS, S5_STATE))
                    + nrm((N_S5, S5_GROUPS, S5_STATE), 0.01))
    gdn_dt = jnp.exp(jax.random.uniform(next(ks), (N_GDN, GDN_HEADS), F32, math.log(DT_MIN), math.log(DT_MAX)))
    return {
        "x": nrm((BATCH, SEQ, D_MODEL), 1.0),
        "c": nrm((BATCH, D_MODEL), 1.0),
        "ada_w": nrm((DEPTH, D_MODEL, 3 * D_MODEL), D_MODEL ** -0.5),
        "ada_b": nrm((DEPTH, 3 * D_MODEL), 0.02),
        "norm_w": 1.0 + nrm((DEPTH, D_MODEL), 0.02),
        "s5_w_in": nrm((N_S5, D_MODEL, 2 * D_INNER), D_MODEL ** -0.5),
        "s5_lambda_re": -0.5 + nrm((N_S5, S5_GROUPS, S5_STATE), 0.01),
        "s5_lambda_im": s5_lambda_im,
        "s5_log_dt": jax.random.uniform(next(ks), (N_S5, S5_GROUPS), F32, math.log(DT_MIN), math.log(DT_MAX)),
        "s5_b_re": nrm((N_S5, S5_GROUPS, S5_STATE, S5_GROUP), (2 * S5_GROUP) ** -0.5),
        "s5_b_im": nrm((N_S5, S5_GROUPS, S5_STATE, S5_GROUP), (2 * S5_GROUP) ** -0.5),
        "s5_c_re": nrm((N_S5, S5_GROUPS, S5_GROUP, S5_STATE), S5_STATE ** -0.5),
        "s5_c_im": nrm((N_S5, S5_GROUPS, S5_GROUP, S5_STATE), S5_STATE ** -0.5),
        "s5_d": nrm((N_S5, D_INNER), 1.0),
        "s5_w_glu": nrm((N_S5, D_INNER, D_INNER), D_INNER ** -0.5),
        "s5_w_out": nrm((N_S5, D_INNER, D_MODEL), D_INNER ** -0.5),
        "gdn_w_in": nrm((N_GDN, D_MODEL, GDN_PROJ), D_MODEL ** -0.5),
        "gdn_conv_w": nrm((N_GDN, GDN_CONV, GDN_CONV_CH), GDN_CONV ** -0.5),
        "gdn_a_log": jnp.log(jax.random.uniform(next(ks), (N_GDN, GDN_HEADS), F32, 1.0, 16.0)),
        "gdn_dt_bias": gdn_dt + jnp.log(-jnp.expm1(-gdn_dt)),
        "gdn_norm_w": 1.0 + nrm((N_GDN, GDN_DV), 0.02),
        "gdn_w_out": nrm((N_GDN, D_INNER, D_MODEL), D_INNER ** -0.5),
        "final_norm_w": 1.0 + nrm((D_MODEL,), 0.02),
    }


def reference(x, c, ada_w, ada_b, norm_w, s5_w_in, s5_lambda_re, s5_lambda_im, s5_log_dt,
              s5_b_re, s5_b_im, s5_c_re, s5_c_im, s5_d, s5_w_glu, s5_w_out,
              gdn_w_in, gdn_conv_w, gdn_a_log, gdn_dt_bias, gdn_norm_w, gdn_w_out, final_norm_w):
    c_act = jax.nn.silu(c)
    for layer in range(DEPTH):
        mod = c_act @ ada_w[layer] + ada_b[layer]
        shift, scale, gate = jnp.split(mod, 3, axis=-1)
        h = rms_norm(x, norm_w[layer]) * (1.0 + scale[:, None, :]) + shift[:, None, :]
        j = layer // N_MIXERS
        if layer % N_MIXERS == 0:
            y = s5_mixer(h, s5_w_in[j], s5_lambda_re[j], s5_lambda_im[j], s5_log_dt[j],
                         s5_b_re[j], s5_b_im[j], s5_c_re[j], s5_c_im[j], s5_d[j],
                         s5_w_glu[j], s5_w_out[j])
        else:
            y = gdn_mixer(h, gdn_w_in[j], gdn_conv_w[j], gdn_a_log[j], gdn_dt_bias[j],
                          gdn_norm_w[j], gdn_w_out[j])
        x = x + (gate[:, None, :] * y).astype(x.dtype)
    return rms_norm(x, final_norm_w)
```

```python
import math
from contextlib import ExitStack
import numpy as np
import concourse.bass as bass
import concourse.mybir as mybir
from concourse.bass_utils import run_bass_kernel_spmd

F32 = mybir.dt.float32
BF16 = mybir.dt.bfloat16
AF = mybir.ActivationFunctionType
ALU = mybir.AluOpType
AX = mybir.AxisListType

D = 1024
E = 2048
T = 1024
NT = T // 128
EPS = 1e-6
ENGS = ("pe", "act", "dve", "pool", "sp")
import os
LOCKSTEP = int(os.environ.get("K_LOCKSTEP", "1"))
OFFSET = int(os.environ.get("K_OFFSET", "0"))


def _dtsize(dt):
    return 4 if dt == F32 else 2


class Buf:
    __slots__ = ("name", "w", "r")

    def __init__(self, name):
        self.name = name
        self.w = None
        self.r = []


class Prog:
    def __init__(self):
        self.ops = {e: [] for e in ENGS}
        self.seen = {e: {} for e in ENGS}
        self.dcnt = {}
        self.last = {e: None for e in ENGS}
        self.pending = {e: [] for e in ENGS}
        self.out_events = []

    def _waits(self, eng, r, w):
        raw = []
        oth = []
        for b in r:
            if b.w is not None:
                raw.append(b.w)
        for b in w:
            if b.w is not None:
                oth.append(b.w)
            oth.extend(b.r)
        d = {}
        for ev in raw:
            st, pos = ev
            if st == eng and eng == "pe":
                continue
            d[st] = max(d.get(st, 0), pos)
        for ev in oth:
            st, pos = ev
            if st == eng:
                continue
            d[st] = max(d.get(st, 0), pos)
        for st, pos in self.pending[eng]:
            d[st] = max(d.get(st, 0), pos)
        self.pending[eng] = []
        out = []
        for st, pos in d.items():
            if self.seen[eng].get(st, 0) >= pos:
                continue
            self.seen[eng][st] = pos
            out.append((st, pos))
        return out

    def op(self, eng, fn, r=(), w=()):
        waits = self._waits(eng, r, w)
        pos = len(self.ops[eng]) + 1
        ev = (eng, pos)
        self.ops[eng].append((waits, fn, ev, None))
        self.last[eng] = ev
        for b in r:
            b.r.append(ev)
        for b in w:
            b.w = ev
            b.r = []
        return ev

    def dma(self, q, fn, sem, r=(), w=(), is_out=False):
        waits = self._waits(q, r, w)
        self.dcnt[sem] = self.dcnt.get(sem, 0) + 16
        ev = ("dma:" + sem, self.dcnt[sem])
        self.ops[q].append((waits, fn, None, sem))
        for b in r:
            b.r.append(ev)
        for b in w:
            b.w = ev
            b.r = []
        if is_out:
            self.out_events.append(ev)
        return ev

    def alias(self, new, olds):
        for o in olds:
            if o.w is not None:
                new.r.append(o.w)
            new.r.extend(o.r)

    def barrier(self):
        evs = [self.last[e] for e in ENGS if self.last[e] is not None]
        evs += [("dma:" + s, c) for s, c in self.dcnt.items()]
        for e in ENGS:
            for ev in evs:
                if ev[0] == e and e == "pe":
                    continue
                self.pending[e].append(ev)

    def emit(self, nc, es):
        needed = {e: set() for e in ENGS}
        for e in ENGS:
            for waits, fn, ev, dsem in self.ops[e]:
                for st, pos in waits:
                    if not st.startswith("dma:"):
                        needed[st].add(pos)
        rank = {e: {p: i + 1 for i, p in enumerate(sorted(needed[e]))} for e in ENGS}
        esem = {e: es.enter_context(nc.semaphore("s_" + e)) for e in ENGS}
        dsem = {s: es.enter_context(nc.semaphore("d_" + s)) for s in self.dcnt}
        fin = {}
        for st, pos in self.out_events:
            fin[st] = max(fin.get(st, 0), pos)
        block = es.enter_context(nc.Block())

        def replay(e, eng):
            for waits, fn, ev, ds in self.ops[e]:
                for st, pos in waits:
                    if st.startswith("dma:"):
                        eng.wait_ge(dsem[st[4:]], pos)
                    else:
                        eng.wait_ge(esem[st], rank[st][pos])
                ins = fn(eng)
                if ds is not None:
                    ins.then_inc(dsem[ds], 16)
                elif ev[1] in needed[e]:
                    ins.then_inc(esem[e], 1)
            if e == "sp":
                for st, pos in fin.items():
                    eng.wait_ge(dsem[st[4:]], pos)

        @block.tensor
        def _(t):
            replay("pe", t)

        @block.scalar
        def _(t):
            replay("act", t)

        @block.vector
        def _(t):
            replay("dve", t)

        @block.gpsimd
        def _(t):
            replay("pool", t)

        @block.sync
        def _(t):
            replay("sp", t)


class Arena:
    def __init__(self, nc, es, words):
        self.t = es.enter_context(nc.sbuf_tensor("arena", [128, words], F32))
        self.words = words
        self.off = 0

    def alloc(self, nbytes):
        w = (nbytes + 31) // 32 * 8
        off = self.off
        self.off += w
        assert self.off <= self.words, ("arena overflow", self.off, self.words)
        return off

    def view(self, off, shape, dt, p0=0):
        n = 1
        for s in shape[1:]:
            n *= s
        words = n * _dtsize(dt) // 4
        ap = self.t[p0:p0 + shape[0], off:off + words]
        if dt != F32:
            ap = ap.bitcast(dt)
        if len(shape) == 3:
            ap = ap.rearrange("p (a b) -> p a b", a=shape[1], b=shape[2])
        elif len(shape) == 4:
            ap = ap.rearrange("p (a b c) -> p a b c", a=shape[1], b=shape[2], c=shape[3])
        return ap

    def new(self, shape, dt):
        n = 1
        for s in shape[1:]:
            n *= s
        off = self.alloc(n * _dtsize(dt))
        return self.view(off, shape, dt)


C_ID = 0
C_SEL = 128
C_EVEN = 192
C_ODD = 193
C_NEGM = 194
C_SMASK = 322
C_TRI = 450
C_CH0 = 578
C_CH1 = 706
C_N = 834


def make_consts():
    c = np.zeros((128, C_N), np.float32)
    c[:, C_ID:C_ID + 128] = np.eye(128, dtype=np.float32)
    g = np.arange(128)
    c[g, C_SEL + g // 2] = 1.0
    c[:, C_EVEN] = (g % 2 == 0)
    c[:, C_ODD] = (g % 2 == 1)
    j = g[:, None]
    i = g[None, :]
    same = (j // 64) == (i // 64)
    c[:, C_NEGM:C_NEGM + 128] = np.where(same & (j <= i), 0.0, -1.0e4)
    c[:, C_SMASK:C_SMASK + 128] = (same & (j < i))
    c[:, C_TRI:C_TRI + 128] = (same & (j <= i))
    c[:, C_CH0:C_CH0 + 128] = (j < 64) & (i >= 0)
    c[:, C_CH1:C_CH1 + 128] = (j >= 64) & (i >= 0)
    return c


def build(seq, stage=2):
    assert seq % T == 0
    NCH = seq // T
    nc = bass.Bass("TRN2", target_bir_lowering=False)
    P = Prog()
    es = ExitStack()

    def din(name, shape):
        return nc.dram_tensor(name, shape, F32, kind="ExternalInput").ap()

    x_d = din("x", [seq, D])
    c_d = din("c", [8, 128])
    adaw_d = din("ada_w", [2, D, 3 * D])
    adab_d = din("ada_b", [1, 2 * 3 * D])
    nw_d = din("norm_w", [16, 128])
    s5win_d = din("s5_w_in", [D, 2 * E])
    lamr_d = din("s5_lambda_re", [128, 64])
    lami_d = din("s5_lambda_im", [128, 64])
    ldt_d = din("s5_log_dt", [128, 1])
    bre_d = din("s5_b_re", [128, 1024])
    bim_d = din("s5_b_im", [128, 1024])
    cre_d = din("s5_c_re", [128, 1024])
    cim_d = din("s5_c_im", [128, 1024])
    dsk_d = din("s5_d", [128, 16])
    wglu_d = din("s5_w_glu", [E, E])
    s5wo_d = din("s5_w_out", [E, D])
    gwin_d = din("gdn_w_in", [D, 6160])
    convw_d = din("gdn_conv_w", [128, 128])
    alog_d = din("gdn_a_log", [1, 8])
    dtb_d = din("gdn_dt_bias", [1, 8])
    gnw_d = din("gdn_norm_w", [1, 256])
    gwo_d = din("gdn_w_out", [E, D])
    fnw_d = din("final_norm_w", [1, D])
    cst_d = din("cst", [128, C_N])
    out_d = nc.dram_tensor("out", [seq, D], F32, kind="ExternalOutput").ap()
    t0_d = nc.dram_tensor("t0_scr", [128, 128, 128], BF16).ap()
    wv_d = nc.dram_tensor("wv_scr", [128, 128, 128], BF16).ap()
    wc_d = nc.dram_tensor("wc_scr", [2, 64, 64, 2, 128], BF16).ap()

    A = Arena(nc, es, 53208)
    psb = [es.enter_context(nc.psum_tensor("ps%d" % i, [128, 512], F32)) for i in range(8)]
    PS = [Buf("ps%d" % i) for i in range(8)]
    psrr = [0]

    pspool = [None]
    pscur = {}

    def nextps():
        if pspool[0] is None:
            i = psrr[0] % 8
            psrr[0] += 1
        else:
            key = tuple(pspool[0])
            c = pscur.get(key, 0)
            i = pspool[0][c % len(pspool[0])]
            pscur[key] = c + 1
        assert PS[i].w is None or PS[i].w[0] != "pe" or len(PS[i].r) > 0, ("psum bank still live", i)
        return i

    def mm(out, lhsT, rhs, start, stop, r, w):
        P.op("pe", lambda e: e.matmul(out, lhsT=lhsT, rhs=rhs, start=start, stop=stop), r=r, w=w)

    def act(out, in_, func, r, w, scale=1.0, bias=0.0, accum=None):
        if accum is None:
            P.op("act", lambda e: e.activation(out=out, in_=in_, func=func, bias=bias, scale=scale), r=r, w=w)
        else:
            P.op("act", lambda e: e.activation(out=out, in_=in_, func=func, bias=bias, scale=scale,
                                               accum_out=accum), r=r, w=w)

    def tt(eng, out, in0, in1, op, r, w):
        P.op(eng, lambda e: e.tensor_tensor(out=out, in0=in0, in1=in1, op=op), r=r, w=w)

    def ts(eng, out, in0, s1, op0, r, w, s2=None, op1=None):
        if op1 is None:
            P.op(eng, lambda e: e.tensor_scalar(out=out, in0=in0, scalar1=s1, scalar2=None, op0=op0), r=r, w=w)
        else:
            P.op(eng, lambda e: e.tensor_scalar(out=out, in0=in0, scalar1=s1, scalar2=s2, op0=op0, op1=op1),
                 r=r, w=w)

    def stt(out, in0, scalar, in1, op0, op1, r, w):
        P.op("dve", lambda e: e.scalar_tensor_tensor(out=out, in0=in0, scalar=scalar, in1=in1, op0=op0, op1=op1),
             r=r, w=w)

    def cp(eng, out, in_, r, w):
        if eng == "act":
            P.op("act", lambda e: e.activation(out=out, in_=in_, func=AF.Copy), r=r, w=w)
        else:
            P.op(eng, lambda e: e.tensor_copy(out=out, in_=in_), r=r, w=w)

    def memset(eng, ap, val, w):
        P.op(eng, lambda e: e.memset(ap, val), w=w)

    def recip(out, in_, r, w):
        P.op("dve", lambda e: e.reciprocal(out=out, in_=in_), r=r, w=w)

    def dma(q, out, in_, sem, r=(), w=(), is_out=False):
        P.dma(q, lambda e: e.dma_start(out=out, in_=in_), sem, r=r, w=w, is_out=is_out)

    CST = A.new([128, C_N], F32)
    B_CST = Buf("cst")
    ident = CST[:, C_ID:C_ID + 128]
    identb = A.new([128, 128], BF16)
    onesf = A.new([128, 128], F32)
    onesb = A.new([128, 128], BF16)
    B_K = Buf("konst")
    XS_OFF = A.alloc(NT * D * 4)
    XS = A.view(XS_OFF, [128, NT, D], F32)
    B_X = [Buf("x%d" % i) for i in range(NT)]
    HT_OFF = A.alloc(8 * T * 2)
    HT = A.view(HT_OFF, [128, 8, T], BF16)
    B_HT = [Buf("ht%d" % i) for i in range(NT)]
    BIG1 = A.alloc(32768)
    BIG2 = A.alloc(32768)
    WOFF = [A.alloc(16384), A.alloc(16384)]
    B_W = [Buf("w0"), Buf("w1")]
    wrr = [0]
    SMALL = A.alloc(8192 + 12288 + 4096)
    GATEB = A.new([128, 2, D], F32)
    FNWB = A.new([128, D], F32)
    WEFF = A.new([128, 2, 8], F32)
    SHIFT = A.new([128, 2, 8], F32)
    B_MOD = Buf("mod")
    AR2 = A.new([128, 2, 64], F32)
    AI2 = A.new([128, 2, 64], F32)
    S3 = [A.new([128, 3, 64], F32), A.new([128, 3, 64], F32)]
    B_S3 = [Buf("s3a"), Buf("s3b")]
    B_AR = Buf("ar")
    STAT = A.new([128, 64], F32)
    B_ST = Buf("stat")
    M1 = A.new([128, 2, 64], F32)
    M2 = A.new([128, 2, 64], F32)
    B_M1 = Buf("m1")
    B_M2 = Buf("m2")
    NTMP = A.new([128, 4, 128], F32)
    B_NTMP = Buf("ntmp")
    XN = [A.new([128, D], BF16), A.new([128, D], BF16)]
    B_XN = [Buf("xn0"), Buf("xn1")]

    def wslot():
        i = wrr[0] % 2
        wrr[0] += 1
        return i

    dma("sp", CST, cst_d, "cst", w=[B_CST])
    cp("dve", identb, ident, r=[B_CST], w=[B_K])
    memset("dve", onesf, 1.0, w=[B_K])
    memset("dve", onesb, 1.0, w=[B_K])
    memset("dve", S3[0], 0.0, w=[B_S3[0]])
    memset("dve", S3[1], 0.0, w=[B_S3[1]])

    so = [BIG2]

    def salloc(shape, dt):
        n = 1
        for s_ in shape[1:]:
            n *= s_
        nb = (n * _dtsize(dt) + 31) // 32 * 8
        v = A.view(so[0], shape, dt)
        so[0] += nb
        assert so[0] <= BIG2 + 8192, "setup scratch overflow"
        return v

    c8 = salloc([8, 128], F32)
    ccol = salloc([128, 8], F32)
    nwr = salloc([16, 128], F32)
    nwc = salloc([128, 16], F32)
    rowb = salloc([1, 512], F32)
    adab = salloc([1, 512], F32)
    scc = salloc([128, 16], F32)
    B_S = Buf("setup_s")
    B_RB = Buf("rowb")
    B_AB0 = Buf("adab")
    dma("sp", c8, c_d, "su1", w=[B_S])
    dma("sp", nwr, nw_d, "su1", w=[B_S])
    dma("sp", FNWB, fnw_d.partition_broadcast(128), "su1", w=[B_MOD])
    B_S.w = ("dma:su1", P.dcnt["su1"])
    B_MOD.w = ("dma:su1", P.dcnt["su1"])
    act(c8, c8, AF.Silu, r=[B_S], w=[B_S])
    i0 = nextps()
    mm(psb[i0][:, 0:8], c8, ident[0:8, 0:8], True, True, r=[B_S, B_CST], w=[PS[i0]])
    cp("dve", ccol, psb[i0][:, 0:8], r=[PS[i0]], w=[B_S])
    i0 = nextps()
    mm(psb[i0][:, 0:16], nwr, ident[0:16, 0:16], True, True, r=[B_S, B_CST], w=[PS[i0]])
    cp("dve", nwc, psb[i0][:, 0:16], r=[PS[i0]], w=[B_S])
    for l in range(2):
        for cb in range(6):
            s = wslot()
            wl = A.view(WOFF[s], [128, 8, 512], F32)
            dma("sp", wl, adaw_d[l, :, cb * 512:(cb + 1) * 512].rearrange("(k p) c -> p k c", p=128), "w%d" % s,
                w=[B_W[s]])
            o = l * 3 * D + cb * 512
            dma("sp", adab, adab_d[:, o:o + 512], "ab", w=[B_AB0])
            i0 = nextps()
            for k in range(8):
                mm(psb[i0][0:1, :], ccol[:, k:k + 1], wl[:, k, :], k == 0, k == 7, r=[B_S, B_W[s]], w=[PS[i0]])
            tt("dve", rowb, psb[i0][0:1, :], adab, ALU.add, r=[PS[i0], B_AB0], w=[B_RB])
            which = cb // 2
            if which < 2:
                i1 = nextps()
                for kk in range(4):
                    mm(psb[i1][:, kk:kk + 1], rowb[0:1, kk * 128:(kk + 1) * 128], onesf[0:1, 0:1], True, True,
                       r=[B_RB, B_K], w=[PS[i1]])
                k0 = (cb % 2) * 4
                if which == 0:
                    cp("dve", SHIFT[:, l, k0:k0 + 4], psb[i1][:, 0:4], r=[PS[i1]], w=[B_MOD])
                else:
                    ts("dve", scc[:, 0:4], psb[i1][:, 0:4], 1.0, ALU.add, r=[PS[i1]], w=[B_S])
                    tt("dve", WEFF[:, l, k0:k0 + 4], scc[:, 0:4], nwc[:, l * 8 + k0:l * 8 + k0 + 4], ALU.mult,
                       r=[B_S], w=[B_MOD])
            else:
                dh = cb % 2
                i1 = nextps()
                mm(psb[i1][:, :], onesf[0:1, :], rowb[0:1, :], True, True, r=[B_RB, B_K], w=[PS[i1]])
                ts("dve", GATEB[:, l, dh * 512:(dh + 1) * 512], psb[i1][:, :], 0.25 if l == 0 else 0.5, ALU.mult,
                   r=[PS[i1]], w=[B_MOD])

    class Reg:
        def __init__(self, base, nbytes):
            self.base = base
            self.off = 0
            self.cap = nbytes // 4

        def new(self, shape, dt):
            n = 1
            for s_ in shape[1:]:
                n *= s_
            nb = (n * _dtsize(dt) + 31) // 32 * 8
            v = A.view(self.base + self.off, shape, dt)
            self.off += nb
            assert self.off <= self.cap, "region overflow"
            return v

    r_small = Reg(SMALL, 8192 + 12288 + 4096)
    r_ht = Reg(HT_OFF, 16384)
    r_b1 = Reg(BIG1 + 4096, 16384)
    T0t = A.view(BIG2, [128, 128, 128], BF16)
    WVt = A.view(WOFF[0], [128, 128, 128], BF16)
    WCt = A.view(XS_OFF, [128, 128, 128], BF16)
    B_T0 = Buf("T0t")
    B_WV = Buf("WVt")
    B_WC = Buf("WCt")
    P.alias(B_T0, [B_S, B_RB, B_AB0])
    P.alias(B_WV, B_W)
    lamr = r_small.new([128, 64], F32)
    lami = r_small.new([128, 64], F32)
    ldt = r_small.new([128, 1], F32)
    dt64 = r_small.new([128, 1], F32)
    ar = r_small.new([128, 64], F32)
    ai = r_small.new([128, 64], F32)
    t1 = r_small.new([128, 64], F32)
    t2 = r_small.new([128, 64], F32)
    t3 = r_small.new([128, 64], F32)
    qre = r_small.new([128, 64], F32)
    qim = r_small.new([128, 64], F32)
    APR = r_small.new([128, 9, 64], F32)
    API = r_small.new([128, 9, 64], F32)
    dsk = r_small.new([128, 16], F32)
    Kd = r_small.new([128, 16, 16], F32)
    Kt = r_small.new([128, 4, 16], F32)
    lre = r_small.new([128, 128], F32)
    bre = r_ht.new([128, 64, 16], F32)
    bim = r_ht.new([128, 64, 16], F32)
    cre = r_ht.new([128, 16, 64], F32)
    cim = r_ht.new([128, 16, 64], F32)
    Gre = r_b1.new([128, 64, 16], F32)
    Gim = r_b1.new([128, 64, 16], F32)
    Gt = r_b1.new([128, 64, 16], F32)
    Gu = r_b1.new([128, 64, 16], F32)
    prodA = A.view(BIG1, [128, 4, 16, 64], F32)
    B_PA = Buf("prodA")
    KtP = r_small.new([128, 2, 16], F32)
    B_PP = Buf("prodP")
    B_KD0 = Buf("kd0")
    B_KD1 = Buf("kd1")
    B_G = Buf("G")
    B_P5 = Buf("s5p")
    for v, d_ in ((lamr, lamr_d), (lami, lami_d), (ldt, ldt_d), (dsk, dsk_d)):
        dma("sp", v, d_, "su4", w=[B_P5])
    dma("sp", bre, bre_d.rearrange("g (p m) -> g p m", m=16), "su4", w=[B_P5])
    dma("sp", bim, bim_d.rearrange("g (p m) -> g p m", m=16), "su4", w=[B_P5])
    dma("sp", cre, cre_d.rearrange("g (m p) -> g m p", p=64), "su4", w=[B_P5])
    dma("sp", cim, cim_d.rearrange("g (m p) -> g m p", p=64), "su4", w=[B_P5])
    cwr = r_small.new([128, 128], F32)
    dma("sp", cwr, convw_d, "su4", w=[B_P5])
    B_P5.w = ("dma:su4", P.dcnt["su4"])

    R5 = [B_P5]
    act(dt64, ldt, AF.Exp, r=R5, w=R5)
    ts("dve", dt64, dt64, 1.0 / 64.0, ALU.mult, r=R5, w=R5)
    act(t1, lamr, AF.Exp, r=R5, w=R5, scale=dt64[:, 0:1])
    act(ai, lami, AF.Sin, r=R5, w=R5, scale=dt64[:, 0:1])
    act(ar, lami, AF.Sin, r=R5, w=R5, scale=dt64[:, 0:1], bias=math.pi / 2)
    tt("dve", ar, ar, t1, ALU.mult, r=R5, w=R5)
    tt("dve", ai, ai, t1, ALU.mult, r=R5, w=R5)
    for _ in range(6):
        tt("dve", t1, ar, ar, ALU.mult, r=R5, w=R5)
        tt("dve", t2, ai, ai, ALU.mult, r=R5, w=R5)
        tt("dve", t3, ar, ai, ALU.mult, r=R5, w=R5)
        tt("dve", ar, t1, t2, ALU.subtract, r=R5, w=R5)
        ts("dve", ai, t3, 2.0, ALU.mult, r=R5, w=R5)
    tt("dve", t1, lamr, lamr, ALU.mult, r=R5, w=R5)
    tt("dve", t2, lami, lami, ALU.mult, r=R5, w=R5)
    tt("dve", t1, t1, t2, ALU.add, r=R5, w=R5)
    recip(t1, t1, r=R5, w=R5)
    ts("dve", t2, ar, -1.0, ALU.add, r=R5, w=R5)
    tt("dve", qre, t2, lamr, ALU.mult, r=R5, w=R5)
    tt("dve", t3, ai, lami, ALU.mult, r=R5, w=R5)
    tt("dve", qre, qre, t3, ALU.add, r=R5, w=R5)
    tt("dve", qre, qre, t1, ALU.mult, r=R5, w=R5)
    tt("dve", qim, ai, lamr, ALU.mult, r=R5, w=R5)
    tt("dve", t3, t2, lami, ALU.mult, r=R5, w=R5)
    tt("dve", qim, qim, t3, ALU.subtract, r=R5, w=R5)
    tt("dve", qim, qim, t1, ALU.mult, r=R5, w=R5)

    def bc_m(v):
        return v.unsqueeze(2).to_broadcast([128, 64, 16])

    tt("dve", Gre, bre, bc_m(qre), ALU.mult, r=R5, w=[B_G])
    tt("dve", Gt, bim, bc_m(qim), ALU.mult, r=R5, w=[B_G])
    tt("dve", Gre, Gre, Gt, ALU.subtract, r=[B_G], w=[B_G])
    tt("dve", Gim, bim, bc_m(qre), ALU.mult, r=R5, w=[B_G])
    tt("dve", Gt, bre, bc_m(qim), ALU.mult, r=R5, w=[B_G])
    tt("dve", Gim, Gim, Gt, ALU.add, r=[B_G], w=[B_G])
    memset("dve", APR[:, 0, :], 1.0, w=R5)
    memset("dve", API[:, 0, :], 0.0, w=R5)
    cp("dve", APR[:, 1, :], ar, r=R5, w=R5)
    cp("dve", API[:, 1, :], ai, r=R5, w=R5)
    for k in range(2, 9):
        tt("dve", t1, APR[:, k - 1, :], ar, ALU.mult, r=R5, w=R5)
        tt("dve", t2, API[:, k - 1, :], ai, ALU.mult, r=R5, w=R5)
        tt("dve", APR[:, k, :], t1, t2, ALU.subtract, r=R5, w=R5)
        tt("dve", t1, APR[:, k - 1, :], ai, ALU.mult, r=R5, w=R5)
        tt("dve", t2, API[:, k - 1, :], ar, ALU.mult, r=R5, w=R5)
        tt("dve", API[:, k, :], t1, t2, ALU.add, r=R5, w=R5)

    memset("pool", T0t, 0.0, w=[B_T0])
    Kd2 = Kd.rearrange("g a b -> g (a b)")
    for d_ in range(8):
        if d_ > 0:
            tt("dve", Gt, Gre, bc_m(ar), ALU.mult, r=[B_G] + R5, w=[B_G])
            tt("dve", Gu, Gim, bc_m(ai), ALU.mult, r=[B_G] + R5, w=[B_G])
            tt("dve", Gt, Gt, Gu, ALU.subtract, r=[B_G], w=[B_G])
            tt("dve", Gu, Gre, bc_m(ai), ALU.mult, r=[B_G] + R5, w=[B_G])
            cp("dve", Gre, Gt, r=[B_G], w=[B_G])
            tt("dve", Gt, Gim, bc_m(ar), ALU.mult, r=[B_G] + R5, w=[B_G])
            tt("dve", Gim, Gt, Gu, ALU.add, r=[B_G], w=[B_G])
        i_ = 7 - d_
        cp("act", WVt[:, i_ * 16:(i_ + 1) * 16, 0:64], Gre.rearrange("g p m -> g m p"), r=[B_G], w=[B_WV])
        cp("act", WVt[:, i_ * 16:(i_ + 1) * 16, 64:128], Gim.rearrange("g p m -> g m p"), r=[B_G], w=[B_WV])
        for sl in range(4):
            msl = slice(sl * 4, sl * 4 + 4)
            gre_b = Gre.rearrange("g p m -> g m p").unsqueeze(1).to_broadcast([128, 4, 16, 64])
            gim_b = Gim.rearrange("g p m -> g m p").unsqueeze(1).to_broadcast([128, 4, 16, 64])
            cre_b = cre[:, msl, :].unsqueeze(2).to_broadcast([128, 4, 16, 64])
            cim_b = cim[:, msl, :].unsqueeze(2).to_broadcast([128, 4, 16, 64])
            tt("dve", prodA, gre_b, cre_b, ALU.mult, r=[B_G] + R5, w=[B_PA])
            P.op("dve", lambda e, o=Kd[:, msl, :], i=prodA: e.tensor_reduce(
                out=o, in_=i, axis=AX.X, op=ALU.add), r=[B_PA], w=[B_KD0])
            tt("dve", prodA, gim_b, cim_b, ALU.mult, r=[B_G] + R5, w=[B_PA])
            P.op("dve", lambda e, o=Kt, i=prodA: e.tensor_reduce(
                out=o, in_=i, axis=AX.X, op=ALU.add), r=[B_PA], w=[B_KD0])
            tt("dve", Kd[:, msl, :], Kd[:, msl, :], Kt, ALU.subtract, r=[B_KD0], w=[B_KD0])
        if d_ == 0:
            tt("dve", Kd2[:, 0:256:17], Kd2[:, 0:256:17], dsk, ALU.add, r=[B_KD0] + R5, w=[B_KD0])
        for i2 in range(8 - d_):
            j2 = i2 + d_
            cp("act", T0t[:, i2 * 16:(i2 + 1) * 16, j2 * 16:(j2 + 1) * 16], Kd.rearrange("g m n -> g n m"),
               r=[B_KD0], w=[B_T0])
    creT = cre.rearrange("g m p -> g p m")
    cimT = cim.rearrange("g m p -> g p m")
    for j_ in range(8):
        pr = bc_m(APR[:, j_ + 1, :])
        pi_ = bc_m(API[:, j_ + 1, :])
        tt("dve", Gt, creT, pr, ALU.mult, r=R5 + [B_G], w=[B_G])
        tt("dve", Gu, cimT, pi_, ALU.mult, r=R5 + [B_G], w=[B_G])
        tt("dve", WCt[:, 0:64, j_ * 16:(j_ + 1) * 16], Gt, Gu, ALU.subtract, r=[B_G], w=[B_WC])
        tt("dve", Gt, creT, pi_, ALU.mult, r=R5 + [B_G], w=[B_G])
        tt("dve", Gu, cimT, pr, ALU.mult, r=R5 + [B_G], w=[B_G])
        tt("dve", Gt, Gt, Gu, ALU.add, r=[B_G], w=[B_G])
        ts("dve", WCt[:, 64:128, j_ * 16:(j_ + 1) * 16], Gt, -1.0, ALU.mult, r=[B_G], w=[B_WC])
    for r8 in range(8):
        rs_ = slice(r8 * 16, (r8 + 1) * 16)
        dma("sp", t0_d[rs_, :, :].rearrange("r g c -> g r c"), T0t[:, rs_, :], "scr", r=[B_T0])
        dma("sp", wv_d[rs_, :, :].rearrange("r g c -> g r c"), WVt[:, rs_, :], "scr", r=[B_WV])
    for g2 in range(2):
        for e_ in range(2):
            for ph in range(2):
                psl = slice(ph * 32, (ph + 1) * 32)
                dma("sp", wc_d[g2, psl, :, e_, :].rearrange("p gp c -> gp p c"),
                    WCt[g2:128:2, e_ * 64 + ph * 32:e_ * 64 + (ph + 1) * 32, :], "scr", r=[B_WC])
    B_SCR = Buf("scr")
    B_SCR.w = ("dma:scr", P.dcnt["scr"])
    for src, dst_is_im in ((APR[:, 8, :], False), (API[:, 8, :], True)):
        ts("dve", lre[:, 0:64], src, CST[:, C_EVEN:C_EVEN + 1], ALU.mult, r=R5 + [B_CST], w=[B_G])
        ts("dve", lre[:, 64:128], src, CST[:, C_ODD:C_ODD + 1], ALU.mult, r=R5 + [B_CST], w=[B_G])
        i0 = nextps()
        mm(psb[i0][:, 0:64], lre, CST[:, C_SEL:C_SEL + 64], True, True, r=[B_G, B_CST], w=[PS[i0]])
        if not dst_is_im:
            cp("dve", AR2[:, 0, :], psb[i0][:, 0:64], r=[PS[i0]], w=[B_AR])
            cp("dve", AR2[:, 1, :], psb[i0][:, 0:64], r=[PS[i0]], w=[B_AR])
        else:
            ts("dve", AI2[:, 0, :], psb[i0][:, 0:64], -1.0, ALU.mult, r=[PS[i0]], w=[B_AR])
            cp("dve", AI2[:, 1, :], psb[i0][:, 0:64], r=[PS[i0]], w=[B_AR])

    GS = A.new([128, 8, 256], F32)
    B_GS = [Buf("gs%d" % i) for i in range(8)]
    HIST = A.new([128, 32, 3], F32)
    B_HIST = Buf("hist")
    CW = A.new([128, 4, 32], F32)
    GNWB = A.new([128, 256], F32)
    NEXPA = A.new([128, 8], F32)
    DTBB = A.new([128, 8], F32)
    NEGONES = A.new([128, 128], F32)
    WAB = A.new([128, 8, 16], BF16)
    GSM = A.new([128, 12, 64], F32)
    B_GSM = Buf("gsm")
    B_GC = Buf("gconst")
    memset("pool", GS, 0.0, w=B_GS)
    memset("pool", HIST, 0.0, w=[B_HIST])
    memset("pool", NEGONES, -1.0, w=[B_GC])
    B_WAB = Buf("wab")
    dma("pool", WAB, gwin_d[:, 6144:6160].rearrange("(k p) c -> p k c", p=128), "suw", w=[B_WAB])
    dma("sp", GNWB, gnw_d.partition_broadcast(128), "su9", w=[B_GC])
    dma("sp", NEXPA, alog_d.partition_broadcast(128), "su9", w=[B_GC])
    dma("sp", DTBB, dtb_d.partition_broadcast(128), "su9", w=[B_GC])
    B_GC.w = ("dma:su9", P.dcnt["su9"])
    i0 = nextps()
    mm(psb[i0][:, 0:128], cwr, ident, True, True, r=[B_P5, B_CST], w=[PS[i0]])
    cp("dve", CW.rearrange("p k t -> p (k t)"), psb[i0][:, 0:128], r=[PS[i0]], w=[B_GC])
    act(NEXPA, NEXPA, AF.Exp, r=[B_GC], w=[B_GC])
    ts("dve", NEXPA, NEXPA, -1.0, ALU.mult, r=[B_GC], w=[B_GC])

    P.barrier()

    UY = A.view(BIG1, [128, 128, 128], BF16)
    B_UY = [Buf("uy%d" % i) for i in range(16)]
    VS = A.view(BIG2, [128, 2, 64, 128], BF16)
    B_VS = Buf("vs")
    YFM = A.view(BIG2, [128, 16, T], BF16)
    B_YFM = [Buf("yfm%d" % i) for i in range(16)]
    Y2 = A.view(BIG1, [128, 16, T], BF16)
    B_Y2 = [Buf("y2%d" % i) for i in range(16)]
    ABAT = A.view(SMALL, [128, 32, 8, 16], BF16)
    B_AB = Buf("abat")
    TBLO = [SMALL + 2048, SMALL + 2048 + 1536]
    T0s = [A.view(o, [128, 8, 128], BF16) for o in TBLO]
    WVs = [A.view(o + 512, [128, 8, 128], BF16) for o in TBLO]
    WCt2 = [A.view(o + 1024, [128, 4, 2, 128], BF16) for o in TBLO]
    B_TB = [Buf("tbl0"), Buf("tbl1")]
    tbrr = [0]
    TMPO = SMALL + 2048 + 3072
    TMP1 = A.view(TMPO, [128, 512], F32)
    TMP2 = A.view(TMPO + 512, [128, 512], BF16)
    TMP3 = A.view(TMPO + 768, [128, 512], BF16)
    B_T1 = Buf("tmp1")
    B_T2 = Buf("tmp2")
    B_T3 = Buf("tmp3")
    B_VSn = [Buf("vsn0"), Buf("vsn1")]
    evq = [0]

    def evac_eng():
        evq[0] += 1
        return "act" if evq[0] % 2 else "dve"

    def norm_transpose(layer):
        for tti in range(NT):
            xt = XS[:, tti, :]
            s = tti % 2
            junk = XN[s]
            act(junk, xt, AF.Square, r=[B_X[tti]], w=[B_XN[s], B_ST], accum=STAT[:, tti:tti + 1])
            act(STAT[:, 8 + tti:9 + tti], STAT[:, tti:tti + 1], AF.Ln, r=[B_ST], w=[B_ST], scale=1.0 / D,
                bias=EPS)
            act(STAT[:, 16 + tti:17 + tti], STAT[:, 8 + tti:9 + tti], AF.Exp, r=[B_ST], w=[B_ST], scale=-0.5)
            ts("dve", XN[s], xt, STAT[:, 16 + tti:17 + tti], ALU.mult, r=[B_X[tti], B_ST], w=[B_XN[s]])
            for half in range(2):
                i0 = nextps()
                for kk in range(4):
                    k = half * 4 + kk
                    mm(psb[i0][:, kk * 128:(kk + 1) * 128], XN[s][:, k * 128:(k + 1) * 128], identb, True, True,
                       r=[B_XN[s], B_K], w=[PS[i0]])
                if half == 0:
                    for kk in range(4):
                        k = half * 4 + kk
                        act(HT[:, k, tti * 128:(tti + 1) * 128], psb[i0][:, kk * 128:(kk + 1) * 128], AF.Identity,
                            r=[PS[i0], B_MOD], w=[B_HT[tti]], scale=WEFF[:, layer, k:k + 1],
                            bias=SHIFT[:, layer, k:k + 1])
                else:
                    hv = HT[:, 4:8, tti * 128:(tti + 1) * 128]
                    tt("dve", NTMP, psb[i0][:, :].rearrange("p (k n) -> p k n", n=128),
                       WEFF[:, layer, 4:8].unsqueeze(2).to_broadcast([128, 4, 128]), ALU.mult, r=[PS[i0], B_MOD],
                       w=[B_NTMP])
                    tt("dve", hv, NTMP, SHIFT[:, layer, 4:8].unsqueeze(2).to_broadcast([128, 4, 128]), ALU.add,
                       r=[B_NTMP, B_MOD], w=[B_HT[tti]])

    def wload(view, src, s):
        dma("pool", view, src, "w%d" % s, w=[B_W[s]])

    def s5_layer(ch):
        P.alias(B_VS, B_YFM)
        for b in B_UY:
            P.alias(b, B_Y2)
        norm_transpose(0)
        for fb in range(4):
            s = wslot()
            Wb = A.view(WOFF[s], [128, 8, 512], BF16)
            wload(Wb, s5win_d[:, fb * 512:(fb + 1) * 512].rearrange("(k p) c -> p k c", p=128), s)
            for i_ in range(8):
                i0 = nextps()
                for k in range(8):
                    mm(psb[i0][:, :], HT[:, k, i_:T:8], Wb[:, k, :], k == 0, k == 7, r=B_HT + [B_W[s]], w=[PS[i0]])
                cp(evac_eng(), ABAT[:, :, i_, :], psb[i0][:, :].rearrange("p (g m) -> p g m", m=16), r=[PS[i0]],
                   w=[B_AB])
            for q4 in range(8):
                i0 = nextps()
                for gg in range(4):
                    gl = q4 * 4 + gg
                    mm(psb[i0][:, gg * 128:(gg + 1) * 128], ABAT[:, gl, :, :].rearrange("p i m -> p (i m)"),
                       identb, True, True, r=[B_AB, B_K], w=[PS[i0]])
                g0 = fb * 32 + q4 * 4
                cp(evac_eng(), UY[:, g0:g0 + 4, :], psb[i0][:, :].rearrange("p (g n) -> p g n", n=128), r=[PS[i0]],
                   w=[B_UY[g0 // 8]])
            for fcl in range(4):
                fc = fb * 4 + fcl
                tb = tbrr[0] % 2
                tbrr[0] += 1
                dma("sp", WVs[tb], wv_d[:, fc * 8:(fc + 1) * 8, :], "tbl%d" % tb, r=[B_SCR],
                    w=[B_TB[tb]])
                ire = nextps()
                iim = nextps()
                for gg in range(8):
                    g = fc * 8 + gg
                    p0 = (g % 2) * 64
                    pr = gg // 2
                    for (ii, c0) in ((ire, 0), (iim, 64)):
                        mm(psb[ii][p0:p0 + 64, pr * 128:(pr + 1) * 128], WVs[tb][:, gg, c0:c0 + 64], UY[:, g, :],
                           True, True, r=[B_TB[tb], B_UY[g // 8]], w=[PS[ii]])
                cp(evac_eng(), VS[:, 0, fc * 4:(fc + 1) * 4, :], psb[ire][:, :].rearrange("p (a n) -> p a n", n=128),
                   r=[PS[ire]], w=[B_VS])
                cp(evac_eng(), VS[:, 1, fc * 4:(fc + 1) * 4, :], psb[iim][:, :].rearrange("p (a n) -> p a n", n=128),
                   r=[PS[iim]], w=[B_VS])
        cur = 0
        for n in range(128):
            So = S3[cur]
            Sn = S3[1 - cur]
            Bo = B_S3[cur]
            Bn = B_S3[1 - cur]
            Bv = B_VSn[n % 2]
            tt("dve", M1, So[:, 0:2, :], AR2, ALU.mult, r=[Bo, B_AR], w=[B_M1])
            tt("dve", M2, So[:, 1::-1, :], AI2, ALU.mult, r=[Bo, B_AR], w=[B_M2])
            tt("dve", M1, M1, M2, ALU.add, r=[B_M1, B_M2], w=[B_M1])
            tt("dve", Sn[:, 0:2, :], M1, VS[:, :, :, n], ALU.add, r=[B_M1, B_VS, Bv], w=[Bn])
            cp("act", VS[:, :, :, n], So[:, 0:2, :], r=[Bo], w=[Bv])
            cur = 1 - cur
        for fc in range(16):
            tb = tbrr[0] % 2
            tbrr[0] += 1
            dma("sp", T0s[tb], t0_d[:, fc * 8:(fc + 1) * 8, :], "tbl%d" % tb, r=[B_SCR],
                w=[B_TB[tb]])
            for g2 in range(2):
                dma("sp", WCt2[tb][g2 * 64:(g2 + 1) * 64, :, :, :], wc_d[g2, :, fc * 4:(fc + 1) * 4, :, :],
                    "tbl%d" % tb, r=[B_SCR], w=[B_TB[tb]])
            banks = []
            for hb in range(2):
                i0 = nextps()
                banks.append(i0)
                for gg in range(4):
                    gl = hb * 4 + gg
                    g = fc * 8 + gl
                    p0 = (g % 2) * 64
                    pr = g // 2
                    o = psb[i0][:, gg * 128:(gg + 1) * 128]
                    RV = [B_VS, B_VSn[0], B_VSn[1], B_TB[tb]]
                    mm(o, UY[:, g, :], T0s[tb][:, gl, :], True, False, r=[B_UY[fc], B_TB[tb]], w=[PS[i0]])
                    mm(o, VS[p0:p0 + 64, 0, pr, :], WCt2[tb][p0:p0 + 64, gl // 2, 0, :], False, False, r=RV,
                       w=[PS[i0]])
                    mm(o, VS[p0:p0 + 64, 1, pr, :], WCt2[tb][p0:p0 + 64, gl // 2, 1, :], False, True, r=RV,
                       w=[PS[i0]])
            yav = UY[:, fc * 8:(fc + 1) * 8, :].rearrange("p g c -> p (g c)").rearrange(
                "p (j g m) -> p g j m", j=8, g=8, m=16)
            for hb in range(2):
                i0 = banks[hb]
                act(yav[:, hb * 4:(hb + 1) * 4, :, :],
                    psb[i0][:, :].rearrange("p (g j m) -> p g j m", g=4, j=8, m=16), AF.Gelu, r=[PS[i0]],
                    w=[B_UY[fc]])
        for b in B_YFM:
            P.alias(b, [B_VS, B_VSn[0], B_VSn[1]])
        for fc in range(16):
            for jh in range(2):
                i0 = nextps()
                for jj in range(4):
                    j_ = jh * 4 + jj
                    mm(psb[i0][:, jj * 128:(jj + 1) * 128], UY[:, fc * 8 + j_, :], identb,
                       True, True, r=[B_UY[fc], B_K], w=[PS[i0]])
                cp(evac_eng(), YFM[:, fc, :].rearrange("p (n j) -> p j n", j=8)[:, jh * 4:(jh + 1) * 4, :],
                   psb[i0][:, :].rearrange("p (j n) -> p j n", n=128), r=[PS[i0]], w=[B_YFM[fc]])
        for b in B_Y2:
            P.alias(b, B_UY)
        for fb in range(4):
            s = wslot()
            Wg = A.view(WOFF[s], [128, 16, 512], BF16)
            wload(Wg, wglu_d[:, fb * 512:(fb + 1) * 512].rearrange("(k p) c -> p k c", p=128), s)
            s2_ = wslot()
            Wz = A.view(WOFF[s2_], [128, 8, 512], BF16)
            wload(Wz, s5win_d[:, E + fb * 512:E + (fb + 1) * 512].rearrange("(k p) c -> p k c", p=128), s2_)
            for ftl in range(4):
                ft = fb * 4 + ftl
                for th in range(2):
                    tsl = slice(th * 512, (th + 1) * 512)
                    ig = nextps()
                    for k in range(16):
                        mm(psb[ig][:, :], Wg[:, k, ftl * 128:(ftl + 1) * 128], YFM[:, k, tsl], k == 0, k == 15,
                           r=[B_W[s]] + B_YFM, w=[PS[ig]])
                    iz = nextps()
                    for k in range(8):
                        mm(psb[iz][:, :], Wz[:, k, ftl * 128:(ftl + 1) * 128], HT[:, k, tsl], k == 0, k == 7,
                           r=[B_W[s2_]] + B_HT, w=[PS[iz]])
                    act(TMP2, psb[ig][:, :], AF.Tanh, r=[PS[ig]], w=[B_T2], scale=0.5)
                    act(TMP3, psb[iz][:, :], AF.Tanh, r=[PS[iz]], w=[B_T3], scale=0.5)
                    stt(TMP2, TMP2, 1.0, YFM[:, ft, tsl], ALU.add, ALU.mult, r=[B_T2, B_YFM[ft]], w=[B_T2])
                    stt(TMP3, TMP3, 1.0, psb[iz][:, :], ALU.add, ALU.mult, r=[B_T3, PS[iz]], w=[B_T3])
                    tt("dve", Y2[:, ft, tsl], TMP2, TMP3, ALU.mult, r=[B_T2, B_T3], w=[B_Y2[ft]])
        out_proj(s5wo_d, 0)

    def out_proj(w_d, layer):
        for dh in range(2):
            s = wslot()
            Wo = A.view(WOFF[s], [128, 16, 512], BF16)
            wload(Wo, w_d[:, dh * 512:(dh + 1) * 512].rearrange("(k p) c -> p k c", p=128), s)
            for tti in range(NT):
                i0 = nextps()
                for k in range(16):
                    mm(psb[i0][:, :], Y2[:, k, tti * 128:(tti + 1) * 128], Wo[:, k, :], k == 0, k == 15,
                       r=B_Y2 + [B_W[s]], w=[PS[i0]])
                tt("dve", TMP1, psb[i0][:, :], GATEB[:, layer, dh * 512:(dh + 1) * 512], ALU.mult,
                   r=[PS[i0], B_MOD], w=[B_T1])
                tt("dve", XS[:, tti, dh * 512:(dh + 1) * 512], XS[:, tti, dh * 512:(dh + 1) * 512], TMP1, ALU.add,
                   r=[B_T1, B_X[tti]], w=[B_X[tti]])

    def final_norm(ch):
        t0 = ch * T
        for tti in range(NT):
            xt = XS[:, tti, :]
            s = tti % 2
            act(XN[s], xt, AF.Square, r=[B_X[tti]], w=[B_XN[s], B_ST], accum=STAT[:, tti:tti + 1])
            act(STAT[:, 8 + tti:9 + tti], STAT[:, tti:tti + 1], AF.Ln, r=[B_ST], w=[B_ST], scale=1.0 / D, bias=EPS)
            act(STAT[:, 16 + tti:17 + tti], STAT[:, 8 + tti:9 + tti], AF.Exp, r=[B_ST], w=[B_ST], scale=-0.5)
            stt(xt, xt, STAT[:, 16 + tti:17 + tti], FNWB, ALU.mult, ALU.mult, r=[B_X[tti], B_ST, B_MOD],
                w=[B_X[tti]])
            dma("sp", out_d[t0 + tti * 128:t0 + (tti + 1) * 128, :], XS[:, tti, :], "o%d" % tti, r=[B_X[tti]],
                is_out=True)
            if ch + 1 < NCH:
                t1_ = t0 + T
                dma("sp", XS[:, tti, :], x_d[t1_ + tti * 128:t1_ + (tti + 1) * 128, :], "x%d" % tti, w=[B_X[tti]])


    TH = 256
    NTH = TH // 128
    NQ = T // TH

    def mkstream(i):
        rgs = Reg(BIG2 + i * 4096, 16384)
        rss = Reg(SMALL + i * 1280, 5120)
        d = {}
        d["QN"] = rgs.new([128, TH], BF16)
        d["KN"] = rgs.new([128, TH], BF16)
        d["VT"] = rgs.new([128, NTH, 256], BF16)
        d["KD"] = rgs.new([128, NTH, 128], BF16)
        d["MT"] = rgs.new([128, NTH, 128], BF16)
        d["QKM"] = rgs.new([128, NTH, 128], BF16)
        d["SZ"] = rgs.new([128, 2, TH], BF16)
        d["VA"] = rgs.new([128, 2, TH], BF16)
        d["DG"] = rgs.new([128, NTH, 128], F32)
        d["BM"] = rgs.new([128, NTH, 128], BF16)
        ov = rgs.base + rgs.off
        rd_ = Reg(ov, 8192)
        for nm in ("ET", "NP", "NT", "P0", "P1", "PT0", "PT1", "RF"):
            d[nm] = rd_.new([128, NTH, 128], F32)
        ra_ = Reg(ov, 8192)
        for j in range(2):
            d["PRE%d" % j] = ra_.new([128, TH + 8], BF16)
            d["CT%d" % j] = ra_.new([128, TH], F32)
            d["TA%d" % j] = ra_.new([128, TH], F32)
            d["SQ%d" % j] = ra_.new([128, TH], BF16)
        d["DW"] = A.view(SMALL + 2560 + i * 1024, [128, 16, 128], BF16)
        d["RM"] = rss.new([128, 256], BF16)
        d["VN"] = rss.new([128, 256], BF16)
        d["OV"] = rss.new([128, 256], F32)
        d["OT0"] = rss.new([128, 256], F32)
        d["OT1"] = rss.new([128, 256], F32)
        d["ON"] = rss.new([128, 256], BF16)
        d["SB"] = rss.new([128, 256], BF16)
        for k_ in list(d.keys()):
            d["B_" + k_] = Buf("%s_%d" % (k_, i))
        d["RA_B"] = [d["B_" + n] for n in ("PRE0", "CT0", "TA0", "SQ0", "PRE1", "CT1", "TA1", "SQ1")]
        d["RD_B"] = [d["B_" + n] for n in ("ET", "NP", "NT", "P0", "P1", "PT0", "PT1", "RF")]
        d["sid"] = i
        return d

    GST = [mkstream(0), mkstream(1)]
    B_WS = [[Buf("ws%d_%d" % (i, j)) for j in range(4)] for i in range(2)]

    def gsm(i):
        return GSM[:, i, :].rearrange("p (a h) -> p a h", h=8)

    BETA, GG, GC, GLT, EGC, NEGEGC, EKD, EGL0, EGL1, GTMP = [gsm(i) for i in range(10)]
    ABR = GSM[:, 10:12, :].rearrange("p a c -> p (a c)").rearrange("p (t c) -> p t c", c=16)

    def vn(ap):
        return ap.rearrange("p (a i) -> p a i", i=128)

    idn = ident.unsqueeze(1).to_broadcast([128, NTH, 128])
    smn = CST[:, C_SMASK:C_SMASK + 128].unsqueeze(1).to_broadcast([128, NTH, 128])
    ngn = CST[:, C_NEGM:C_NEGM + 128].unsqueeze(1).to_broadcast([128, NTH, 128])
    WH = {}
    W2 = NTH * 128

    def gdn_gates():
        i0 = nextps()
        for tti in range(NT):
            for k in range(8):
                mm(psb[i0][:, tti * 16:(tti + 1) * 16], HT[:, k, tti * 128:(tti + 1) * 128], WAB[:, k, :], k == 0,
                   k == 7, r=[B_HT[tti], B_WAB], w=[PS[i0]])
        G = [B_GSM]
        cp("dve", ABR, psb[i0][:, 0:128].rearrange("p (t c) -> p t c", c=16), r=[PS[i0]], w=G)
        act(BETA, ABR[:, :, 0:8], AF.Tanh, r=G, w=G, scale=0.5)
        ts("dve", BETA, BETA, 0.5, ALU.mult, r=G, w=G, s2=0.5, op1=ALU.add)
        tt("dve", GTMP, ABR[:, :, 8:16], DTBB.unsqueeze(1).to_broadcast([128, 8, 8]), ALU.add, r=G + [B_GC], w=G)
        act(GTMP, GTMP, AF.Exp, r=G, w=G)
        act(GTMP, GTMP, AF.Ln, r=G, w=G, bias=1.0)
        tt("dve", GG, GTMP, NEXPA.unsqueeze(1).to_broadcast([128, 8, 8]), ALU.mult, r=G + [B_GC], w=G)
        i0 = nextps()
        grhs = GSM[:, 1, :]
        mm(psb[i0][:, 0:64], CST[:, C_TRI:C_TRI + 128], grhs, True, True, r=G + [B_CST], w=[PS[i0]])
        mm(psb[i0][0:64, 64:128], CST[:, C_CH0:C_CH0 + 64], grhs, True, True, r=G + [B_CST], w=[PS[i0]])
        mm(psb[i0][64:128, 64:128], CST[:, C_CH1 + 64:C_CH1 + 128], grhs, True, True, r=G + [B_CST], w=[PS[i0]])
        mm(psb[i0][:, 128:192], CST[:, C_CH0:C_CH0 + 128], grhs, True, True, r=G + [B_CST], w=[PS[i0]])
        mm(psb[i0][:, 192:256], CST[:, C_CH1:C_CH1 + 128], grhs, True, True, r=G + [B_CST], w=[PS[i0]])
        cp("dve", GSM[:, 2, :], psb[i0][:, 0:64], r=[PS[i0]], w=G)
        act(GSM[:, 4, :], psb[i0][:, 0:64], AF.Exp, r=[PS[i0]], w=G)
        ts("dve", GSM[:, 5, :], GSM[:, 4, :], -1.0, ALU.mult, r=G, w=G)
        tt("dve", GSM[:, 3, :], psb[i0][:, 64:128], GSM[:, 2, :], ALU.subtract, r=[PS[i0]] + G, w=G)
        act(GSM[:, 6, :], GSM[:, 3, :], AF.Exp, r=G, w=G)
        act(GSM[:, 7, :], psb[i0][:, 128:192], AF.Exp, r=[PS[i0]], w=G)
        act(GSM[:, 8, :], psb[i0][:, 192:256], AF.Exp, r=[PS[i0]], w=G)

    def gdn_prep(h, qi, S):
        sid = S["sid"]
        tb0 = qi * NTH
        hs = slice(qi * TH, (qi + 1) * TH)
        if qi == 0:
            s = sid
            Wh = A.view(WOFF[s], [128, 8, 768], BF16)
            c0 = 0
            for j, (src0, n_) in enumerate(((h * 128, 128), (1024 + h * 128, 128), (2048 + h * 256, 256),
                                           (4096 + h * 256, 256))):
                dma("pool", Wh[:, :, c0:c0 + n_], gwin_d[:, src0:src0 + n_].rearrange("(k p) c -> p k c", p=128),
                    "ws%d_%d" % (s, j), w=[B_WS[s][j]])
                c0 += n_
            WH[h] = (Wh, s)
            yield
            for jt, tidx_ in enumerate((h, 8 + h, 16 + 2 * h, 17 + 2 * h)):
                for tap in range(4):
                    ts("pool", S["DW"][:, jt * 4 + tap, :], ident, CW[:, tap, tidx_:tidx_ + 1], ALU.mult,
                       r=[B_CST, B_GC], w=[S["B_DW"]])
                yield
        Wh, s = WH[h]
        for b in S["RA_B"]:
            P.alias(b, S["RD_B"])
        pairs = ((("q", 0, h, 0, 0), ("k", 128, 8 + h, 0, 1)),
                 (("v", 256, 16 + 2 * h, 0, 2), ("v", 384, 17 + 2 * h, 1, 2)))
        for pair in pairs:
            pi = []
            for j, (name, wc0, tidx, sub, seg) in enumerate(pair):
                i0 = nextps()
                for k in range(8):
                    mm(psb[i0][:, 0:TH], Wh[:, k, wc0:wc0 + 128], HT[:, k, hs], k == 0, k == 7,
                       r=[B_WS[s][seg]] + B_HT, w=[PS[i0]])
                pi.append(i0)
                yield
            pcs = []
            for j, (name, wc0, tidx, sub, seg) in enumerate(pair):
                PRE, B_PRE = S["PRE%d" % j], S["B_PRE%d" % j]
                cp("act", PRE[:, 3:3 + TH], psb[pi[j]][:, 0:TH], r=[PS[pi[j]]], w=[B_PRE])
                cp("pool", PRE[:, 0:3], HIST[:, tidx, :], r=[B_HIST], w=[B_PRE])
                yield
            for j, (name, wc0, tidx, sub, seg) in enumerate(pair):
                PRE, B_PRE = S["PRE%d" % j], S["B_PRE%d" % j]
                jt = (0 if name == "q" else 1) if name != "v" else 2 + sub
                ic = nextps()
                for tap in range(4):
                    mm(psb[ic][:, 0:TH], S["DW"][:, jt * 4 + tap, :], PRE[:, tap:tap + TH], tap == 0, tap == 3,
                       r=[S["B_DW"], B_PRE], w=[PS[ic]])
                cp("pool", HIST[:, tidx, :], PRE[:, TH:TH + 3], r=[B_PRE], w=[B_HIST])
                pcs.append(ic)
                yield
            for j, (name, wc0, tidx, sub, seg) in enumerate(pair):
                CT, B_CT = S["CT%d" % j], S["B_CT%d" % j]
                TA, B_TA = S["TA%d" % j], S["B_TA%d" % j]
                act(TA, psb[pcs[j]][:, 0:TH], AF.Tanh, r=[PS[pcs[j]]], w=[B_TA], scale=0.5)
                yield
                if name == "v":
                    stt(S["VA"][:, sub, :], TA, 1.0, psb[pcs[j]][:, 0:TH], ALU.add, ALU.mult,
                        r=[B_TA, PS[pcs[j]]], w=[S["B_VA"]])
                else:
                    stt(CT, TA, 1.0, psb[pcs[j]][:, 0:TH], ALU.add, ALU.mult, r=[B_TA, PS[pcs[j]]], w=[B_CT])
                yield
            if pair[0][0] == "v":
                continue
            ips = []
            for j in range(2):
                CT, B_CT = S["CT%d" % j], S["B_CT%d" % j]
                SQ, B_SQ = S["SQ%d" % j], S["B_SQ%d" % j]
                act(SQ, CT, AF.Square, r=[B_CT], w=[B_SQ])
                i0 = nextps()
                mm(psb[i0][:, 0:TH], onesb, SQ, True, True, r=[B_SQ, B_K], w=[PS[i0]])
                ips.append(i0)
                yield
            for j in range(2):
                CT, B_CT = S["CT%d" % j], S["B_CT%d" % j]
                TA, B_TA = S["TA%d" % j], S["B_TA%d" % j]
                act(TA, psb[ips[j]][:, 0:TH], AF.Ln, r=[PS[ips[j]]], w=[B_TA], bias=4.0 * EPS)
                act(TA, TA, AF.Exp, r=[B_TA], w=[B_TA], scale=-0.5)
                if j == 0:
                    stt(S["QN"], CT, 128.0 ** -0.5, TA, ALU.mult, ALU.mult, r=[B_CT, B_TA], w=[S["B_QN"]])
                else:
                    stt(S["KN"], CT, 1.0, TA, ALU.mult, ALU.mult, r=[B_CT, B_TA], w=[S["B_KN"]])
                yield
        for sub in range(2):
            wc0 = 512 + sub * 128
            i0 = nextps()
            for k in range(8):
                mm(psb[i0][:, 0:TH], Wh[:, k, wc0:wc0 + 128], HT[:, k, hs], k == 0, k == 7,
                   r=[B_WS[s][3]] + B_HT, w=[PS[i0]])
            TA, B_TA = S["TA%d" % sub], S["B_TA%d" % sub]
            act(TA, psb[i0][:, 0:TH], AF.Tanh, r=[PS[i0]], w=[B_TA], scale=0.5)
            stt(S["SZ"][:, sub, :], TA, 1.0, psb[i0][:, 0:TH], ALU.add, ALU.mult, r=[B_TA, PS[i0]], w=[S["B_SZ"]])
            yield
        KN_ = S["KN"]
        QN_ = S["QN"]
        i0 = nextps()
        for tl in range(NTH):
            mm(psb[i0][:, tl * 128:(tl + 1) * 128], KN_[:, tl * 128:(tl + 1) * 128], identb, True, True,
               r=[S["B_KN"], B_K], w=[PS[i0]])
        tt("dve", S["KD"], vn(psb[i0][:, 0:W2]), EKD[:, tb0:tb0 + NTH, h:h + 1].to_broadcast([128, NTH, 128]),
           ALU.mult, r=[PS[i0], B_GSM], w=[S["B_KD"]])
        yield
        i0 = nextps()
        for tl in range(NTH):
            for vt in range(2):
                o = tl * 256 + vt * 128
                mm(psb[i0][:, o:o + 128], S["VA"][:, vt, tl * 128:(tl + 1) * 128], identb, True, True,
                   r=[S["B_VA"], B_K], w=[PS[i0]])
        act(S["VT"], psb[i0][:, :].rearrange("p (a e) -> p a e", e=256), AF.Identity, r=[PS[i0]], w=[S["B_VT"]],
            scale=0.5)
        yield
        for b in S["RD_B"]:
            P.alias(b, S["RA_B"])
        DG, BM = S["DG"], S["BM"]
        tt("dve", DG, idn, GC[:, tb0:tb0 + NTH, h:h + 1].to_broadcast([128, NTH, 128]), ALU.mult,
           r=[B_CST, B_GSM], w=[S["B_DG"]])
        tt("dve", BM, smn, BETA[:, tb0:tb0 + NTH, h:h + 1].to_broadcast([128, NTH, 128]), ALU.mult,
           r=[B_CST, B_GSM], w=[S["B_BM"]])
        yield
        ikk = nextps()
        iqk = nextps()
        igd = nextps()
        for tl in range(NTH):
            cs = slice(tl * 128, (tl + 1) * 128)
            mm(psb[ikk][:, cs], KN_[:, cs], KN_[:, cs], True, True, r=[S["B_KN"]], w=[PS[ikk]])
            mm(psb[iqk][:, cs], KN_[:, cs], QN_[:, cs], True, True, r=[S["B_KN"], S["B_QN"]], w=[PS[iqk]])
            mm(psb[igd][:, cs], onesf, DG[:, tl, :], True, False, r=[B_K, S["B_DG"]], w=[PS[igd]])
            mm(psb[igd][:, cs], DG[:, tl, :], NEGONES, False, False, r=[S["B_DG"], B_GC], w=[PS[igd]])
            mm(psb[igd][:, cs], ident, CST[:, C_NEGM:C_NEGM + 128], False, True, r=[B_CST], w=[PS[igd]])
        yield
        ET, NP_, NT_, RF = S["ET"], S["NP"], S["NT"], S["RF"]
        Pb = [S["P0"], S["P1"]]
        PTb = [S["PT0"], S["PT1"]]
        B_PB = [S["B_P0"], S["B_P1"]]
        B_PTB = [S["B_PT0"], S["B_PT1"]]
        act(ET, vn(psb[igd][:, 0:W2]), AF.Exp, r=[PS[igd]], w=[S["B_ET"]])
        yield
        tt("dve", S["QKM"], vn(psb[iqk][:, 0:W2]), ET, ALU.mult, r=[PS[iqk], S["B_ET"]], w=[S["B_QKM"]])
        tt("dve", NP_, vn(psb[ikk][:, 0:W2]), ET, ALU.mult, r=[PS[ikk], S["B_ET"]], w=[S["B_NP"]])
        yield
        tt("dve", NP_, NP_, BM, ALU.mult, r=[S["B_NP"], S["B_BM"]], w=[S["B_NP"]])
        yield
        i0 = nextps()
        for tl in range(NTH):
            o = slice(tl * 128, (tl + 1) * 128)
            mm(psb[i0][:, o], NP_[:, tl, :], ident, True, True, r=[S["B_NP"], B_CST], w=[PS[i0]])
        cp("act", NT_, vn(psb[i0][:, 0:W2]), r=[PS[i0]], w=[S["B_NT"]])
        tt("dve", RF, idn, NP_, ALU.subtract, r=[B_CST, S["B_NP"]], w=[S["B_RF"]])
        yield
        ip = nextps()
        ipt = nextps()
        for tl in range(NTH):
            o = slice(tl * 128, (tl + 1) * 128)
            mm(psb[ip][:, o], NT_[:, tl, :], NP_[:, tl, :], True, True, r=[S["B_NT"], S["B_NP"]], w=[PS[ip]])
            mm(psb[ipt][:, o], NP_[:, tl, :], NT_[:, tl, :], True, True, r=[S["B_NT"], S["B_NP"]], w=[PS[ipt]])
        cur = 0
        cp("act", Pb[0], vn(psb[ip][:, 0:W2]), r=[PS[ip]], w=[B_PB[0]])
        cp("dve", PTb[0], vn(psb[ipt][:, 0:W2]), r=[PS[ipt]], w=[B_PTB[0]])
        yield
        for k in range(1, 6):
            iq = nextps()
            for tl in range(NTH):
                o = slice(tl * 128, (tl + 1) * 128)
                mm(psb[iq][:, o], ident, RF[:, tl, :], True, False, r=[B_CST, S["B_RF"]], w=[PS[iq]])
                mm(psb[iq][:, o], PTb[cur][:, tl, :], RF[:, tl, :], False, True, r=[B_PTB[cur], S["B_RF"]],
                   w=[PS[iq]])
            if k < 5:
                ip = nextps()
                ipt = nextps()
                for tl in range(NTH):
                    o = slice(tl * 128, (tl + 1) * 128)
                    mm(psb[ip][:, o], PTb[cur][:, tl, :], Pb[cur][:, tl, :], True, True,
                       r=[B_PTB[cur], B_PB[cur]], w=[PS[ip]])
                    mm(psb[ipt][:, o], Pb[cur][:, tl, :], PTb[cur][:, tl, :], True, True,
                       r=[B_PTB[cur], B_PB[cur]], w=[PS[ipt]])
            yield
            if k < 5:
                cp("act", RF, vn(psb[iq][:, 0:W2]), r=[PS[iq]], w=[S["B_RF"]])
            if k < 5:
                cp("act", Pb[1 - cur], vn(psb[ip][:, 0:W2]), r=[PS[ip]], w=[B_PB[1 - cur]])
                cp("dve", PTb[1 - cur], vn(psb[ipt][:, 0:W2]), r=[PS[ipt]], w=[B_PTB[1 - cur]])
                cur = 1 - cur
            else:
                cp("act", S["MT"], vn(psb[iq][:, 0:W2]), r=[PS[iq]], w=[S["B_MT"]])
            yield

    def gdn_loop(h, qi, S):
        sid = S["sid"]
        tb0 = qi * NTH
        KN_, QN_, VT_, KD_, MT_, QKM_, SZ_ = S["KN"], S["QN"], S["VT"], S["KD"], S["MT"], S["QKM"], S["SZ"]
        RM, VN, OV, ON, SB = S["RM"], S["VN"], S["OV"], S["ON"], S["SB"]
        B_RM, B_VN, B_OV, B_ON, B_SB = S["B_RM"], S["B_VN"], S["B_OV"], S["B_ON"], S["B_SB"]
        st0 = 32 + 4 * sid
        cp("act", SB, GS[:, h, :], r=[B_GS[h]], w=[B_SB])
        for cidx in range(2 * NTH):
            tl = cidx // 2
            t_ = tb0 + tl
            hf = cidx % 2
            p0 = hf * 64
            cs = slice(tl * 128 + p0, tl * 128 + p0 + 64)
            pp = slice(p0, p0 + 64)
            iks = nextps()
            mm(psb[iks][pp, 0:256], KN_[:, cs], SB, True, True, r=[S["B_KN"], B_SB], w=[PS[iks]])
            iqs = nextps()
            mm(psb[iqs][pp, 0:256], QN_[:, cs], SB, True, True, r=[S["B_QN"], B_SB], w=[PS[iqs]])
            yield
            stt(RM[pp, :], psb[iks][pp, 0:256], NEGEGC[pp, t_, h:h + 1], VT_[pp, tl, :], ALU.mult, ALU.add,
                r=[PS[iks], B_GSM, S["B_VT"]], w=[B_RM])
            yield
            ivn = nextps()
            mm(psb[ivn][pp, 0:256], MT_[pp, tl, p0:p0 + 64], RM[pp, :], True, True, r=[S["B_MT"], B_RM],
               w=[PS[ivn]])
            yield
            act(VN[pp, :], psb[ivn][pp, 0:256], AF.Identity, r=[PS[ivn], B_GSM], w=[B_VN],
                scale=BETA[pp, t_, h:h + 1])
            yield
            isu = nextps()
            mm(psb[isu][:, 0:256], KD_[pp, tl, :], VN[pp, :], True, True, r=[S["B_KD"], B_VN], w=[PS[isu]])
            iqv = nextps()
            mm(psb[iqv][pp, 0:256], QKM_[pp, tl, p0:p0 + 64], VN[pp, :], True, True, r=[S["B_QKM"], B_VN],
               w=[PS[iqv]])
            yield
            egl = EGL0 if hf == 0 else EGL1
            stt(GS[:, h, :], GS[:, h, :], egl[:, t_, h:h + 1], psb[isu][:, 0:256], ALU.mult, ALU.add,
                r=[B_GS[h], B_GSM, PS[isu]], w=[B_GS[h]])
            cp("act", OV[pp, :], psb[iqv][pp, 0:256], r=[PS[iqv]], w=[B_OV])
            yield
            cp("act", SB, GS[:, h, :], r=[B_GS[h]], w=[B_SB])
            Ot = S["OT%d" % (tl % 2)]
            B_Ot = S["B_OT%d" % (tl % 2)]
            stt(Ot[pp, :], psb[iqs][pp, 0:256], EGC[pp, t_, h:h + 1], OV[pp, :], ALU.mult, ALU.add,
                r=[PS[iqs], B_GSM, B_OV], w=[B_Ot])
            yield
            if hf == 1:
                act(ON, Ot, AF.Square, r=[B_Ot], w=[B_ON, B_ST], accum=STAT[:, st0:st0 + 1])
                yield
                act(STAT[:, st0 + 1:st0 + 2], STAT[:, st0:st0 + 1], AF.Ln, r=[B_ST], w=[B_ST], scale=1.0 / 256,
                    bias=EPS)
                act(STAT[:, st0 + 2:st0 + 3], STAT[:, st0 + 1:st0 + 2], AF.Exp, r=[B_ST], w=[B_ST], scale=-0.5)
                yield
                stt(ON, Ot, STAT[:, st0 + 2:st0 + 3], GNWB, ALU.mult, ALU.mult, r=[B_Ot, B_ST, B_GC], w=[B_ON])
                yield
                i0 = nextps()
                for et in range(2):
                    mm(psb[i0][:, et * 128:(et + 1) * 128], ON[:, et * 128:(et + 1) * 128], identb, True, True,
                       r=[B_ON, B_K], w=[PS[i0]])
                yield
                tt("dve", Y2[:, 2 * h:2 * h + 2, t_ * 128:(t_ + 1) * 128],
                   psb[i0][:, 0:256].rearrange("p (a n) -> p a n", n=128), SZ_[:, :, tl * 128:(tl + 1) * 128],
                   ALU.mult, r=[PS[i0], S["B_SZ"]], w=[B_Y2[2 * h], B_Y2[2 * h + 1]])
                yield

    def head_gen(h, S):
        for qi in range(NQ):
            yield from gdn_prep(h, qi, S)
            yield from gdn_loop(h, qi, S)

    def gdn_layer(ch):
        P.barrier()
        norm_transpose(1)
        gdn_gates()
        for hp in range(4):
            gens = [head_gen(2 * hp, GST[0]), head_gen(2 * hp + 1, GST[1])]
            alive = [True, True]
            pspool[0] = [0, 1, 2, 3]
            for _ in range(OFFSET):
                try:
                    next(gens[0])
                except StopIteration:
                    alive[0] = False
                    break
            if LOCKSTEP == 0:
                for g_ in gens:
                    for _ in g_:
                        pass
                alive = [False, False]
            while alive[0] or alive[1]:
                for gi in range(2):
                    pspool[0] = [0, 1, 2, 3] if gi == 0 else [4, 5, 6, 7]
                    for _rep in range(max(1, LOCKSTEP)):
                        if alive[gi]:
                            try:
                                next(gens[gi])
                            except StopIteration:
                                alive[gi] = False
            pspool[0] = None
        P.barrier()
        out_proj(gwo_d, 1)
        P.barrier()

    for ch in range(NCH):
        t0 = ch * T
        if ch == 0 or stage < 2:
            for tti in range(NT):
                dma("sp", XS[:, tti, :], x_d[t0 + tti * 128:t0 + (tti + 1) * 128, :], "x%d" % tti, w=[B_X[tti]])
        s5_layer(ch)
        if stage >= 2:
            gdn_layer(ch)
            final_norm(ch)
        else:
            for tti in range(NT):
                dma("sp", out_d[t0 + tti * 128:t0 + (tti + 1) * 128, :], XS[:, tti, :], "o%d" % tti,
                    r=[B_X[tti]], is_out=True)

    P.emit(nc, es)
    es.close()
    return nc


def make_in_maps(inputs, seq, ncores):
    f = lambda a: np.ascontiguousarray(np.asarray(a, dtype=np.float32))
    shared = {
        "ada_w": f(inputs["ada_w"]),
        "ada_b": f(inputs["ada_b"]).reshape(1, 6 * D),
        "norm_w": f(inputs["norm_w"]).reshape(16, 128),
        "s5_w_in": f(inputs["s5_w_in"])[0],
        "s5_lambda_re": f(inputs["s5_lambda_re"])[0],
        "s5_lambda_im": f(inputs["s5_lambda_im"])[0],
        "s5_log_dt": f(inputs["s5_log_dt"])[0].reshape(128, 1),
        "s5_b_re": f(inputs["s5_b_re"])[0].reshape(128, 1024),
        "s5_b_im": f(inputs["s5_b_im"])[0].reshape(128, 1024),
        "s5_c_re": f(inputs["s5_c_re"])[0].reshape(128, 1024),
        "s5_c_im": f(inputs["s5_c_im"])[0].reshape(128, 1024),
        "s5_d": f(inputs["s5_d"])[0].reshape(128, 16),
        "s5_w_glu": f(inputs["s5_w_glu"])[0],
        "s5_w_out": f(inputs["s5_w_out"])[0],
        "gdn_w_in": f(inputs["gdn_w_in"])[0],
        "gdn_conv_w": f(inputs["gdn_conv_w"])[0].reshape(128, 128),
        "gdn_a_log": f(inputs["gdn_a_log"]).reshape(1, 8),
        "gdn_dt_bias": f(inputs["gdn_dt_bias"]).reshape(1, 8),
        "gdn_norm_w": f(inputs["gdn_norm_w"]).reshape(1, 256),
        "gdn_w_out": f(inputs["gdn_w_out"])[0],
        "final_norm_w": f(inputs["final_norm_w"]).reshape(1, D),
        "cst": make_consts(),
    }
    x = f(inputs["x"])
    c = f(inputs["c"])
    maps = []
    for b in range(ncores):
        m = dict(shared)
        m["x"] = np.ascontiguousarray(x[b, :seq])
        m["c"] = np.ascontiguousarray(c[b].reshape(8, 128))
        maps.append(m)
    return maps


_NC_CACHE = {}


def kernel(**inputs):
    x = np.asarray(inputs["x"])
    nb, seq, _ = x.shape
    key = (seq, 2)
    if key not in _NC_CACHE:
        _NC_CACHE[key] = build(seq, 2)
    nc = _NC_CACHE[key]
    maps = make_in_maps(inputs, seq, nb)
    res = run_bass_kernel_spmd(nc, maps, core_ids=list(range(nb)))
    out = np.stack([np.asarray(r["out"], dtype=np.float32) for r in res.results], axis=0)
    return out
```

```python
import math
from contextlib import ExitStack
import numpy as np
import concourse.bass as bass
import concourse.mybir as mybir
from concourse.bass_utils import run_bass_kernel_spmd

F32 = mybir.dt.float32
BF16 = mybir.dt.bfloat16
AF = mybir.ActivationFunctionType
ALU = mybir.AluOpType
AX = mybir.AxisListType

D = 1024
E = 2048
T = 1024
NT = T // 128
EPS = 1e-6
ENGS = ("pe", "act", "dve", "pool", "sp")
import os
LOCKSTEP = int(os.environ.get("K_LOCKSTEP", "1"))
OFFSET = int(os.environ.get("K_OFFSET", "0"))


def _dtsize(dt):
    return 4 if dt == F32 else 2


class Buf:
    __slots__ = ("name", "w", "r")

    def __init__(self, name):
        self.name = name
        self.w = None
        self.r = []


class Prog:
    def __init__(self):
        self.ops = {e: [] for e in ENGS}
        self.seen = {e: {} for e in ENGS}
        self.dcnt = {}
        self.last = {e: None for e in ENGS}
        self.pending = {e: [] for e in ENGS}
        self.out_events = []

    def _waits(self, eng, r, w):
        raw = []
        oth = []
        for b in r:
            if b.w is not None:
                raw.append(b.w)
        for b in w:
            if b.w is not None:
                oth.append(b.w)
            oth.extend(b.r)
        d = {}
        for ev in raw:
            st, pos = ev
            if st == eng and eng == "pe":
                continue
            d[st] = max(d.get(st, 0), pos)
        for ev in oth:
            st, pos = ev
            if st == eng:
                continue
            d[st] = max(d.get(st, 0), pos)
        for st, pos in self.pending[eng]:
            d[st] = max(d.get(st, 0), pos)
        self.pending[eng] = []
        out = []
        for st, pos in d.items():
            if self.seen[eng].get(st, 0) >= pos:
                continue
            self.seen[eng][st] = pos
            out.append((st, pos))
        return out

    def op(self, eng, fn, r=(), w=()):
        waits = self._waits(eng, r, w)
        pos = len(self.ops[eng]) + 1
        ev = (eng, pos)
        self.ops[eng].append((waits, fn, ev, None))
        self.last[eng] = ev
        for b in r:
            b.r.append(ev)
        for b in w:
            b.w = ev
            b.r = []
        return ev

    def dma(self, q, fn, sem, r=(), w=(), is_out=False):
        waits = self._waits(q, r, w)
        self.dcnt[sem] = self.dcnt.get(sem, 0) + 16
        ev = ("dma:" + sem, self.dcnt[sem])
        self.ops[q].append((waits, fn, None, sem))
        for b in r:
            b.r.append(ev)
        for b in w:
            b.w = ev
            b.r = []
        if is_out:
            self.out_events.append(ev)
        return ev

    def alias(self, new, olds):
        for o in olds:
            if o.w is not None:
                new.r.append(o.w)
            new.r.extend(o.r)

    def barrier(self):
        evs = [self.last[e] for e in ENGS if self.last[e] is not None]
        evs += [("dma:" + s, c) for s, c in self.dcnt.items()]
        for e in ENGS:
            for ev in evs:
                if ev[0] == e and e == "pe":
                    continue
                self.pending[e].append(ev)

    def emit(self, nc, es):
        needed = {e: set() for e in ENGS}
        for e in ENGS:
            for waits, fn, ev, dsem in self.ops[e]:
                for st, pos in waits:
                    if not st.startswith("dma:"):
                        needed[st].add(pos)
        rank = {e: {p: i + 1 for i, p in enumerate(sorted(needed[e]))} for e in ENGS}
        esem = {e: es.enter_context(nc.semaphore("s_" + e)) for e in ENGS}
        dsem = {s: es.enter_context(nc.semaphore("d_" + s)) for s in self.dcnt}
        fin = {}
        for st, pos in self.out_events:
            fin[st] = max(fin.get(st, 0), pos)
        block = es.enter_context(nc.Block())

        def replay(e, eng):
            for waits, fn, ev, ds in self.ops[e]:
                for st, pos in waits:
                    if st.startswith("dma:"):
                        eng.wait_ge(dsem[st[4:]], pos)
                    else:
                        eng.wait_ge(esem[st], rank[st][pos])
                ins = fn(eng)
                if ds is not None:
                    ins.then_inc(dsem[ds], 16)
                elif ev[1] in needed[e]:
                    ins.then_inc(esem[e], 1)
            if e == "sp":
                for st, pos in fin.items():
                    eng.wait_ge(dsem[st[4:]], pos)

        @block.tensor
        def _(t):
            replay("pe", t)

        @block.scalar
        def _(t):
            replay("act", t)

        @block.vector
        def _(t):
            replay("dve", t)

        @block.gpsimd
        def _(t):
            replay("pool", t)

        @block.sync
        def _(t):
            replay("sp", t)


class Arena:
    def __init__(self, nc, es, words):
        self.t = es.enter_context(nc.sbuf_tensor("arena", [128, words], F32))
        self.words = words
        self.off = 0

    def alloc(self, nbytes):
        w = (nbytes + 31) // 32 * 8
        off = self.off
        self.off += w
        assert self.off <= self.words, ("arena overflow", self.off, self.words)
        return off

    def view(self, off, shape, dt, p0=0):
        n = 1
        for s in shape[1:]:
            n *= s
        words = n * _dtsize(dt) // 4
        ap = self.t[p0:p0 + shape[0], off:off + words]
        if dt != F32:
            ap = ap.bitcast(dt)
        if len(shape) == 3:
            ap = ap.rearrange("p (a b) -> p a b", a=shape[1], b=shape[2])
        elif len(shape) == 4:
            ap = ap.rearrange("p (a b c) -> p a b c", a=shape[1], b=shape[2], c=shape[3])
        return ap

    def new(self, shape, dt):
        n = 1
        for s in shape[1:]:
            n *= s
        off = self.alloc(n * _dtsize(dt))
        return self.view(off, shape, dt)


C_ID = 0
C_SEL = 128
C_EVEN = 192
C_ODD = 193
C_NEGM = 194
C_SMASK = 322
C_TRI = 450
C_CH0 = 578
C_CH1 = 706
C_N = 834


def make_consts():
    c = np.zeros((128, C_N), np.float32)
    c[:, C_ID:C_ID + 128] = np.eye(128, dtype=np.float32)
    g = np.arange(128)
    c[g, C_SEL + g // 2] = 1.0
    c[:, C_EVEN] = (g % 2 == 0)
    c[:, C_ODD] = (g % 2 == 1)
    j = g[:, None]
    i = g[None, :]
    same = (j // 64) == (i // 64)
    c[:, C_NEGM:C_NEGM + 128] = np.where(same & (j <= i), 0.0, -1.0e4)
    c[:, C_SMASK:C_SMASK + 128] = (same & (j < i))
    c[:, C_TRI:C_TRI + 128] = (same & (j <= i))
    c[:, C_CH0:C_CH0 + 128] = (j < 64) & (i >= 0)
    c[:, C_CH1:C_CH1 + 128] = (j >= 64) & (i >= 0)
    return c


def build(seq, stage=2):
    assert seq % T == 0
    NCH = seq // T
    nc = bass.Bass("TRN2", target_bir_lowering=False)
    P = Prog()
    es = ExitStack()

    def din(name, shape):
        return nc.dram_tensor(name, shape, F32, kind="ExternalInput").ap()

    x_d = din("x", [seq, D])
    c_d = din("c", [8, 128])
    adaw_d = din("ada_w", [2, D, 3 * D])
    adab_d = din("ada_b", [1, 2 * 3 * D])
    nw_d = din("norm_w", [16, 128])
    s5win_d = din("s5_w_in", [D, 2 * E])
    lamr_d = din("s5_lambda_re", [128, 64])
    lami_d = din("s5_lambda_im", [128, 64])
    ldt_d = din("s5_log_dt", [128, 1])
    bre_d = din("s5_b_re", [128, 1024])
    bim_d = din("s5_b_im", [128, 1024])
    cre_d = din("s5_c_re", [128, 1024])
    cim_d = din("s5_c_im", [128, 1024])
    dsk_d = din("s5_d", [128, 16])
    wglu_d = din("s5_w_glu", [E, E])
    s5wo_d = din("s5_w_out", [E, D])
    gwin_d = din("gdn_w_in", [D, 6160])
    convw_d = din("gdn_conv_w", [128, 128])
    alog_d = din("gdn_a_log", [1, 8])
    dtb_d = din("gdn_dt_bias", [1, 8])
    gnw_d = din("gdn_norm_w", [1, 256])
    gwo_d = din("gdn_w_out", [E, D])
    fnw_d = din("final_norm_w", [1, D])
    cst_d = din("cst", [128, C_N])
    out_d = nc.dram_tensor("out", [seq, D], F32, kind="ExternalOutput").ap()
    t0_d = nc.dram_tensor("t0_scr", [128, 128, 128], BF16).ap()
    wv_d = nc.dram_tensor("wv_scr", [128, 128, 128], BF16).ap()
    wc_d = nc.dram_tensor("wc_scr", [2, 64, 64, 2, 128], BF16).ap()

    A = Arena(nc, es, 53208)
    psb = [es.enter_context(nc.psum_tensor("ps%d" % i, [128, 512], F32)) for i in range(8)]
    PS = [Buf("ps%d" % i) for i in range(8)]
    psrr = [0]

    pspool = [None]
    pscur = {}

    def nextps():
        if pspool[0] is None:
            i = psrr[0] % 8
            psrr[0] += 1
        else:
            key = tuple(pspool[0])
            c = pscur.get(key, 0)
            i = pspool[0][c % len(pspool[0])]
            pscur[key] = c + 1
        assert PS[i].w is None or PS[i].w[0] != "pe" or len(PS[i].r) > 0, ("psum bank still live", i)
        return i

    def mm(out, lhsT, rhs, start, stop, r, w):
        P.op("pe", lambda e: e.matmul(out, lhsT=lhsT, rhs=rhs, start=start, stop=stop), r=r, w=w)

    def act(out, in_, func, r, w, scale=1.0, bias=0.0, accum=None):
        if accum is None:
            P.op("act", lambda e: e.activation(out=out, in_=in_, func=func, bias=bias, scale=scale), r=r, w=w)
        else:
            P.op("act", lambda e: e.activation(out=out, in_=in_, func=func, bias=bias, scale=scale,
                                               accum_out=accum), r=r, w=w)

    def tt(eng, out, in0, in1, op, r, w):
        P.op(eng, lambda e: e.tensor_tensor(out=out, in0=in0, in1=in1, op=op), r=r, w=w)

    def ts(eng, out, in0, s1, op0, r, w, s2=None, op1=None):
        if op1 is None:
            P.op(eng, lambda e: e.tensor_scalar(out=out, in0=in0, scalar1=s1, scalar2=None, op0=op0), r=r, w=w)
        else:
            P.op(eng, lambda e: e.tensor_scalar(out=out, in0=in0, scalar1=s1, scalar2=s2, op0=op0, op1=op1),
                 r=r, w=w)

    def stt(out, in0, scalar, in1, op0, op1, r, w):
        P.op("dve", lambda e: e.scalar_tensor_tensor(out=out, in0=in0, scalar=scalar, in1=in1, op0=op0, op1=op1),
             r=r, w=w)

    def cp(eng, out, in_, r, w):
        if eng == "act":
            P.op("act", lambda e: e.activation(out=out, in_=in_, func=AF.Copy), r=r, w=w)
        else:
            P.op(eng, lambda e: e.tensor_copy(out=out, in_=in_), r=r, w=w)

    def memset(eng, ap, val, w):
        P.op(eng, lambda e: e.memset(ap, val), w=w)

    def recip(out, in_, r, w):
        P.op("dve", lambda e: e.reciprocal(out=out, in_=in_), r=r, w=w)

    def dma(q, out, in_, sem, r=(), w=(), is_out=False):
        P.dma(q, lambda e: e.dma_start(out=out, in_=in_), sem, r=r, w=w, is_out=is_out)

    CST = A.new([128, C_N], F32)
    B_CST = Buf("cst")
    ident = CST[:, C_ID:C_ID + 128]
    identb = A.new([128, 128], BF16)
    onesf = A.new([128, 128], F32)
    onesb = A.new([128, 128], BF16)
    B_K = Buf("konst")
    XS_OFF = A.alloc(NT * D * 4)
    XS = A.view(XS_OFF, [128, NT, D], F32)
    B_X = [Buf("x%d" % i) for i in range(NT)]
    HT_OFF = A.alloc(8 * T * 2)
    HT = A.view(HT_OFF, [128, 8, T], BF16)
    B_HT = [Buf("ht%d" % i) for i in range(NT)]
    BIG1 = A.alloc(32768)
    BIG2 = A.alloc(32768)
    WOFF = [A.alloc(16384), A.alloc(16384)]
    B_W = [Buf("w0"), Buf("w1")]
    wrr = [0]
    SMALL = A.alloc(8192 + 12288 + 4096)
    GATEB = A.new([128, 2, D], F32)
    FNWB = A.new([128, D], F32)
    WEFF = A.new([128, 2, 8], F32)
    SHIFT = A.new([128, 2, 8], F32)
    B_MOD = Buf("mod")
    AR2 = A.new([128, 2, 64], F32)
    AI2 = A.new([128, 2, 64], F32)
    S3 = [A.new([128, 3, 64], F32), A.new([128, 3, 64], F32)]
    B_S3 = [Buf("s3a"), Buf("s3b")]
    B_AR = Buf("ar")
    STAT = A.new([128, 64], F32)
    B_ST = Buf("stat")
    M1 = A.new([128, 2, 64], F32)
    M2 = A.new([128, 2, 64], F32)
    B_M1 = Buf("m1")
    B_M2 = Buf("m2")
    NTMP = A.new([128, 4, 128], F32)
    B_NTMP = Buf("ntmp")
    XN = [A.new([128, D], BF16), A.new([128, D], BF16)]
    B_XN = [Buf("xn0"), Buf("xn1")]

    def wslot():
        i = wrr[0] % 2
        wrr[0] += 1
        return i

    dma("sp", CST, cst_d, "cst", w=[B_CST])
    cp("dve", identb, ident, r=[B_CST], w=[B_K])
    memset("dve", onesf, 1.0, w=[B_K])
    memset("dve", onesb, 1.0, w=[B_K])
    memset("dve", S3[0], 0.0, w=[B_S3[0]])
    memset("dve", S3[1], 0.0, w=[B_S3[1]])

    so = [BIG2]

    def salloc(shape, dt):
        n = 1
        for s_ in shape[1:]:
            n *= s_
        nb = (n * _dtsize(dt) + 31) // 32 * 8
        v = A.view(so[0], shape, dt)
        so[0] += nb
        assert so[0] <= BIG2 + 8192, "setup scratch overflow"
        return v

    c8 = salloc([8, 128], F32)
    ccol = salloc([128, 8], F32)
    nwr = salloc([16, 128], F32)
    nwc = salloc([128, 16], F32)
    rowb = salloc([1, 512], F32)
    adab = salloc([1, 512], F32)
    scc = salloc([128, 16], F32)
    B_S = Buf("setup_s")
    B_RB = Buf("rowb")
    B_AB0 = Buf("adab")
    dma("sp", c8, c_d, "su1", w=[B_S])
    dma("sp", nwr, nw_d, "su1", w=[B_S])
    dma("sp", FNWB, fnw_d.partition_broadcast(128), "su1", w=[B_MOD])
    B_S.w = ("dma:su1", P.dcnt["su1"])
    B_MOD.w = ("dma:su1", P.dcnt["su1"])
    act(c8, c8, AF.Silu, r=[B_S], w=[B_S])
    i0 = nextps()
    mm(psb[i0][:, 0:8], c8, ident[0:8, 0:8], True, True, r=[B_S, B_CST], w=[PS[i0]])
    cp("dve", ccol, psb[i0][:, 0:8], r=[PS[i0]], w=[B_S])
    i0 = nextps()
    mm(psb[i0][:, 0:16], nwr, ident[0:16, 0:16], True, True, r=[B_S, B_CST], w=[PS[i0]])
    cp("dve", nwc, psb[i0][:, 0:16], r=[PS[i0]], w=[B_S])
    for l in range(2):
        for cb in range(6):
            s = wslot()
            wl = A.view(WOFF[s], [128, 8, 512], F32)
            dma("sp", wl, adaw_d[l, :, cb * 512:(cb + 1) * 512].rearrange("(k p) c -> p k c", p=128), "w%d" % s,
                w=[B_W[s]])
            o = l * 3 * D + cb * 512
            dma("sp", adab, adab_d[:, o:o + 512], "ab", w=[B_AB0])
            i0 = nextps()
            for k in range(8):
                mm(psb[i0][0:1, :], ccol[:, k:k + 1], wl[:, k, :], k == 0, k == 7, r=[B_S, B_W[s]], w=[PS[i0]])
            tt("dve", rowb, psb[i0][0:1, :], adab, ALU.add, r=[PS[i0], B_AB0], w=[B_RB])
            which = cb // 2
            if which < 2:
                i1 = nextps()
                for kk in range(4):
                    mm(psb[i1][:, kk:kk + 1], rowb[0:1, kk * 128:(kk + 1) * 128], onesf[0:1, 0:1], True, True,
                       r=[B_RB, B_K], w=[PS[i1]])
                k0 = (cb % 2) * 4
                if which == 0:
                    cp("dve", SHIFT[:, l, k0:k0 + 4], psb[i1][:, 0:4], r=[PS[i1]], w=[B_MOD])
                else:
                    ts("dve", scc[:, 0:4], psb[i1][:, 0:4], 1.0, ALU.add, r=[PS[i1]], w=[B_S])
                    tt("dve", WEFF[:, l, k0:k0 + 4], scc[:, 0:4], nwc[:, l * 8 + k0:l * 8 + k0 + 4], ALU.mult,
                       r=[B_S], w=[B_MOD])
            else:
                dh = cb % 2
                i1 = nextps()
                mm(psb[i1][:, :], onesf[0:1, :], rowb[0:1, :], True, True, r=[B_RB, B_K], w=[PS[i1]])
                ts("dve", GATEB[:, l, dh * 512:(dh + 1) * 512], psb[i1][:, :], 0.25 if l == 0 else 0.5, ALU.mult,
                   r=[PS[i1]], w=[B_MOD])

    class Reg:
        def __init__(self, base, nbytes):
            self.base = base
            self.off = 0
            self.cap = nbytes // 4

        def new(self, shape, dt):
            n = 1
            for s_ in shape[1:]:
                n *= s_
            nb = (n * _dtsize(dt) + 31) // 32 * 8
            v = A.view(self.base + self.off, shape, dt)
            self.off += nb
            assert self.off <= self.cap, "region overflow"
            return v

    r_small = Reg(SMALL, 8192 + 12288 + 4096)
    r_ht = Reg(HT_OFF, 16384)
    r_b1 = Reg(BIG1 + 4096, 16384)
    T0t = A.view(BIG2, [128, 128, 128], BF16)
    WVt = A.view(WOFF[0], [128, 128, 128], BF16)
    WCt = A.view(XS_OFF, [128, 128, 128], BF16)
    B_T0 = Buf("T0t")
    B_WV = Buf("WVt")
    B_WC = Buf("WCt")
    P.alias(B_T0, [B_S, B_RB, B_AB0])
    P.alias(B_WV, B_W)
    lamr = r_small.new([128, 64], F32)
    lami = r_small.new([128, 64], F32)
    ldt = r_small.new([128, 1], F32)
    dt64 = r_small.new([128, 1], F32)
    ar = r_small.new([128, 64], F32)
    ai = r_small.new([128, 64], F32)
    t1 = r_small.new([128, 64], F32)
    t2 = r_small.new([128, 64], F32)
    t3 = r_small.new([128, 64], F32)
    qre = r_small.new([128, 64], F32)
    qim = r_small.new([128, 64], F32)
    APR = r_small.new([128, 9, 64], F32)
    API = r_small.new([128, 9, 64], F32)
    dsk = r_small.new([128, 16], F32)
    Kd = r_small.new([128, 16, 16], F32)
    Kt = r_small.new([128, 4, 16], F32)
    lre = r_small.new([128, 128], F32)
    bre = r_ht.new([128, 64, 16], F32)
    bim = r_ht.new([128, 64, 16], F32)
    cre = r_ht.new([128, 16, 64], F32)
    cim = r_ht.new([128, 16, 64], F32)
    Gre = r_b1.new([128, 64, 16], F32)
    Gim = r_b1.new([128, 64, 16], F32)
    Gt = r_b1.new([128, 64, 16], F32)
    Gu = r_b1.new([128, 64, 16], F32)
    prodA = A.view(BIG1, [128, 4, 16, 64], F32)
    B_PA = Buf("prodA")
    KtP = r_small.new([128, 2, 16], F32)
    B_PP = Buf("prodP")
    B_KD0 = Buf("kd0")
    B_KD1 = Buf("kd1")
    B_G = Buf("G")
    B_P5 = Buf("s5p")
    for v, d_ in ((lamr, lamr_d), (lami, lami_d), (ldt, ldt_d), (dsk, dsk_d)):
        dma("sp", v, d_, "su4", w=[B_P5])
    dma("sp", bre, bre_d.rearrange("g (p m) -> g p m", m=16), "su4", w=[B_P5])
    dma("sp", bim, bim_d.rearrange("g (p m) -> g p m", m=16), "su4", w=[B_P5])
    dma("sp", cre, cre_d.rearrange("g (m p) -> g m p", p=64), "su4", w=[B_P5])
    dma("sp", cim, cim_d.rearrange("g (m p) -> g m p", p=64), "su4", w=[B_P5])
    cwr = r_small.new([128, 128], F32)
    dma("sp", cwr, convw_d, "su4", w=[B_P5])
    B_P5.w = ("dma:su4", P.dcnt["su4"])

    R5 = [B_P5]
    act(dt64, ldt, AF.Exp, r=R5, w=R5)
    ts("dve", dt64, dt64, 1.0 / 64.0, ALU.mult, r=R5, w=R5)
    act(t1, lamr, AF.Exp, r=R5, w=R5, scale=dt64[:, 0:1])
    act(ai, lami, AF.Sin, r=R5, w=R5, scale=dt64[:, 0:1])
    act(ar, lami, AF.Sin, r=R5, w=R5, scale=dt64[:, 0:1], bias=math.pi / 2)
    tt("dve", ar, ar, t1, ALU.mult, r=R5, w=R5)
    tt("dve", ai, ai, t1, ALU.mult, r=R5, w=R5)
    for _ in range(6):
        tt("dve", t1, ar, ar, ALU.mult, r=R5, w=R5)
        tt("dve", t2, ai, ai, ALU.mult, r=R5, w=R5)
        tt("dve", t3, ar, ai, ALU.mult, r=R5, w=R5)
        tt("dve", ar, t1, t2, ALU.subtract, r=R5, w=R5)
        ts("dve", ai, t3, 2.0, ALU.mult, r=R5, w=R5)
    tt("dve", t1, lamr, lamr, ALU.mult, r=R5, w=R5)
    tt("dve", t2, lami, lami, ALU.mult, r=R5, w=R5)
    tt("dve", t1, t1, t2, ALU.add, r=R5, w=R5)
    recip(t1, t1, r=R5, w=R5)
    ts("dve", t2, ar, -1.0, ALU.add, r=R5, w=R5)
    tt("dve", qre, t2, lamr, ALU.mult, r=R5, w=R5)
    tt("dve", t3, ai, lami, ALU.mult, r=R5, w=R5)
    tt("dve", qre, qre, t3, ALU.add, r=R5, w=R5)
    tt("dve", qre, qre, t1, ALU.mult, r=R5, w=R5)
    tt("dve", qim, ai, lamr, ALU.mult, r=R5, w=R5)
    tt("dve", t3, t2, lami, ALU.mult, r=R5, w=R5)
    tt("dve", qim, qim, t3, ALU.subtract, r=R5, w=R5)
    tt("dve", qim, qim, t1, ALU.mult, r=R5, w=R5)

    def bc_m(v):
        return v.unsqueeze(2).to_broadcast([128, 64, 16])

    tt("dve", Gre, bre, bc_m(qre), ALU.mult, r=R5, w=[B_G])
    tt("dve", Gt, bim, bc_m(qim), ALU.mult, r=R5, w=[B_G])
    tt("dve", Gre, Gre, Gt, ALU.subtract, r=[B_G], w=[B_G])
    tt("dve", Gim, bim, bc_m(qre), ALU.mult, r=R5, w=[B_G])
    tt("dve", Gt, bre, bc_m(qim), ALU.mult, r=R5, w=[B_G])
    tt("dve", Gim, Gim, Gt, ALU.add, r=[B_G], w=[B_G])
    memset("dve", APR[:, 0, :], 1.0, w=R5)
    memset("dve", API[:, 0, :], 0.0, w=R5)
    cp("dve", APR[:, 1, :], ar, r=R5, w=R5)
    cp("dve", API[:, 1, :], ai, r=R5, w=R5)
    for k in range(2, 9):
        tt("dve", t1, APR[:, k - 1, :], ar, ALU.mult, r=R5, w=R5)
        tt("dve", t2, API[:, k - 1, :], ai, ALU.mult, r=R5, w=R5)
        tt("dve", APR[:, k, :], t1, t2, ALU.subtract, r=R5, w=R5)
        tt("dve", t1, APR[:, k - 1, :], ai, ALU.mult, r=R5, w=R5)
        tt("dve", t2, API[:, k - 1, :], ar, ALU.mult, r=R5, w=R5)
        tt("dve", API[:, k, :], t1, t2, ALU.add, r=R5, w=R5)

    memset("pool", T0t, 0.0, w=[B_T0])
    Kd2 = Kd.rearrange("g a b -> g (a b)")
    for d_ in range(8):
        if d_ > 0:
            tt("dve", Gt, Gre, bc_m(ar), ALU.mult, r=[B_G] + R5, w=[B_G])
            tt("dve", Gu, Gim, bc_m(ai), ALU.mult, r=[B_G] + R5, w=[B_G])
            tt("dve", Gt, Gt, Gu, ALU.subtract, r=[B_G], w=[B_G])
            tt("dve", Gu, Gre, bc_m(ai), ALU.mult, r=[B_G] + R5, w=[B_G])
            cp("dve", Gre, Gt, r=[B_G], w=[B_G])
            tt("dve", Gt, Gim, bc_m(ar), ALU.mult, r=[B_G] + R5, w=[B_G])
            tt("dve", Gim, Gt, Gu, ALU.add, r=[B_G], w=[B_G])
        i_ = 7 - d_
        cp("act", WVt[:, i_ * 16:(i_ + 1) * 16, 0:64], Gre.rearrange("g p m -> g m p"), r=[B_G], w=[B_WV])
        cp("act", WVt[:, i_ * 16:(i_ + 1) * 16, 64:128], Gim.rearrange("g p m -> g m p"), r=[B_G], w=[B_WV])
        for sl in range(4):
            msl = slice(sl * 4, sl * 4 + 4)
            gre_b = Gre.rearrange("g p m -> g m p").unsqueeze(1).to_broadcast([128, 4, 16, 64])
            gim_b = Gim.rearrange("g p m -> g m p").unsqueeze(1).to_broadcast([128, 4, 16, 64])
            cre_b = cre[:, msl, :].unsqueeze(2).to_broadcast([128, 4, 16, 64])
            cim_b = cim[:, msl, :].unsqueeze(2).to_broadcast([128, 4, 16, 64])
            tt("dve", prodA, gre_b, cre_b, ALU.mult, r=[B_G] + R5, w=[B_PA])
            P.op("dve", lambda e, o=Kd[:, msl, :], i=prodA: e.tensor_reduce(
                out=o, in_=i, axis=AX.X, op=ALU.add), r=[B_PA], w=[B_KD0])
            tt("dve", prodA, gim_b, cim_b, ALU.mult, r=[B_G] + R5, w=[B_PA])
            P.op("dve", lambda e, o=Kt, i=prodA: e.tensor_reduce(
                out=o, in_=i, axis=AX.X, op=ALU.add), r=[B_PA], w=[B_KD0])
            tt("dve", Kd[:, msl, :], Kd[:, msl, :], Kt, ALU.subtract, r=[B_KD0], w=[B_KD0])
        if d_ == 0:
            tt("dve", Kd2[:, 0:256:17], Kd2[:, 0:256:17], dsk, ALU.add, r=[B_KD0] + R5, w=[B_KD0])
        for i2 in range(8 - d_):
            j2 = i2 + d_
            cp("act", T0t[:, i2 * 16:(i2 + 1) * 16, j2 * 16:(j2 + 1) * 16], Kd.rearrange("g m n -> g n m"),
               r=[B_KD0], w=[B_T0])
    creT = cre.rearrange("g m p -> g p m")
    cimT = cim.rearrange("g m p -> g p m")
    for j_ in range(8):
        pr = bc_m(APR[:, j_ + 1, :])
        pi_ = bc_m(API[:, j_ + 1, :])
        tt("dve", Gt, creT, pr, ALU.mult, r=R5 + [B_G], w=[B_G])
        tt("dve", Gu, cimT, pi_, ALU.mult, r=R5 + [B_G], w=[B_G])
        tt("dve", WCt[:, 0:64, j_ * 16:(j_ + 1) * 16], Gt, Gu, ALU.subtract, r=[B_G], w=[B_WC])
        tt("dve", Gt, creT, pi_, ALU.mult, r=R5 + [B_G], w=[B_G])
        tt("dve", Gu, cimT, pr, ALU.mult, r=R5 + [B_G], w=[B_G])
        tt("dve", Gt, Gt, Gu, ALU.add, r=[B_G], w=[B_G])
        ts("dve", WCt[:, 64:128, j_ * 16:(j_ + 1) * 16], Gt, -1.0, ALU.mult, r=[B_G], w=[B_WC])
    for r8 in range(8):
        rs_ = slice(r8 * 16, (r8 + 1) * 16)
        dma("sp", t0_d[rs_, :, :].rearrange("r g c -> g r c"), T0t[:, rs_, :], "scr", r=[B_T0])
        dma("sp", wv_d[rs_, :, :].rearrange("r g c -> g r c"), WVt[:, rs_, :], "scr", r=[B_WV])
    for g2 in range(2):
        for e_ in range(2):
            for ph in range(2):
                psl = slice(ph * 32, (ph + 1) * 32)
                dma("sp", wc_d[g2, psl, :, e_, :].rearrange("p gp c -> gp p c"),
                    WCt[g2:128:2, e_ * 64 + ph * 32:e_ * 64 + (ph + 1) * 32, :], "scr", r=[B_WC])
    B_SCR = Buf("scr")
    B_SCR.w = ("dma:scr", P.dcnt["scr"])
    for src, dst_is_im in ((APR[:, 8, :], False), (API[:, 8, :], True)):
        ts("dve", lre[:, 0:64], src, CST[:, C_EVEN:C_EVEN + 1], ALU.mult, r=R5 + [B_CST], w=[B_G])
        ts("dve", lre[:, 64:128], src, CST[:, C_ODD:C_ODD + 1], ALU.mult, r=R5 + [B_CST], w=[B_G])
        i0 = nextps()
        mm(psb[i0][:, 0:64], lre, CST[:, C_SEL:C_SEL + 64], True, True, r=[B_G, B_CST], w=[PS[i0]])
        if not dst_is_im:
            cp("dve", AR2[:, 0, :], psb[i0][:, 0:64], r=[PS[i0]], w=[B_AR])
            cp("dve", AR2[:, 1, :], psb[i0][:, 0:64], r=[PS[i0]], w=[B_AR])
        else:
            ts("dve", AI2[:, 0, :], psb[i0][:, 0:64], -1.0, ALU.mult, r=[PS[i0]], w=[B_AR])
            cp("dve", AI2[:, 1, :], psb[i0][:, 0:64], r=[PS[i0]], w=[B_AR])

    GS = A.new([128, 8, 256], F32)
    B_GS = [Buf("gs%d" % i) for i in range(8)]
    HIST = A.new([128, 32, 3], F32)
    B_HIST = Buf("hist")
    CW = A.new([128, 4, 32], F32)
    GNWB = A.new([128, 256], F32)
    NEXPA = A.new([128, 8], F32)
    DTBB = A.new([128, 8], F32)
    NEGONES = A.new([128, 128], F32)
    WAB = A.new([128, 8, 16], BF16)
    GSM = A.new([128, 12, 64], F32)
    B_GSM = Buf("gsm")
    B_GC = Buf("gconst")
    memset("pool", GS, 0.0, w=B_GS)
    memset("pool", HIST, 0.0, w=[B_HIST])
    memset("pool", NEGONES, -1.0, w=[B_GC])
    B_WAB = Buf("wab")
    dma("pool", WAB, gwin_d[:, 6144:6160].rearrange("(k p) c -> p k c", p=128), "suw", w=[B_WAB])
    dma("sp", GNWB, gnw_d.partition_broadcast(128), "su9", w=[B_GC])
    dma("sp", NEXPA, alog_d.partition_broadcast(128), "su9", w=[B_GC])
    dma("sp", DTBB, dtb_d.partition_broadcast(128), "su9", w=[B_GC])
    B_GC.w = ("dma:su9", P.dcnt["su9"])
    i0 = nextps()
    mm(psb[i0][:, 0:128], cwr, ident, True, True, r=[B_P5, B_CST], w=[PS[i0]])
    cp("dve", CW.rearrange("p k t -> p (k t)"), psb[i0][:, 0:128], r=[PS[i0]], w=[B_GC])
    act(NEXPA, NEXPA, AF.Exp, r=[B_GC], w=[B_GC])
    ts("dve", NEXPA, NEXPA, -1.0, ALU.mult, r=[B_GC], w=[B_GC])

    P.barrier()

    UY = A.view(BIG1, [128, 128, 128], BF16)
    B_UY = [Buf("uy%d" % i) for i in range(16)]
    VS = A.view(BIG2, [128, 2, 64, 128], BF16)
    B_VS = Buf("vs")
    YFM = A.view(BIG2, [128, 16, T], BF16)
    B_YFM = [Buf("yfm%d" % i) for i in range(16)]
    Y2 = A.view(BIG1, [128, 16, T], BF16)
    B_Y2 = [Buf("y2%d" % i) for i in range(16)]
    ABATS = [A.view(SMALL, [128, 32, 8, 16], BF16), A.view(SMALL + 3072, [128, 32, 8, 16], BF16)]
    B_ABS = [Buf("abat0"), Buf("abat1")]
    WVs = [A.view(SMALL + 2048 + 512 * i, [128, 8, 128], BF16) for i in range(2)]
    T0s = [A.view(SMALL + 3072 + 1024 * i, [128, 8, 128], BF16) for i in range(2)]
    WCt2 = [A.view(SMALL + 3072 + 1024 * i + 512, [128, 4, 2, 128], BF16) for i in range(2)]
    B_TW = [Buf("tw0"), Buf("tw1")]
    B_TB = [Buf("tbl0"), Buf("tbl1")]
    tbrr = [0]
    TMPO = SMALL + 2048 + 3072
    TMP1 = A.view(TMPO, [128, 512], F32)
    TMP2 = A.view(TMPO + 512, [128, 512], BF16)
    TMP3 = A.view(TMPO + 768, [128, 512], BF16)
    B_T1 = Buf("tmp1")
    B_T2 = Buf("tmp2")
    B_T3 = Buf("tmp3")
    B_VSn = [Buf("vsn0"), Buf("vsn1")]
    evq = [0]

    def evac_eng():
        evq[0] += 1
        return "act" if evq[0] % 2 else "dve"

    def norm_transpose(layer):
        for tti in range(NT):
            xt = XS[:, tti, :]
            s = tti % 2
            junk = XN[s]
            act(junk, xt, AF.Square, r=[B_X[tti]], w=[B_XN[s], B_ST], accum=STAT[:, tti:tti + 1])
            act(STAT[:, 8 + tti:9 + tti], STAT[:, tti:tti + 1], AF.Ln, r=[B_ST], w=[B_ST], scale=1.0 / D,
                bias=EPS)
            act(STAT[:, 16 + tti:17 + tti], STAT[:, 8 + tti:9 + tti], AF.Exp, r=[B_ST], w=[B_ST], scale=-0.5)
            ts("dve", XN[s], xt, STAT[:, 16 + tti:17 + tti], ALU.mult, r=[B_X[tti], B_ST], w=[B_XN[s]])
            for half in range(2):
                i0 = nextps()
                for kk in range(4):
                    k = half * 4 + kk
                    mm(psb[i0][:, kk * 128:(kk + 1) * 128], XN[s][:, k * 128:(k + 1) * 128], identb, True, True,
                       r=[B_XN[s], B_K], w=[PS[i0]])
                if half == 0:
                    for kk in range(4):
                        k = half * 4 + kk
                        act(HT[:, k, tti * 128:(tti + 1) * 128], psb[i0][:, kk * 128:(kk + 1) * 128], AF.Identity,
                            r=[PS[i0], B_MOD], w=[B_HT[tti]], scale=WEFF[:, layer, k:k + 1],
                            bias=SHIFT[:, layer, k:k + 1])
                else:
                    hv = HT[:, 4:8, tti * 128:(tti + 1) * 128]
                    tt("dve", NTMP, psb[i0][:, :].rearrange("p (k n) -> p k n", n=128),
                       WEFF[:, layer, 4:8].unsqueeze(2).to_broadcast([128, 4, 128]), ALU.mult, r=[PS[i0], B_MOD],
                       w=[B_NTMP])
                    tt("dve", hv, NTMP, SHIFT[:, layer, 4:8].unsqueeze(2).to_broadcast([128, 4, 128]), ALU.add,
                       r=[B_NTMP, B_MOD], w=[B_HT[tti]])

    def wload(view, src, s):
        dma("pool", view, src, "w%d" % s, w=[B_W[s]])

    def s5_layer(ch):
        P.alias(B_VS, B_YFM)
        for b in B_UY:
            P.alias(b, B_Y2)
        norm_transpose(0)
        P.alias(B_ABS[1], B_TB)
        for fb in range(4):
            ABAT = ABATS[fb % 2]
            B_AB = B_ABS[fb % 2]
            s = wslot()
            Wb = A.view(WOFF[s], [128, 8, 512], BF16)
            wload(Wb, s5win_d[:, fb * 512:(fb + 1) * 512].rearrange("(k p) c -> p k c", p=128), s)
            for i_ in range(8):
                i0 = nextps()
                for k in range(8):
                    mm(psb[i0][:, :], HT[:, k, i_:T:8], Wb[:, k, :], k == 0, k == 7, r=B_HT + [B_W[s]], w=[PS[i0]])
                cp(evac_eng(), ABAT[:, :, i_, :], psb[i0][:, :].rearrange("p (g m) -> p g m", m=16), r=[PS[i0]],
                   w=[B_AB])
            for q4 in range(8):
                i0 = nextps()
                for gg in range(4):
                    gl = q4 * 4 + gg
                    mm(psb[i0][:, gg * 128:(gg + 1) * 128], ABAT[:, gl, :, :].rearrange("p i m -> p (i m)"),
                       identb, True, True, r=[B_AB, B_K], w=[PS[i0]])
                g0 = fb * 32 + q4 * 4
                cp(evac_eng(), UY[:, g0:g0 + 4, :], psb[i0][:, :].rearrange("p (g n) -> p g n", n=128), r=[PS[i0]],
                   w=[B_UY[g0 // 8]])
            for fcl in range(4):
                fc = fb * 4 + fcl
                tb = tbrr[0] % 2
                tbrr[0] += 1
                dma("sp", WVs[tb], wv_d[:, fc * 8:(fc + 1) * 8, :], "tw%d" % tb, r=[B_SCR],
                    w=[B_TW[tb]])
                ire = nextps()
                iim = nextps()
                for gg in range(8):
                    g = fc * 8 + gg
                    p0 = (g % 2) * 64
                    pr = gg // 2
                    for (ii, c0) in ((ire, 0), (iim, 64)):
                        mm(psb[ii][p0:p0 + 64, pr * 128:(pr + 1) * 128], WVs[tb][:, gg, c0:c0 + 64], UY[:, g, :],
                           True, True, r=[B_TW[tb], B_UY[g // 8]], w=[PS[ii]])
                cp(evac_eng(), VS[:, 0, fc * 4:(fc + 1) * 4, :], psb[ire][:, :].rearrange("p (a n) -> p a n", n=128),
                   r=[PS[ire]], w=[B_VS])
                cp(evac_eng(), VS[:, 1, fc * 4:(fc + 1) * 4, :], psb[iim][:, :].rearrange("p (a n) -> p a n", n=128),
                   r=[PS[iim]], w=[B_VS])
        cur = 0
        for n in range(128):
            So = S3[cur]
            Sn = S3[1 - cur]
            Bo = B_S3[cur]
            Bn = B_S3[1 - cur]
            Bv = B_VSn[n % 2]
            tt("dve", M1, So[:, 0:2, :], AR2, ALU.mult, r=[Bo, B_AR], w=[B_M1])
            tt("dve", M2, So[:, 1::-1, :], AI2, ALU.mult, r=[Bo, B_AR], w=[B_M2])
            tt("dve", M1, M1, M2, ALU.add, r=[B_M1, B_M2], w=[B_M1])
            tt("dve", Sn[:, 0:2, :], M1, VS[:, :, :, n], ALU.add, r=[B_M1, B_VS, Bv], w=[Bn])
            cp("act", VS[:, :, :, n], So[:, 0:2, :], r=[Bo], w=[Bv])
            cur = 1 - cur
        for b_ in B_TB:
            P.alias(b_, [B_ABS[1]])
        for fc in range(16):
            tb = tbrr[0] % 2
            tbrr[0] += 1
            dma("sp", T0s[tb], t0_d[:, fc * 8:(fc + 1) * 8, :], "tbl%d" % tb, r=[B_SCR],
                w=[B_TB[tb]])
            for g2 in range(2):
                dma("sp", WCt2[tb][g2 * 64:(g2 + 1) * 64, :, :, :], wc_d[g2, :, fc * 4:(fc + 1) * 4, :, :],
                    "tbl%d" % tb, r=[B_SCR], w=[B_TB[tb]])
            banks = []
            for hb in range(2):
                i0 = nextps()
                banks.append(i0)
                for gg in range(4):
                    gl = hb * 4 + gg
                    g = fc * 8 + gl
                    p0 = (g % 2) * 64
                    pr = g // 2
                    o = psb[i0][:, gg * 128:(gg + 1) * 128]
                    RV = [B_VS, B_VSn[0], B_VSn[1], B_TB[tb]]
                    mm(o, UY[:, g, :], T0s[tb][:, gl, :], True, False, r=[B_UY[fc], B_TB[tb]], w=[PS[i0]])
                    mm(o, VS[p0:p0 + 64, 0, pr, :], WCt2[tb][p0:p0 + 64, gl // 2, 0, :], False, False, r=RV,
                       w=[PS[i0]])
                    mm(o, VS[p0:p0 + 64, 1, pr, :], WCt2[tb][p0:p0 + 64, gl // 2, 1, :], False, True, r=RV,
                       w=[PS[i0]])
            yav = UY[:, fc * 8:(fc + 1) * 8, :].rearrange("p g c -> p (g c)").rearrange(
                "p (j g m) -> p g j m", j=8, g=8, m=16)
            for hb in range(2):
                i0 = banks[hb]
                act(yav[:, hb * 4:(hb + 1) * 4, :, :],
                    psb[i0][:, :].rearrange("p (g j m) -> p g j m", g=4, j=8, m=16), AF.Gelu, r=[PS[i0]],
                    w=[B_UY[fc]])
        for b in B_YFM:
            P.alias(b, [B_VS, B_VSn[0], B_VSn[1]])
        for fc in range(16):
            for jh in range(2):
                i0 = nextps()
                for jj in range(4):
                    j_ = jh * 4 + jj
                    mm(psb[i0][:, jj * 128:(jj + 1) * 128], UY[:, fc * 8 + j_, :], identb,
                       True, True, r=[B_UY[fc], B_K], w=[PS[i0]])
                cp(evac_eng(), YFM[:, fc, :].rearrange("p (n j) -> p j n", j=8)[:, jh * 4:(jh + 1) * 4, :],
                   psb[i0][:, :].rearrange("p (j n) -> p j n", n=128), r=[PS[i0]], w=[B_YFM[fc]])
        for b in B_Y2:
            P.alias(b, B_UY)
        for fb in range(4):
            s = wslot()
            Wg = A.view(WOFF[s], [128, 16, 512], BF16)
            wload(Wg, wglu_d[:, fb * 512:(fb + 1) * 512].rearrange("(k p) c -> p k c", p=128), s)
            s2_ = wslot()
            Wz = A.view(WOFF[s2_], [128, 8, 512], BF16)
            wload(Wz, s5win_d[:, E + fb * 512:E + (fb + 1) * 512].rearrange("(k p) c -> p k c", p=128), s2_)
            for ftl in range(4):
                ft = fb * 4 + ftl
                for th in range(2):
                    tsl = slice(th * 512, (th + 1) * 512)
                    ig = nextps()
                    for k in range(16):
                        mm(psb[ig][:, :], Wg[:, k, ftl * 128:(ftl + 1) * 128], YFM[:, k, tsl], k == 0, k == 15,
                           r=[B_W[s]] + B_YFM, w=[PS[ig]])
                    iz = nextps()
                    for k in range(8):
                        mm(psb[iz][:, :], Wz[:, k, ftl * 128:(ftl + 1) * 128], HT[:, k, tsl], k == 0, k == 7,
                           r=[B_W[s2_]] + B_HT, w=[PS[iz]])
                    act(TMP2, psb[ig][:, :], AF.Tanh, r=[PS[ig]], w=[B_T2], scale=0.5)
                    act(TMP3, psb[iz][:, :], AF.Tanh, r=[PS[iz]], w=[B_T3], scale=0.5)
                    stt(TMP2, TMP2, 1.0, YFM[:, ft, tsl], ALU.add, ALU.mult, r=[B_T2, B_YFM[ft]], w=[B_T2])
                    stt(TMP3, TMP3, 1.0, psb[iz][:, :], ALU.add, ALU.mult, r=[B_T3, PS[iz]], w=[B_T3])
                    tt("dve", Y2[:, ft, tsl], TMP2, TMP3, ALU.mult, r=[B_T2, B_T3], w=[B_Y2[ft]])
        out_proj(s5wo_d, 0)

    def out_proj(w_d, layer):
        for dh in range(2):
            s = wslot()
            Wo = A.view(WOFF[s], [128, 16, 512], BF16)
            wload(Wo, w_d[:, dh * 512:(dh + 1) * 512].rearrange("(k p) c -> p k c", p=128), s)
            for tti in range(NT):
                i0 = nextps()
                for k in range(16):
                    mm(psb[i0][:, :], Y2[:, k, tti * 128:(tti + 1) * 128], Wo[:, k, :], k == 0, k == 15,
                       r=B_Y2 + [B_W[s]], w=[PS[i0]])
                tt("dve", TMP1, psb[i0][:, :], GATEB[:, layer, dh * 512:(dh + 1) * 512], ALU.mult,
                   r=[PS[i0], B_MOD], w=[B_T1])
                tt("dve", XS[:, tti, dh * 512:(dh + 1) * 512], XS[:, tti, dh * 512:(dh + 1) * 512], TMP1, ALU.add,
                   r=[B_T1, B_X[tti]], w=[B_X[tti]])

    def final_norm(ch):
        t0 = ch * T
        for tti in range(NT):
            xt = XS[:, tti, :]
            s = tti % 2
            act(XN[s], xt, AF.Square, r=[B_X[tti]], w=[B_XN[s], B_ST], accum=STAT[:, tti:tti + 1])
            act(STAT[:, 8 + tti:9 + tti], STAT[:, tti:tti + 1], AF.Ln, r=[B_ST], w=[B_ST], scale=1.0 / D, bias=EPS)
            act(STAT[:, 16 + tti:17 + tti], STAT[:, 8 + tti:9 + tti], AF.Exp, r=[B_ST], w=[B_ST], scale=-0.5)
            stt(xt, xt, STAT[:, 16 + tti:17 + tti], FNWB, ALU.mult, ALU.mult, r=[B_X[tti], B_ST, B_MOD],
                w=[B_X[tti]])
            dma("sp", out_d[t0 + tti * 128:t0 + (tti + 1) * 128, :], XS[:, tti, :], "o%d" % tti, r=[B_X[tti]],
                is_out=True)
            if ch + 1 < NCH:
                t1_ = t0 + T
                dma("sp", XS[:, tti, :], x_d[t1_ + tti * 128:t1_ + (tti + 1) * 128, :], "x%d" % tti, w=[B_X[tti]])


    TH = 256
    NTH = TH // 128
    NQ = T // TH

    def mkstream(i):
        rgs = Reg(BIG2 + i * 4096, 16384)
        rss = Reg(SMALL + i * 1280, 5120)
        d = {}
        d["QN"] = rgs.new([128, TH], BF16)
        d["KN"] = rgs.new([128, TH], BF16)
        d["VT"] = rgs.new([128, NTH, 256], BF16)
        d["KD"] = rgs.new([128, NTH, 128], BF16)
        d["MT"] = rgs.new([128, NTH, 128], BF16)
        d["QKM"] = rgs.new([128, NTH, 128], BF16)
        d["SZ"] = rgs.new([128, 2, TH], BF16)
        d["VA"] = rgs.new([128, 2, TH], BF16)
        d["DG"] = rgs.new([128, NTH, 128], F32)
        d["BM"] = rgs.new([128, NTH, 128], BF16)
        ov = rgs.base + rgs.off
        rd_ = Reg(ov, 8192)
        for nm in ("ET", "NP", "NT", "P0", "P1", "PT0", "PT1", "RF"):
            d[nm] = rd_.new([128, NTH, 128], F32)
        ra_ = Reg(ov, 8192)
        for j in range(2):
            d["PRE%d" % j] = ra_.new([128, TH + 8], BF16)
            d["CT%d" % j] = ra_.new([128, TH], F32)
            d["TA%d" % j] = ra_.new([128, TH], F32)
            d["SQ%d" % j] = ra_.new([128, TH], BF16)
        d["DW"] = A.view(SMALL + 2560 + i * 1024, [128, 16, 128], BF16)
        d["RM"] = rss.new([128, 256], BF16)
        d["VN"] = rss.new([128, 256], BF16)
        d["OV"] = rss.new([128, 256], F32)
        d["OT0"] = rss.new([128, 256], F32)
        d["OT1"] = rss.new([128, 256], F32)
        d["ON"] = rss.new([128, 256], BF16)
        d["SB"] = rss.new([128, 256], BF16)
        for k_ in list(d.keys()):
            d["B_" + k_] = Buf("%s_%d" % (k_, i))
        d["RA_B"] = [d["B_" + n] for n in ("PRE0", "CT0", "TA0", "SQ0", "PRE1", "CT1", "TA1", "SQ1")]
        d["RD_B"] = [d["B_" + n] for n in ("ET", "NP", "NT", "P0", "P1", "PT0", "PT1", "RF")]
        d["sid"] = i
        return d

    GST = [mkstream(0), mkstream(1)]
    B_WS = [[Buf("ws%d_%d" % (i, j)) for j in range(4)] for i in range(2)]

    def gsm(i):
        return GSM[:, i, :].rearrange("p (a h) -> p a h", h=8)

    BETA, GG, GC, GLT, EGC, NEGEGC, EKD, EGL0, EGL1, GTMP = [gsm(i) for i in range(10)]
    ABR = GSM[:, 10:12, :].rearrange("p a c -> p (a c)").rearrange("p (t c) -> p t c", c=16)

    def vn(ap):
        return ap.rearrange("p (a i) -> p a i", i=128)

    idn = ident.unsqueeze(1).to_broadcast([128, NTH, 128])
    smn = CST[:, C_SMASK:C_SMASK + 128].unsqueeze(1).to_broadcast([128, NTH, 128])
    ngn = CST[:, C_NEGM:C_NEGM + 128].unsqueeze(1).to_broadcast([128, NTH, 128])
    WH = {}
    W2 = NTH * 128

    def gdn_gates():
        i0 = nextps()
        for tti in range(NT):
            for k in range(8):
                mm(psb[i0][:, tti * 16:(tti + 1) * 16], HT[:, k, tti * 128:(tti + 1) * 128], WAB[:, k, :], k == 0,
                   k == 7, r=[B_HT[tti], B_WAB], w=[PS[i0]])
        G = [B_GSM]
        cp("dve", ABR, psb[i0][:, 0:128].rearrange("p (t c) -> p t c", c=16), r=[PS[i0]], w=G)
        act(BETA, ABR[:, :, 0:8], AF.Tanh, r=G, w=G, scale=0.5)
        ts("dve", BETA, BETA, 0.5, ALU.mult, r=G, w=G, s2=0.5, op1=ALU.add)
        tt("dve", GTMP, ABR[:, :, 8:16], DTBB.unsqueeze(1).to_broadcast([128, 8, 8]), ALU.add, r=G + [B_GC], w=G)
        act(GTMP, GTMP, AF.Exp, r=G, w=G)
        act(GTMP, GTMP, AF.Ln, r=G, w=G, bias=1.0)
        tt("dve", GG, GTMP, NEXPA.unsqueeze(1).to_broadcast([128, 8, 8]), ALU.mult, r=G + [B_GC], w=G)
        i0 = nextps()
        grhs = GSM[:, 1, :]
        mm(psb[i0][:, 0:64], CST[:, C_TRI:C_TRI + 128], grhs, True, True, r=G + [B_CST], w=[PS[i0]])
        mm(psb[i0][0:64, 64:128], CST[:, C_CH0:C_CH0 + 64], grhs, True, True, r=G + [B_CST], w=[PS[i0]])
        mm(psb[i0][64:128, 64:128], CST[:, C_CH1 + 64:C_CH1 + 128], grhs, True, True, r=G + [B_CST], w=[PS[i0]])
        mm(psb[i0][:, 128:192], CST[:, C_CH0:C_CH0 + 128], grhs, True, True, r=G + [B_CST], w=[PS[i0]])
        mm(psb[i0][:, 192:256], CST[:, C_CH1:C_CH1 + 128], grhs, True, True, r=G + [B_CST], w=[PS[i0]])
        cp("dve", GSM[:, 2, :], psb[i0][:, 0:64], r=[PS[i0]], w=G)
        act(GSM[:, 4, :], psb[i0][:, 0:64], AF.Exp, r=[PS[i0]], w=G)
        ts("dve", GSM[:, 5, :], GSM[:, 4, :], -1.0, ALU.mult, r=G, w=G)
        tt("dve", GSM[:, 3, :], psb[i0][:, 64:128], GSM[:, 2, :], ALU.subtract, r=[PS[i0]] + G, w=G)
        act(GSM[:, 6, :], GSM[:, 3, :], AF.Exp, r=G, w=G)
        act(GSM[:, 7, :], psb[i0][:, 128:192], AF.Exp, r=[PS[i0]], w=G)
        act(GSM[:, 8, :], psb[i0][:, 192:256], AF.Exp, r=[PS[i0]], w=G)

    def gdn_prep(h, qi, S):
        sid = S["sid"]
        tb0 = qi * NTH
        hs = slice(qi * TH, (qi + 1) * TH)
        if qi == 0:
            s = sid
            Wh = A.view(WOFF[s], [128, 8, 768], BF16)
            c0 = 0
            for j, (src0, n_) in enumerate(((h * 128, 128), (1024 + h * 128, 128), (2048 + h * 256, 256),
                                           (4096 + h * 256, 256))):
                dma("pool", Wh[:, :, c0:c0 + n_], gwin_d[:, src0:src0 + n_].rearrange("(k p) c -> p k c", p=128),
                    "ws%d_%d" % (s, j), w=[B_WS[s][j]])
                c0 += n_
            WH[h] = (Wh, s)
            yield
            for jt, tidx_ in enumerate((h, 8 + h, 16 + 2 * h, 17 + 2 * h)):
                for tap in range(4):
                    ts("pool", S["DW"][:, jt * 4 + tap, :], ident, CW[:, tap, tidx_:tidx_ + 1], ALU.mult,
                       r=[B_CST, B_GC], w=[S["B_DW"]])
                yield
        Wh, s = WH[h]
        for b in S["RA_B"]:
            P.alias(b, S["RD_B"])
        pairs = ((("q", 0, h, 0, 0), ("k", 128, 8 + h, 0, 1)),
                 (("v", 256, 16 + 2 * h, 0, 2), ("v", 384, 17 + 2 * h, 1, 2)))
        for pair in pairs:
            pi = []
            for j, (name, wc0, tidx, sub, seg) in enumerate(pair):
                i0 = nextps()
                for k in range(8):
                    mm(psb[i0][:, 0:TH], Wh[:, k, wc0:wc0 + 128], HT[:, k, hs], k == 0, k == 7,
                       r=[B_WS[s][seg]] + B_HT, w=[PS[i0]])
                pi.append(i0)
                yield
            pcs = []
            for j, (name, wc0, tidx, sub, seg) in enumerate(pair):
                PRE, B_PRE = S["PRE%d" % j], S["B_PRE%d" % j]
                cp("act", PRE[:, 3:3 + TH], psb[pi[j]][:, 0:TH], r=[PS[pi[j]]], w=[B_PRE])
                cp("pool", PRE[:, 0:3], HIST[:, tidx, :], r=[B_HIST], w=[B_PRE])
                yield
            for j, (name, wc0, tidx, sub, seg) in enumerate(pair):
                PRE, B_PRE = S["PRE%d" % j], S["B_PRE%d" % j]
                jt = (0 if name == "q" else 1) if name != "v" else 2 + sub
                ic = nextps()
                for tap in range(4):
                    mm(psb[ic][:, 0:TH], S["DW"][:, jt * 4 + tap, :], PRE[:, tap:tap + TH], tap == 0, tap == 3,
                       r=[S["B_DW"], B_PRE], w=[PS[ic]])
                cp("pool", HIST[:, tidx, :], PRE[:, TH:TH + 3], r=[B_PRE], w=[B_HIST])
                pcs.append(ic)
                yield
            for j, (name, wc0, tidx, sub, seg) in enumerate(pair):
                CT, B_CT = S["CT%d" % j], S["B_CT%d" % j]
                TA, B_TA = S["TA%d" % j], S["B_TA%d" % j]
                act(TA, psb[pcs[j]][:, 0:TH], AF.Tanh, r=[PS[pcs[j]]], w=[B_TA], scale=0.5)
                yield
                if name == "v":
                    stt(S["VA"][:, sub, :], TA, 1.0, psb[pcs[j]][:, 0:TH], ALU.add, ALU.mult,
                        r=[B_TA, PS[pcs[j]]], w=[S["B_VA"]])
                else:
                    stt(CT, TA, 1.0, psb[pcs[j]][:, 0:TH], ALU.add, ALU.mult, r=[B_TA, PS[pcs[j]]], w=[B_CT])
                yield
            if pair[0][0] == "v":
                continue
            ips = []
            for j in range(2):
                CT, B_CT = S["CT%d" % j], S["B_CT%d" % j]
                SQ, B_SQ = S["SQ%d" % j], S["B_SQ%d" % j]
                act(SQ, CT, AF.Square, r=[B_CT], w=[B_SQ])
                i0 = nextps()
                mm(psb[i0][:, 0:TH], onesb, SQ, True, True, r=[B_SQ, B_K], w=[PS[i0]])
                ips.append(i0)
                yield
            for j in range(2):
                CT, B_CT = S["CT%d" % j], S["B_CT%d" % j]
                TA, B_TA = S["TA%d" % j], S["B_TA%d" % j]
                act(TA, psb[ips[j]][:, 0:TH], AF.Ln, r=[PS[ips[j]]], w=[B_TA], bias=4.0 * EPS)
                act(TA, TA, AF.Exp, r=[B_TA], w=[B_TA], scale=-0.5)
                if j == 0:
                    stt(S["QN"], CT, 128.0 ** -0.5, TA, ALU.mult, ALU.mult, r=[B_CT, B_TA], w=[S["B_QN"]])
                else:
                    stt(S["KN"], CT, 1.0, TA, ALU.mult, ALU.mult, r=[B_CT, B_TA], w=[S["B_KN"]])
                yield
        for sub in range(2):
            wc0 = 512 + sub * 128
            i0 = nextps()
            for k in range(8):
                mm(psb[i0][:, 0:TH], Wh[:, k, wc0:wc0 + 128], HT[:, k, hs], k == 0, k == 7,
                   r=[B_WS[s][3]] + B_HT, w=[PS[i0]])
            TA, B_TA = S["TA%d" % sub], S["B_TA%d" % sub]
            act(TA, psb[i0][:, 0:TH], AF.Tanh, r=[PS[i0]], w=[B_TA], scale=0.5)
            stt(S["SZ"][:, sub, :], TA, 1.0, psb[i0][:, 0:TH], ALU.add, ALU.mult, r=[B_TA, PS[i0]], w=[S["B_SZ"]])
            yield
        KN_ = S["KN"]
        QN_ = S["QN"]
        i0 = nextps()
        for tl in range(NTH):
            mm(psb[i0][:, tl * 128:(tl + 1) * 128], KN_[:, tl * 128:(tl + 1) * 128], identb, True, True,
               r=[S["B_KN"], B_K], w=[PS[i0]])
        tt("dve", S["KD"], vn(psb[i0][:, 0:W2]), EKD[:, tb0:tb0 + NTH, h:h + 1].to_broadcast([128, NTH, 128]),
           ALU.mult, r=[PS[i0], B_GSM], w=[S["B_KD"]])
        yield
        i0 = nextps()
        for tl in range(NTH):
            for vt in range(2):
                o = tl * 256 + vt * 128
                mm(psb[i0][:, o:o + 128], S["VA"][:, vt, tl * 128:(tl + 1) * 128], identb, True, True,
                   r=[S["B_VA"], B_K], w=[PS[i0]])
        act(S["VT"], psb[i0][:, :].rearrange("p (a e) -> p a e", e=256), AF.Identity, r=[PS[i0]], w=[S["B_VT"]],
            scale=0.5)
        yield
        for b in S["RD_B"]:
            P.alias(b, S["RA_B"])
        DG, BM = S["DG"], S["BM"]
        tt("dve", DG, idn, GC[:, tb0:tb0 + NTH, h:h + 1].to_broadcast([128, NTH, 128]), ALU.mult,
           r=[B_CST, B_GSM], w=[S["B_DG"]])
        tt("dve", BM, smn, BETA[:, tb0:tb0 + NTH, h:h + 1].to_broadcast([128, NTH, 128]), ALU.mult,
           r=[B_CST, B_GSM], w=[S["B_BM"]])
        yield
        ikk = nextps()
        iqk = nextps()
        igd = nextps()
        for tl in range(NTH):
            cs = slice(tl * 128, (tl + 1) * 128)
            mm(psb[ikk][:, cs], KN_[:, cs], KN_[:, cs], True, True, r=[S["B_KN"]], w=[PS[ikk]])
            mm(psb[iqk][:, cs], KN_[:, cs], QN_[:, cs], True, True, r=[S["B_KN"], S["B_QN"]], w=[PS[iqk]])
            mm(psb[igd][:, cs], onesf, DG[:, tl, :], True, False, r=[B_K, S["B_DG"]], w=[PS[igd]])
            mm(psb[igd][:, cs], DG[:, tl, :], NEGONES, False, False, r=[S["B_DG"], B_GC], w=[PS[igd]])
            mm(psb[igd][:, cs], ident, CST[:, C_NEGM:C_NEGM + 128], False, True, r=[B_CST], w=[PS[igd]])
        yield
        ET, NP_, NT_, RF = S["ET"], S["NP"], S["NT"], S["RF"]
        Pb = [S["P0"], S["P1"]]
        PTb = [S["PT0"], S["PT1"]]
        B_PB = [S["B_P0"], S["B_P1"]]
        B_PTB = [S["B_PT0"], S["B_PT1"]]
        act(ET, vn(psb[igd][:, 0:W2]), AF.Exp, r=[PS[igd]], w=[S["B_ET"]])
        yield
        tt("dve", S["QKM"], vn(psb[iqk][:, 0:W2]), ET, ALU.mult, r=[PS[iqk], S["B_ET"]], w=[S["B_QKM"]])
        tt("dve", NP_, vn(psb[ikk][:, 0:W2]), ET, ALU.mult, r=[PS[ikk], S["B_ET"]], w=[S["B_NP"]])
        yield
        tt("dve", NP_, NP_, BM, ALU.mult, r=[S["B_NP"], S["B_BM"]], w=[S["B_NP"]])
        yield
        i0 = nextps()
        for tl in range(NTH):
            o = slice(tl * 128, (tl + 1) * 128)
            mm(psb[i0][:, o], NP_[:, tl, :], ident, True, True, r=[S["B_NP"], B_CST], w=[PS[i0]])
        cp("act", NT_, vn(psb[i0][:, 0:W2]), r=[PS[i0]], w=[S["B_NT"]])
        tt("dve", RF, idn, NP_, ALU.subtract, r=[B_CST, S["B_NP"]], w=[S["B_RF"]])
        yield
        ip = nextps()
        ipt = nextps()
        for tl in range(NTH):
            o = slice(tl * 128, (tl + 1) * 128)
            mm(psb[ip][:, o], NT_[:, tl, :], NP_[:, tl, :], True, True, r=[S["B_NT"], S["B_NP"]], w=[PS[ip]])
            mm(psb[ipt][:, o], NP_[:, tl, :], NT_[:, tl, :], True, True, r=[S["B_NT"], S["B_NP"]], w=[PS[ipt]])
        cur = 0
        cp("act", Pb[0], vn(psb[ip][:, 0:W2]), r=[PS[ip]], w=[B_PB[0]])
        cp("dve", PTb[0], vn(psb[ipt][:, 0:W2]), r=[PS[ipt]], w=[B_PTB[0]])
        yield
        for k in range(1, 6):
            iq = nextps()
            for tl in range(NTH):
                o = slice(tl * 128, (tl + 1) * 128)
                mm(psb[iq][:, o], PTb[cur][:, tl, :], RF[:, tl, :], True, True, r=[B_PTB[cur], S["B_RF"]],
                   w=[PS[iq]])
            if k < 5:
                ip = nextps()
                ipt = nextps()
                for tl in range(NTH):
                    o = slice(tl * 128, (tl + 1) * 128)
                    mm(psb[ip][:, o], PTb[cur][:, tl, :], Pb[cur][:, tl, :], True, True,
                       r=[B_PTB[cur], B_PB[cur]], w=[PS[ip]])
                    mm(psb[ipt][:, o], Pb[cur][:, tl, :], PTb[cur][:, tl, :], True, True,
                       r=[B_PTB[cur], B_PB[cur]], w=[PS[ipt]])
            yield
            tt("dve", RF, RF, vn(psb[iq][:, 0:W2]), ALU.add, r=[S["B_RF"], PS[iq]], w=[S["B_RF"]])
            if k < 5:
                cp("act", Pb[1 - cur], vn(psb[ip][:, 0:W2]), r=[PS[ip]], w=[B_PB[1 - cur]])
                cp("dve", PTb[1 - cur], vn(psb[ipt][:, 0:W2]), r=[PS[ipt]], w=[B_PTB[1 - cur]])
                cur = 1 - cur
            else:
                cp("act", S["MT"], RF, r=[S["B_RF"]], w=[S["B_MT"]])
            yield

    def gdn_loop(h, qi, S):
        sid = S["sid"]
        tb0 = qi * NTH
        KN_, QN_, VT_, KD_, MT_, QKM_, SZ_ = S["KN"], S["QN"], S["VT"], S["KD"], S["MT"], S["QKM"], S["SZ"]
        RM, VN, OV, ON, SB = S["RM"], S["VN"], S["OV"], S["ON"], S["SB"]
        B_RM, B_VN, B_OV, B_ON, B_SB = S["B_RM"], S["B_VN"], S["B_OV"], S["B_ON"], S["B_SB"]
        st0 = 32 + 4 * sid
        cp("act", SB, GS[:, h, :], r=[B_GS[h]], w=[B_SB])
        for cidx in range(2 * NTH):
            tl = cidx // 2
            t_ = tb0 + tl
            hf = cidx % 2
            p0 = hf * 64
            cs = slice(tl * 128 + p0, tl * 128 + p0 + 64)
            pp = slice(p0, p0 + 64)
            iks = nextps()
            mm(psb[iks][pp, 0:256], KN_[:, cs], SB, True, True, r=[S["B_KN"], B_SB], w=[PS[iks]])
            iqs = nextps()
            mm(psb[iqs][pp, 0:256], QN_[:, cs], SB, True, True, r=[S["B_QN"], B_SB], w=[PS[iqs]])
            yield
            stt(RM[pp, :], psb[iks][pp, 0:256], NEGEGC[pp, t_, h:h + 1], VT_[pp, tl, :], ALU.mult, ALU.add,
                r=[PS[iks], B_GSM, S["B_VT"]], w=[B_RM])
            yield
            ivn = nextps()
            mm(psb[ivn][pp, 0:256], MT_[pp, tl, p0:p0 + 64], RM[pp, :], True, True, r=[S["B_MT"], B_RM],
               w=[PS[ivn]])
            yield
            act(VN[pp, :], psb[ivn][pp, 0:256], AF.Identity, r=[PS[ivn], B_GSM], w=[B_VN],
                scale=BETA[pp, t_, h:h + 1])
            yield
            isu = nextps()
            mm(psb[isu][:, 0:256], KD_[pp, tl, :], VN[pp, :], True, True, r=[S["B_KD"], B_VN], w=[PS[isu]])
            iqv = nextps()
            mm(psb[iqv][pp, 0:256], QKM_[pp, tl, p0:p0 + 64], VN[pp, :], True, True, r=[S["B_QKM"], B_VN],
               w=[PS[iqv]])
            yield
            egl = EGL0 if hf == 0 else EGL1
            stt(GS[:, h, :], GS[:, h, :], egl[:, t_, h:h + 1], psb[isu][:, 0:256], ALU.mult, ALU.add,
                r=[B_GS[h], B_GSM, PS[isu]], w=[B_GS[h]])
            cp("act", OV[pp, :], psb[iqv][pp, 0:256], r=[PS[iqv]], w=[B_OV])
            yield
            cp("act", SB, GS[:, h, :], r=[B_GS[h]], w=[B_SB])
            Ot = S["OT%d" % (tl % 2)]
            B_Ot = S["B_OT%d" % (tl % 2)]
            stt(Ot[pp, :], psb[iqs][pp, 0:256], EGC[pp, t_, h:h + 1], OV[pp, :], ALU.mult, ALU.add,
                r=[PS[iqs], B_GSM, B_OV], w=[B_Ot])
            yield
            if hf == 1:
                act(ON, Ot, AF.Square, r=[B_Ot], w=[B_ON, B_ST], accum=STAT[:, st0:st0 + 1])
                yield
                act(STAT[:, st0 + 1:st0 + 2], STAT[:, st0:st0 + 1], AF.Ln, r=[B_ST], w=[B_ST], scale=1.0 / 256,
                    bias=EPS)
                act(STAT[:, st0 + 2:st0 + 3], STAT[:, st0 + 1:st0 + 2], AF.Exp, r=[B_ST], w=[B_ST], scale=-0.5)
                yield
                stt(ON, Ot, STAT[:, st0 + 2:st0 + 3], GNWB, ALU.mult, ALU.mult, r=[B_Ot, B_ST, B_GC], w=[B_ON])
                yield
                i0 = nextps()
                for et in range(2):
                    mm(psb[i0][:, et * 128:(et + 1) * 128], ON[:, et * 128:(et + 1) * 128], identb, True, True,
                       r=[B_ON, B_K], w=[PS[i0]])
                yield
                tt("dve", Y2[:, 2 * h:2 * h + 2, t_ * 128:(t_ + 1) * 128],
                   psb[i0][:, 0:256].rearrange("p (a n) -> p a n", n=128), SZ_[:, :, tl * 128:(tl + 1) * 128],
                   ALU.mult, r=[PS[i0], S["B_SZ"]], w=[B_Y2[2 * h], B_Y2[2 * h + 1]])
                yield

    def head_gen(h, S):
        for qi in range(NQ):
            yield from gdn_prep(h, qi, S)
            yield from gdn_loop(h, qi, S)

    def gdn_layer(ch):
        P.barrier()
        norm_transpose(1)
        gdn_gates()
        for hp in range(4):
            gens = [head_gen(2 * hp, GST[0]), head_gen(2 * hp + 1, GST[1])]
            alive = [True, True]
            pspool[0] = [0, 1, 2, 3]
            for _ in range(OFFSET):
                try:
                    next(gens[0])
                except StopIteration:
                    alive[0] = False
                    break
            if LOCKSTEP == 0:
                for g_ in gens:
                    for _ in g_:
                        pass
                alive = [False, False]
            while alive[0] or alive[1]:
                for gi in range(2):
                    pspool[0] = [0, 1, 2, 3] if gi == 0 else [4, 5, 6, 7]
                    for _rep in range(max(1, LOCKSTEP)):
                        if alive[gi]:
                            try:
                                next(gens[gi])
                            except StopIteration:
                                alive[gi] = False
            pspool[0] = None
        P.barrier()
        out_proj(gwo_d, 1)
        P.barrier()

    for ch in range(NCH):
        t0 = ch * T
        if ch == 0 or stage < 2:
            for tti in range(NT):
                dma("sp", XS[:, tti, :], x_d[t0 + tti * 128:t0 + (tti + 1) * 128, :], "x%d" % tti, w=[B_X[tti]])
        s5_layer(ch)
        if stage >= 2:
            gdn_layer(ch)
            final_norm(ch)
        else:
            for tti in range(NT):
                dma("sp", out_d[t0 + tti * 128:t0 + (tti + 1) * 128, :], XS[:, tti, :], "o%d" % tti,
                    r=[B_X[tti]], is_out=True)

    P.emit(nc, es)
    es.close()
    return nc


def make_in_maps(inputs, seq, ncores):
    f = lambda a: np.ascontiguousarray(np.asarray(a, dtype=np.float32))
    shared = {
        "ada_w": f(inputs["ada_w"]),
        "ada_b": f(inputs["ada_b"]).reshape(1, 6 * D),
        "norm_w": f(inputs["norm_w"]).reshape(16, 128),
        "s5_w_in": f(inputs["s5_w_in"])[0],
        "s5_lambda_re": f(inputs["s5_lambda_re"])[0],
        "s5_lambda_im": f(inputs["s5_lambda_im"])[0],
        "s5_log_dt": f(inputs["s5_log_dt"])[0].reshape(128, 1),
        "s5_b_re": f(inputs["s5_b_re"])[0].reshape(128, 1024),
        "s5_b_im": f(inputs["s5_b_im"])[0].reshape(128, 1024),
        "s5_c_re": f(inputs["s5_c_re"])[0].reshape(128, 1024),
        "s5_c_im": f(inputs["s5_c_im"])[0].reshape(128, 1024),
        "s5_d": f(inputs["s5_d"])[0].reshape(128, 16),
        "s5_w_glu": f(inputs["s5_w_glu"])[0],
        "s5_w_out": f(inputs["s5_w_out"])[0],
        "gdn_w_in": f(inputs["gdn_w_in"])[0],
        "gdn_conv_w": f(inputs["gdn_conv_w"])[0].reshape(128, 128),
        "gdn_a_log": f(inputs["gdn_a_log"]).reshape(1, 8),
        "gdn_dt_bias": f(inputs["gdn_dt_bias"]).reshape(1, 8),
        "gdn_norm_w": f(inputs["gdn_norm_w"]).reshape(1, 256),
        "gdn_w_out": f(inputs["gdn_w_out"])[0],
        "final_norm_w": f(inputs["final_norm_w"]).reshape(1, D),
        "cst": make_consts(),
    }
    x = f(inputs["x"])
    c = f(inputs["c"])
    maps = []
    for b in range(ncores):
        m = dict(shared)
        m["x"] = np.ascontiguousarray(x[b, :seq])
        m["c"] = np.ascontiguousarray(c[b].reshape(8, 128))
        maps.append(m)
    return maps


_NC_CACHE = {}


def kernel(**inputs):
    x = np.asarray(inputs["x"])
    nb, seq, _ = x.shape
    key = (seq, 2)
    if key not in _NC_CACHE:
        _NC_CACHE[key] = build(seq, 2)
    nc = _NC_CACHE[key]
    maps = make_in_maps(inputs, seq, nb)
    res = run_bass_kernel_spmd(nc, maps, core_ids=list(range(nb)))
    out = np.stack([np.asarray(r["out"], dtype=np.float32) for r in res.results], axis=0)
    return out
```

```python
import math
from contextlib import ExitStack
import numpy as np
import concourse.bass as bass
import concourse.mybir as mybir
from concourse.bass_utils import run_bass_kernel_spmd

F32 = mybir.dt.float32
BF16 = mybir.dt.bfloat16
AF = mybir.ActivationFunctionType
ALU = mybir.AluOpType
AX = mybir.AxisListType

D = 1024
E = 2048
T = 1024
NT = T // 128
EPS = 1e-6
ENGS = ("pe", "act", "dve", "pool", "sp")
import os
LOCKSTEP = int(os.environ.get("K_LOCKSTEP", "1"))
OFFSET = int(os.environ.get("K_OFFSET", "0"))


def _dtsize(dt):
    return 4 if dt == F32 else 2


class Buf:
    __slots__ = ("name", "w", "r")

    def __init__(self, name):
        self.name = name
        self.w = None
        self.r = []


class Prog:
    def __init__(self):
        self.ops = {e: [] for e in ENGS}
        self.seen = {e: {} for e in ENGS}
        self.dcnt = {}
        self.last = {e: None for e in ENGS}
        self.pending = {e: [] for e in ENGS}
        self.out_events = []

    def _waits(self, eng, r, w):
        raw = []
        oth = []
        for b in r:
            if b.w is not None:
                raw.append(b.w)
        for b in w:
            if b.w is not None:
                oth.append(b.w)
            oth.extend(b.r)
        d = {}
        for ev in raw:
            st, pos = ev
            if st == eng and eng == "pe":
                continue
            d[st] = max(d.get(st, 0), pos)
        for ev in oth:
            st, pos = ev
            if st == eng:
                continue
            d[st] = max(d.get(st, 0), pos)
        for st, pos in self.pending[eng]:
            d[st] = max(d.get(st, 0), pos)
        self.pending[eng] = []
        out = []
        for st, pos in d.items():
            if self.seen[eng].get(st, 0) >= pos:
                continue
            self.seen[eng][st] = pos
            out.append((st, pos))
        return out

    def op(self, eng, fn, r=(), w=()):
        waits = self._waits(eng, r, w)
        pos = len(self.ops[eng]) + 1
        ev = (eng, pos)
        self.ops[eng].append((waits, fn, ev, None))
        self.last[eng] = ev
        for b in r:
            b.r.append(ev)
        for b in w:
            b.w = ev
            b.r = []
        return ev

    def dma(self, q, fn, sem, r=(), w=(), is_out=False):
        waits = self._waits(q, r, w)
        self.dcnt[sem] = self.dcnt.get(sem, 0) + 16
        ev = ("dma:" + sem, self.dcnt[sem])
        self.ops[q].append((waits, fn, None, sem))
        for b in r:
            b.r.append(ev)
        for b in w:
            b.w = ev
            b.r = []
        if is_out:
            self.out_events.append(ev)
        return ev

    def alias(self, new, olds):
        for o in olds:
            if o.w is not None:
                new.r.append(o.w)
            new.r.extend(o.r)

    def barrier(self):
        evs = [self.last[e] for e in ENGS if self.last[e] is not None]
        evs += [("dma:" + s, c) for s, c in self.dcnt.items()]
        for e in ENGS:
            for ev in evs:
                if ev[0] == e and e == "pe":
                    continue
                self.pending[e].append(ev)

    def emit(self, nc, es):
        needed = {e: set() for e in ENGS}
        for e in ENGS:
            for waits, fn, ev, dsem in self.ops[e]:
                for st, pos in waits:
                    if not st.startswith("dma:"):
                        needed[st].add(pos)
        rank = {e: {p: i + 1 for i, p in enumerate(sorted(needed[e]))} for e in ENGS}
        esem = {e: es.enter_context(nc.semaphore("s_" + e)) for e in ENGS}
        dsem = {s: es.enter_context(nc.semaphore("d_" + s)) for s in self.dcnt}
        fin = {}
        for st, pos in self.out_events:
            fin[st] = max(fin.get(st, 0), pos)
        block = es.enter_context(nc.Block())

        def replay(e, eng):
            for waits, fn, ev, ds in self.ops[e]:
                for st, pos in waits:
                    if st.startswith("dma:"):
                        eng.wait_ge(dsem[st[4:]], pos)
                    else:
                        eng.wait_ge(esem[st], rank[st][pos])
                ins = fn(eng)
                if ds is not None:
                    ins.then_inc(dsem[ds], 16)
                elif ev[1] in needed[e]:
                    ins.then_inc(esem[e], 1)
            if e == "sp":
                for st, pos in fin.items():
                    eng.wait_ge(dsem[st[4:]], pos)

        @block.tensor
        def _(t):
            replay("pe", t)

        @block.scalar
        def _(t):
            replay("act", t)

        @block.vector
        def _(t):
            replay("dve", t)

        @block.gpsimd
        def _(t):
            replay("pool", t)

        @block.sync
        def _(t):
            replay("sp", t)


class Arena:
    def __init__(self, nc, es, words):
        self.t = es.enter_context(nc.sbuf_tensor("arena", [128, words], F32))
        self.words = words
        self.off = 0

    def alloc(self, nbytes):
        w = (nbytes + 31) // 32 * 8
        off = self.off
        self.off += w
        assert self.off <= self.words, ("arena overflow", self.off, self.words)
        return off

    def view(self, off, shape, dt, p0=0):
        n = 1
        for s in shape[1:]:
            n *= s
        words = n * _dtsize(dt) // 4
        ap = self.t[p0:p0 + shape[0], off:off + words]
        if dt != F32:
            ap = ap.bitcast(dt)
        if len(shape) == 3:
            ap = ap.rearrange("p (a b) -> p a b", a=shape[1], b=shape[2])
        elif len(shape) == 4:
            ap = ap.rearrange("p (a b c) -> p a b c", a=shape[1], b=shape[2], c=shape[3])
        return ap

    def new(self, shape, dt):
        n = 1
        for s in shape[1:]:
            n *= s
        off = self.alloc(n * _dtsize(dt))
        return self.view(off, shape, dt)


C_ID = 0
C_SEL = 128
C_EVEN = 192
C_ODD = 193
C_NEGM = 194
C_SMASK = 322
C_TRI = 450
C_CH0 = 578
C_CH1 = 706
C_N = 834


def make_consts():
    c = np.zeros((128, C_N), np.float32)
    c[:, C_ID:C_ID + 128] = np.eye(128, dtype=np.float32)
    g = np.arange(128)
    c[g, C_SEL + g // 2] = 1.0
    c[:, C_EVEN] = (g % 2 == 0)
    c[:, C_ODD] = (g % 2 == 1)
    j = g[:, None]
    i = g[None, :]
    same = (j // 64) == (i // 64)
    c[:, C_NEGM:C_NEGM + 128] = np.where(same & (j <= i), 0.0, -1.0e4)
    c[:, C_SMASK:C_SMASK + 128] = (same & (j < i))
    c[:, C_TRI:C_TRI + 128] = (same & (j <= i))
    c[:, C_CH0:C_CH0 + 128] = (j < 64) & (i >= 0)
    c[:, C_CH1:C_CH1 + 128] = (j >= 64) & (i >= 0)
    return c


def build(seq, stage=2):
    assert seq % T == 0
    NCH = seq // T
    nc = bass.Bass("TRN2", target_bir_lowering=False)
    P = Prog()
    es = ExitStack()

    def din(name, shape):
        return nc.dram_tensor(name, shape, F32, kind="ExternalInput").ap()

    x_d = din("x", [seq, D])
    c_d = din("c", [8, 128])
    adaw_d = din("ada_w", [2, D, 3 * D])
    adab_d = din("ada_b", [1, 2 * 3 * D])
    nw_d = din("norm_w", [16, 128])
    s5win_d = din("s5_w_in", [D, 2 * E])
    lamr_d = din("s5_lambda_re", [128, 64])
    lami_d = din("s5_lambda_im", [128, 64])
    ldt_d = din("s5_log_dt", [128, 1])
    bre_d = din("s5_b_re", [128, 1024])
    bim_d = din("s5_b_im", [128, 1024])
    cre_d = din("s5_c_re", [128, 1024])
    cim_d = din("s5_c_im", [128, 1024])
    dsk_d = din("s5_d", [128, 16])
    wglu_d = din("s5_w_glu", [E, E])
    s5wo_d = din("s5_w_out", [E, D])
    gwin_d = din("gdn_w_in", [D, 6160])
    convw_d = din("gdn_conv_w", [128, 128])
    alog_d = din("gdn_a_log", [1, 8])
    dtb_d = din("gdn_dt_bias", [1, 8])
    gnw_d = din("gdn_norm_w", [1, 256])
    gwo_d = din("gdn_w_out", [E, D])
    fnw_d = din("final_norm_w", [1, D])
    cst_d = din("cst", [128, C_N])
    out_d = nc.dram_tensor("out", [seq, D], F32, kind="ExternalOutput").ap()
    t0_d = nc.dram_tensor("t0_scr", [128, 128, 128], BF16).ap()
    wv_d = nc.dram_tensor("wv_scr", [128, 128, 128], BF16).ap()
    wc_d = nc.dram_tensor("wc_scr", [2, 64, 64, 2, 128], BF16).ap()

    A = Arena(nc, es, 53208)
    psb = [es.enter_context(nc.psum_tensor("ps%d" % i, [128, 512], F32)) for i in range(8)]
    PS = [Buf("ps%d" % i) for i in range(8)]
    psrr = [0]

    pspool = [None]
    pscur = {}

    def nextps():
        if pspool[0] is None:
            i = psrr[0] % 8
            psrr[0] += 1
        else:
            key = tuple(pspool[0])
            c = pscur.get(key, 0)
            i = pspool[0][c % len(pspool[0])]
            pscur[key] = c + 1
        assert PS[i].w is None or PS[i].w[0] != "pe" or len(PS[i].r) > 0, ("psum bank still live", i)
        return i

    def mm(out, lhsT, rhs, start, stop, r, w):
        P.op("pe", lambda e: e.matmul(out, lhsT=lhsT, rhs=rhs, start=start, stop=stop), r=r, w=w)

    def act(out, in_, func, r, w, scale=1.0, bias=0.0, accum=None):
        if accum is None:
            P.op("act", lambda e: e.activation(out=out, in_=in_, func=func, bias=bias, scale=scale), r=r, w=w)
        else:
            P.op("act", lambda e: e.activation(out=out, in_=in_, func=func, bias=bias, scale=scale,
                                               accum_out=accum), r=r, w=w)

    def tt(eng, out, in0, in1, op, r, w):
        P.op(eng, lambda e: e.tensor_tensor(out=out, in0=in0, in1=in1, op=op), r=r, w=w)

    def ts(eng, out, in0, s1, op0, r, w, s2=None, op1=None):
        if op1 is None:
            P.op(eng, lambda e: e.tensor_scalar(out=out, in0=in0, scalar1=s1, scalar2=None, op0=op0), r=r, w=w)
        else:
            P.op(eng, lambda e: e.tensor_scalar(out=out, in0=in0, scalar1=s1, scalar2=s2, op0=op0, op1=op1),
                 r=r, w=w)

    def stt(out, in0, scalar, in1, op0, op1, r, w):
        P.op("dve", lambda e: e.scalar_tensor_tensor(out=out, in0=in0, scalar=scalar, in1=in1, op0=op0, op1=op1),
             r=r, w=w)

    def cp(eng, out, in_, r, w):
        if eng == "act":
            P.op("act", lambda e: e.activation(out=out, in_=in_, func=AF.Copy), r=r, w=w)
        else:
            P.op(eng, lambda e: e.tensor_copy(out=out, in_=in_), r=r, w=w)

    def memset(eng, ap, val, w):
        P.op(eng, lambda e: e.memset(ap, val), w=w)

    def recip(out, in_, r, w):
        P.op("dve", lambda e: e.reciprocal(out=out, in_=in_), r=r, w=w)

    def dma(q, out, in_, sem, r=(), w=(), is_out=False):
        P.dma(q, lambda e: e.dma_start(out=out, in_=in_), sem, r=r, w=w, is_out=is_out)

    CST = A.new([128, C_N], F32)
    B_CST = Buf("cst")
    ident = CST[:, C_ID:C_ID + 128]
    identb = A.new([128, 128], BF16)
    onesf = A.new([128, 128], F32)
    onesb = A.new([128, 128], BF16)
    B_K = Buf("konst")
    XS_OFF = A.alloc(NT * D * 4)
    XS = A.view(XS_OFF, [128, NT, D], F32)
    B_X = [Buf("x%d" % i) for i in range(NT)]
    HT_OFF = A.alloc(8 * T * 2)
    HT = A.view(HT_OFF, [128, 8, T], BF16)
    B_HT = [Buf("ht%d" % i) for i in range(NT)]
    BIG1 = A.alloc(32768)
    BIG2 = A.alloc(32768)
    WOFF = [A.alloc(16384), A.alloc(16384)]
    B_W = [Buf("w0"), Buf("w1")]
    wrr = [0]
    SMALL = A.alloc(8192 + 12288 + 4096)
    GATEB = A.new([128, 2, D], F32)
    FNWB = A.new([128, D], F32)
    WEFF = A.new([128, 2, 8], F32)
    SHIFT = A.new([128, 2, 8], F32)
    B_MOD = Buf("mod")
    AR2 = A.new([128, 2, 64], F32)
    AI2 = A.new([128, 2, 64], F32)
    S3 = [A.new([128, 3, 64], F32), A.new([128, 3, 64], F32)]
    B_S3 = [Buf("s3a"), Buf("s3b")]
    B_AR = Buf("ar")
    STAT = A.new([128, 64], F32)
    B_ST = Buf("stat")
    M1 = A.new([128, 2, 64], F32)
    M2 = A.new([128, 2, 64], F32)
    B_M1 = Buf("m1")
    B_M2 = Buf("m2")
    NTMP = A.new([128, 4, 128], F32)
    B_NTMP = Buf("ntmp")
    XN = [A.new([128, D], BF16), A.new([128, D], BF16)]
    B_XN = [Buf("xn0"), Buf("xn1")]

    def wslot():
        i = wrr[0] % 2
        wrr[0] += 1
        return i

    dma("sp", CST, cst_d, "cst", w=[B_CST])
    cp("dve", identb, ident, r=[B_CST], w=[B_K])
    memset("dve", onesf, 1.0, w=[B_K])
    memset("dve", onesb, 1.0, w=[B_K])
    memset("dve", S3[0], 0.0, w=[B_S3[0]])
    memset("dve", S3[1], 0.0, w=[B_S3[1]])

    so = [BIG2]

    def salloc(shape, dt):
        n = 1
        for s_ in shape[1:]:
            n *= s_
        nb = (n * _dtsize(dt) + 31) // 32 * 8
        v = A.view(so[0], shape, dt)
        so[0] += nb
        assert so[0] <= BIG2 + 8192, "setup scratch overflow"
        return v

    c8 = salloc([8, 128], F32)
    ccol = salloc([128, 8], F32)
    nwr = salloc([16, 128], F32)
    nwc = salloc([128, 16], F32)
    rowb = salloc([1, 512], F32)
    adab = salloc([1, 512], F32)
    scc = salloc([128, 16], F32)
    B_S = Buf("setup_s")
    B_RB = Buf("rowb")
    B_AB0 = Buf("adab")
    dma("sp", c8, c_d, "su1", w=[B_S])
    dma("sp", nwr, nw_d, "su1", w=[B_S])
    dma("sp", FNWB, fnw_d.partition_broadcast(128), "su1", w=[B_MOD])
    B_S.w = ("dma:su1", P.dcnt["su1"])
    B_MOD.w = ("dma:su1", P.dcnt["su1"])
    act(c8, c8, AF.Silu, r=[B_S], w=[B_S])
    i0 = nextps()
    mm(psb[i0][:, 0:8], c8, ident[0:8, 0:8], True, True, r=[B_S, B_CST], w=[PS[i0]])
    cp("dve", ccol, psb[i0][:, 0:8], r=[PS[i0]], w=[B_S])
    i0 = nextps()
    mm(psb[i0][:, 0:16], nwr, ident[0:16, 0:16], True, True, r=[B_S, B_CST], w=[PS[i0]])
    cp("dve", nwc, psb[i0][:, 0:16], r=[PS[i0]], w=[B_S])
    for l in range(2):
        for cb in range(6):
            s = wslot()
            wl = A.view(WOFF[s], [128, 8, 512], F32)
            dma("sp", wl, adaw_d[l, :, cb * 512:(cb + 1) * 512].rearrange("(k p) c -> p k c", p=128), "w%d" % s,
                w=[B_W[s]])
            o = l * 3 * D + cb * 512
            dma("sp", adab, adab_d[:, o:o + 512], "ab", w=[B_AB0])
            i0 = nextps()
            for k in range(8):
                mm(psb[i0][0:1, :], ccol[:, k:k + 1], wl[:, k, :], k == 0, k == 7, r=[B_S, B_W[s]], w=[PS[i0]])
            tt("dve", rowb, psb[i0][0:1, :], adab, ALU.add, r=[PS[i0], B_AB0], w=[B_RB])
            which = cb // 2
            if which < 2:
                i1 = nextps()
                for kk in range(4):
                    mm(psb[i1][:, kk:kk + 1], rowb[0:1, kk * 128:(kk + 1) * 128], onesf[0:1, 0:1], True, True,
                       r=[B_RB, B_K], w=[PS[i1]])
                k0 = (cb % 2) * 4
                if which == 0:
                    cp("dve", SHIFT[:, l, k0:k0 + 4], psb[i1][:, 0:4], r=[PS[i1]], w=[B_MOD])
                else:
                    ts("dve", scc[:, 0:4], psb[i1][:, 0:4], 1.0, ALU.add, r=[PS[i1]], w=[B_S])
                    tt("dve", WEFF[:, l, k0:k0 + 4], scc[:, 0:4], nwc[:, l * 8 + k0:l * 8 + k0 + 4], ALU.mult,
                       r=[B_S], w=[B_MOD])
            else:
                dh = cb % 2
                i1 = nextps()
                mm(psb[i1][:, :], onesf[0:1, :], rowb[0:1, :], True, True, r=[B_RB, B_K], w=[PS[i1]])
                ts("dve", GATEB[:, l, dh * 512:(dh + 1) * 512], psb[i1][:, :], 0.25 if l == 0 else 0.5, ALU.mult,
                   r=[PS[i1]], w=[B_MOD])

    class Reg:
        def __init__(self, base, nbytes):
            self.base = base
            self.off = 0
            self.cap = nbytes // 4

        def new(self, shape, dt):
            n = 1
            for s_ in shape[1:]:
                n *= s_
            nb = (n * _dtsize(dt) + 31) // 32 * 8
            v = A.view(self.base + self.off, shape, dt)
            self.off += nb
            assert self.off <= self.cap, "region overflow"
            return v

    r_small = Reg(SMALL, 8192 + 12288 + 4096)
    r_ht = Reg(HT_OFF, 16384)
    r_b1 = Reg(BIG1 + 4096, 16384)
    T0t = A.view(BIG2, [128, 128, 128], BF16)
    WVt = A.view(WOFF[0], [128, 128, 128], BF16)
    WCt = A.view(XS_OFF, [128, 128, 128], BF16)
    B_T0 = Buf("T0t")
    B_WV = Buf("WVt")
    B_WC = Buf("WCt")
    P.alias(B_T0, [B_S, B_RB, B_AB0])
    P.alias(B_WV, B_W)
    lamr = r_small.new([128, 64], F32)
    lami = r_small.new([128, 64], F32)
    ldt = r_small.new([128, 1], F32)
    dt64 = r_small.new([128, 1], F32)
    ar = r_small.new([128, 64], F32)
    ai = r_small.new([128, 64], F32)
    t1 = r_small.new([128, 64], F32)
    t2 = r_small.new([128, 64], F32)
    t3 = r_small.new([128, 64], F32)
    qre = r_small.new([128, 64], F32)
    qim = r_small.new([128, 64], F32)
    APR = r_small.new([128, 9, 64], F32)
    API = r_small.new([128, 9, 64], F32)
    dsk = r_small.new([128, 16], F32)
    Kd = r_small.new([128, 16, 16], F32)
    Kt = r_small.new([128, 4, 16], F32)
    lre = r_small.new([128, 128], F32)
    bre = r_ht.new([128, 64, 16], F32)
    bim = r_ht.new([128, 64, 16], F32)
    cre = r_ht.new([128, 16, 64], F32)
    cim = r_ht.new([128, 16, 64], F32)
    Gre = r_b1.new([128, 64, 16], F32)
    Gim = r_b1.new([128, 64, 16], F32)
    Gt = r_b1.new([128, 64, 16], F32)
    Gu = r_b1.new([128, 64, 16], F32)
    prodA = A.view(BIG1, [128, 4, 16, 64], F32)
    B_PA = Buf("prodA")
    KtP = r_small.new([128, 2, 16], F32)
    B_PP = Buf("prodP")
    B_KD0 = Buf("kd0")
    B_KD1 = Buf("kd1")
    B_G = Buf("G")
    B_P5 = Buf("s5p")
    for v, d_ in ((lamr, lamr_d), (lami, lami_d), (ldt, ldt_d), (dsk, dsk_d)):
        dma("sp", v, d_, "su4", w=[B_P5])
    dma("sp", bre, bre_d.rearrange("g (p m) -> g p m", m=16), "su4", w=[B_P5])
    dma("sp", bim, bim_d.rearrange("g (p m) -> g p m", m=16), "su4", w=[B_P5])
    dma("sp", cre, cre_d.rearrange("g (m p) -> g m p", p=64), "su4", w=[B_P5])
    dma("sp", cim, cim_d.rearrange("g (m p) -> g m p", p=64), "su4", w=[B_P5])
    cwr = r_small.new([128, 128], F32)
    dma("sp", cwr, convw_d, "su4", w=[B_P5])
    B_P5.w = ("dma:su4", P.dcnt["su4"])

    R5 = [B_P5]
    act(dt64, ldt, AF.Exp, r=R5, w=R5)
    ts("dve", dt64, dt64, 1.0 / 64.0, ALU.mult, r=R5, w=R5)
    act(t1, lamr, AF.Exp, r=R5, w=R5, scale=dt64[:, 0:1])
    act(ai, lami, AF.Sin, r=R5, w=R5, scale=dt64[:, 0:1])
    act(ar, lami, AF.Sin, r=R5, w=R5, scale=dt64[:, 0:1], bias=math.pi / 2)
    tt("dve", ar, ar, t1, ALU.mult, r=R5, w=R5)
    tt("dve", ai, ai, t1, ALU.mult, r=R5, w=R5)
    for _ in range(6):
        tt("dve", t1, ar, ar, ALU.mult, r=R5, w=R5)
        tt("dve", t2, ai, ai, ALU.mult, r=R5, w=R5)
        tt("dve", t3, ar, ai, ALU.mult, r=R5, w=R5)
        tt("dve", ar, t1, t2, ALU.subtract, r=R5, w=R5)
        ts("dve", ai, t3, 2.0, ALU.mult, r=R5, w=R5)
    tt("dve", t1, lamr, lamr, ALU.mult, r=R5, w=R5)
    tt("dve", t2, lami, lami, ALU.mult, r=R5, w=R5)
    tt("dve", t1, t1, t2, ALU.add, r=R5, w=R5)
    recip(t1, t1, r=R5, w=R5)
    ts("dve", t2, ar, -1.0, ALU.add, r=R5, w=R5)
    tt("dve", qre, t2, lamr, ALU.mult, r=R5, w=R5)
    tt("dve", t3, ai, lami, ALU.mult, r=R5, w=R5)
    tt("dve", qre, qre, t3, ALU.add, r=R5, w=R5)
    tt("dve", qre, qre, t1, ALU.mult, r=R5, w=R5)
    tt("dve", qim, ai, lamr, ALU.mult, r=R5, w=R5)
    tt("dve", t3, t2, lami, ALU.mult, r=R5, w=R5)
    tt("dve", qim, qim, t3, ALU.subtract, r=R5, w=R5)
    tt("dve", qim, qim, t1, ALU.mult, r=R5, w=R5)

    def bc_m(v):
        return v.unsqueeze(2).to_broadcast([128, 64, 16])

    tt("dve", Gre, bre, bc_m(qre), ALU.mult, r=R5, w=[B_G])
    tt("dve", Gt, bim, bc_m(qim), ALU.mult, r=R5, w=[B_G])
    tt("dve", Gre, Gre, Gt, ALU.subtract, r=[B_G], w=[B_G])
    tt("dve", Gim, bim, bc_m(qre), ALU.mult, r=R5, w=[B_G])
    tt("dve", Gt, bre, bc_m(qim), ALU.mult, r=R5, w=[B_G])
    tt("dve", Gim, Gim, Gt, ALU.add, r=[B_G], w=[B_G])
    memset("dve", APR[:, 0, :], 1.0, w=R5)
    memset("dve", API[:, 0, :], 0.0, w=R5)
    cp("dve", APR[:, 1, :], ar, r=R5, w=R5)
    cp("dve", API[:, 1, :], ai, r=R5, w=R5)
    for k in range(2, 9):
        tt("dve", t1, APR[:, k - 1, :], ar, ALU.mult, r=R5, w=R5)
        tt("dve", t2, API[:, k - 1, :], ai, ALU.mult, r=R5, w=R5)
        tt("dve", APR[:, k, :], t1, t2, ALU.subtract, r=R5, w=R5)
        tt("dve", t1, APR[:, k - 1, :], ai, ALU.mult, r=R5, w=R5)
        tt("dve", t2, API[:, k - 1, :], ar, ALU.mult, r=R5, w=R5)
        tt("dve", API[:, k, :], t1, t2, ALU.add, r=R5, w=R5)

    memset("pool", T0t, 0.0, w=[B_T0])
    Kd2 = Kd.rearrange("g a b -> g (a b)")
    for d_ in range(8):
        if d_ > 0:
            tt("dve", Gt, Gre, bc_m(ar), ALU.mult, r=[B_G] + R5, w=[B_G])
            tt("dve", Gu, Gim, bc_m(ai), ALU.mult, r=[B_G] + R5, w=[B_G])
            tt("dve", Gt, Gt, Gu, ALU.subtract, r=[B_G], w=[B_G])
            tt("dve", Gu, Gre, bc_m(ai), ALU.mult, r=[B_G] + R5, w=[B_G])
            cp("dve", Gre, Gt, r=[B_G], w=[B_G])
            tt("dve", Gt, Gim, bc_m(ar), ALU.mult, r=[B_G] + R5, w=[B_G])
            tt("dve", Gim, Gt, Gu, ALU.add, r=[B_G], w=[B_G])
        i_ = 7 - d_
        cp("act", WVt[:, i_ * 16:(i_ + 1) * 16, 0:64], Gre.rearrange("g p m -> g m p"), r=[B_G], w=[B_WV])
        cp("act", WVt[:, i_ * 16:(i_ + 1) * 16, 64:128], Gim.rearrange("g p m -> g m p"), r=[B_G], w=[B_WV])
        for sl in range(4):
            msl = slice(sl * 4, sl * 4 + 4)
            gre_b = Gre.rearrange("g p m -> g m p").unsqueeze(1).to_broadcast([128, 4, 16, 64])
            gim_b = Gim.rearrange("g p m -> g m p").unsqueeze(1).to_broadcast([128, 4, 16, 64])
            cre_b = cre[:, msl, :].unsqueeze(2).to_broadcast([128, 4, 16, 64])
            cim_b = cim[:, msl, :].unsqueeze(2).to_broadcast([128, 4, 16, 64])
            tt("dve", prodA, gre_b, cre_b, ALU.mult, r=[B_G] + R5, w=[B_PA])
            P.op("dve", lambda e, o=Kd[:, msl, :], i=prodA: e.tensor_reduce(
                out=o, in_=i, axis=AX.X, op=ALU.add), r=[B_PA], w=[B_KD0])
            tt("dve", prodA, gim_b, cim_b, ALU.mult, r=[B_G] + R5, w=[B_PA])
            P.op("dve", lambda e, o=Kt, i=prodA: e.tensor_reduce(
                out=o, in_=i, axis=AX.X, op=ALU.add), r=[B_PA], w=[B_KD0])
            tt("dve", Kd[:, msl, :], Kd[:, msl, :], Kt, ALU.subtract, r=[B_KD0], w=[B_KD0])
        if d_ == 0:
            tt("dve", Kd2[:, 0:256:17], Kd2[:, 0:256:17], dsk, ALU.add, r=[B_KD0] + R5, w=[B_KD0])
        for i2 in range(8 - d_):
            j2 = i2 + d_
            cp("act", T0t[:, i2 * 16:(i2 + 1) * 16, j2 * 16:(j2 + 1) * 16], Kd.rearrange("g m n -> g n m"),
               r=[B_KD0], w=[B_T0])
    creT = cre.rearrange("g m p -> g p m")
    cimT = cim.rearrange("g m p -> g p m")
    for j_ in range(8):
        pr = bc_m(APR[:, j_ + 1, :])
        pi_ = bc_m(API[:, j_ + 1, :])
        tt("dve", Gt, creT, pr, ALU.mult, r=R5 + [B_G], w=[B_G])
        tt("dve", Gu, cimT, pi_, ALU.mult, r=R5 + [B_G], w=[B_G])
        tt("dve", WCt[:, 0:64, j_ * 16:(j_ + 1) * 16], Gt, Gu, ALU.subtract, r=[B_G], w=[B_WC])
        tt("dve", Gt, creT, pi_, ALU.mult, r=R5 + [B_G], w=[B_G])
        tt("dve", Gu, cimT, pr, ALU.mult, r=R5 + [B_G], w=[B_G])
        tt("dve", Gt, Gt, Gu, ALU.add, r=[B_G], w=[B_G])
        ts("dve", WCt[:, 64:128, j_ * 16:(j_ + 1) * 16], Gt, -1.0, ALU.mult, r=[B_G], w=[B_WC])
    for r8 in range(8):
        rs_ = slice(r8 * 16, (r8 + 1) * 16)
        dma("sp", t0_d[rs_, :, :].rearrange("r g c -> g r c"), T0t[:, rs_, :], "scr", r=[B_T0])
        dma("sp", wv_d[rs_, :, :].rearrange("r g c -> g r c"), WVt[:, rs_, :], "scr", r=[B_WV])
    for g2 in range(2):
        for e_ in range(2):
            for ph in range(2):
                psl = slice(ph * 32, (ph + 1) * 32)
                dma("sp", wc_d[g2, psl, :, e_, :].rearrange("p gp c -> gp p c"),
                    WCt[g2:128:2, e_ * 64 + ph * 32:e_ * 64 + (ph + 1) * 32, :], "scr", r=[B_WC])
    B_SCR = Buf("scr")
    B_SCR.w = ("dma:scr", P.dcnt["scr"])
    for src, dst_is_im in ((APR[:, 8, :], False), (API[:, 8, :], True)):
        ts("dve", lre[:, 0:64], src, CST[:, C_EVEN:C_EVEN + 1], ALU.mult, r=R5 + [B_CST], w=[B_G])
        ts("dve", lre[:, 64:128], src, CST[:, C_ODD:C_ODD + 1], ALU.mult, r=R5 + [B_CST], w=[B_G])
        i0 = nextps()
        mm(psb[i0][:, 0:64], lre, CST[:, C_SEL:C_SEL + 64], True, True, r=[B_G, B_CST], w=[PS[i0]])
        if not dst_is_im:
            cp("dve", AR2[:, 0, :], psb[i0][:, 0:64], r=[PS[i0]], w=[B_AR])
            cp("dve", AR2[:, 1, :], psb[i0][:, 0:64], r=[PS[i0]], w=[B_AR])
        else:
            ts("dve", AI2[:, 0, :], psb[i0][:, 0:64], -1.0, ALU.mult, r=[PS[i0]], w=[B_AR])
            cp("dve", AI2[:, 1, :], psb[i0][:, 0:64], r=[PS[i0]], w=[B_AR])

    GS = A.new([128, 8, 256], F32)
    B_GS = [Buf("gs%d" % i) for i in range(8)]
    HIST = A.new([128, 32, 3], F32)
    B_HIST = Buf("hist")
    CW = A.new([128, 4, 32], F32)
    GNWB = A.new([128, 256], F32)
    NEXPA = A.new([128, 8], F32)
    DTBB = A.new([128, 8], F32)
    NEGONES = A.new([128, 128], F32)
    WAB = A.new([128, 8, 16], BF16)
    GSM = A.new([128, 12, 64], F32)
    B_GSM = Buf("gsm")
    B_GC = Buf("gconst")
    memset("pool", GS, 0.0, w=B_GS)
    memset("pool", HIST, 0.0, w=[B_HIST])
    memset("pool", NEGONES, -1.0, w=[B_GC])
    B_WAB = Buf("wab")
    dma("pool", WAB, gwin_d[:, 6144:6160].rearrange("(k p) c -> p k c", p=128), "suw", w=[B_WAB])
    dma("sp", GNWB, gnw_d.partition_broadcast(128), "su9", w=[B_GC])
    dma("sp", NEXPA, alog_d.partition_broadcast(128), "su9", w=[B_GC])
    dma("sp", DTBB, dtb_d.partition_broadcast(128), "su9", w=[B_GC])
    B_GC.w = ("dma:su9", P.dcnt["su9"])
    i0 = nextps()
    mm(psb[i0][:, 0:128], cwr, ident, True, True, r=[B_P5, B_CST], w=[PS[i0]])
    cp("dve", CW.rearrange("p k t -> p (k t)"), psb[i0][:, 0:128], r=[PS[i0]], w=[B_GC])
    act(NEXPA, NEXPA, AF.Exp, r=[B_GC], w=[B_GC])
    ts("dve", NEXPA, NEXPA, -1.0, ALU.mult, r=[B_GC], w=[B_GC])

    P.barrier()

    UY = A.view(BIG1, [128, 128, 128], BF16)
    B_UY = [Buf("uy%d" % i) for i in range(16)]
    VS = A.view(BIG2, [128, 2, 64, 128], BF16)
    B_VS = Buf("vs")
    YFM = A.view(BIG2, [128, 16, T], BF16)
    B_YFM = [Buf("yfm%d" % i) for i in range(16)]
    Y2 = A.view(BIG1, [128, 16, T], BF16)
    B_Y2 = [Buf("y2%d" % i) for i in range(16)]
    ABAT = A.view(SMALL, [128, 32, 8, 16], BF16)
    B_AB = Buf("abat")
    TBLO = [SMALL + 2048, SMALL + 2048 + 1536]
    T0s = [A.view(o, [128, 8, 128], BF16) for o in TBLO]
    WVs = [A.view(o + 512, [128, 8, 128], BF16) for o in TBLO]
    WCt2 = [A.view(o + 1024, [128, 4, 2, 128], BF16) for o in TBLO]
    B_TB = [Buf("tbl0"), Buf("tbl1")]
    tbrr = [0]
    TMPO = SMALL + 2048 + 3072
    TMP1 = A.view(TMPO, [128, 512], F32)
    TMP2 = A.view(TMPO + 512, [128, 512], BF16)
    TMP3 = A.view(TMPO + 768, [128, 512], BF16)
    B_T1 = Buf("tmp1")
    B_T2 = Buf("tmp2")
    B_T3 = Buf("tmp3")
    B_VSn = [Buf("vsn0"), Buf("vsn1")]
    evq = [0]

    def evac_eng():
        evq[0] += 1
        return "act" if evq[0] % 2 else "dve"

    def norm_transpose(layer):
        for tti in range(NT):
            xt = XS[:, tti, :]
            s = tti % 2
            junk = XN[s]
            act(junk, xt, AF.Square, r=[B_X[tti]], w=[B_XN[s], B_ST], accum=STAT[:, tti:tti + 1])
            act(STAT[:, 8 + tti:9 + tti], STAT[:, tti:tti + 1], AF.Ln, r=[B_ST], w=[B_ST], scale=1.0 / D,
                bias=EPS)
            act(STAT[:, 16 + tti:17 + tti], STAT[:, 8 + tti:9 + tti], AF.Exp, r=[B_ST], w=[B_ST], scale=-0.5)
            ts("dve", XN[s], xt, STAT[:, 16 + tti:17 + tti], ALU.mult, r=[B_X[tti], B_ST], w=[B_XN[s]])
            for half in range(2):
                i0 = nextps()
                for kk in range(4):
                    k = half * 4 + kk
                    mm(psb[i0][:, kk * 128:(kk + 1) * 128], XN[s][:, k * 128:(k + 1) * 128], identb, True, True,
                       r=[B_XN[s], B_K], w=[PS[i0]])
                if half == 0:
                    for kk in range(4):
                        k = half * 4 + kk
                        act(HT[:, k, tti * 128:(tti + 1) * 128], psb[i0][:, kk * 128:(kk + 1) * 128], AF.Identity,
                            r=[PS[i0], B_MOD], w=[B_HT[tti]], scale=WEFF[:, layer, k:k + 1],
                            bias=SHIFT[:, layer, k:k + 1])
                else:
                    hv = HT[:, 4:8, tti * 128:(tti + 1) * 128]
                    tt("dve", NTMP, psb[i0][:, :].rearrange("p (k n) -> p k n", n=128),
                       WEFF[:, layer, 4:8].unsqueeze(2).to_broadcast([128, 4, 128]), ALU.mult, r=[PS[i0], B_MOD],
                       w=[B_NTMP])
                    tt("dve", hv, NTMP, SHIFT[:, layer, 4:8].unsqueeze(2).to_broadcast([128, 4, 128]), ALU.add,
                       r=[B_NTMP, B_MOD], w=[B_HT[tti]])

    def wload(view, src, s):
        dma("pool", view, src, "w%d" % s, w=[B_W[s]])

    def s5_layer(ch):
        P.alias(B_VS, B_YFM)
        for b in B_UY:
            P.alias(b, B_Y2)
        norm_transpose(0)
        for fb in range(4):
            s = wslot()
            Wb = A.view(WOFF[s], [128, 8, 512], BF16)
            wload(Wb, s5win_d[:, fb * 512:(fb + 1) * 512].rearrange("(k p) c -> p k c", p=128), s)
            for i_ in range(8):
                i0 = nextps()
                for k in range(8):
                    mm(psb[i0][:, :], HT[:, k, i_:T:8], Wb[:, k, :], k == 0, k == 7, r=B_HT + [B_W[s]], w=[PS[i0]])
                cp(evac_eng(), ABAT[:, :, i_, :], psb[i0][:, :].rearrange("p (g m) -> p g m", m=16), r=[PS[i0]],
                   w=[B_AB])
            for q4 in range(8):
                i0 = nextps()
                for gg in range(4):
                    gl = q4 * 4 + gg
                    mm(psb[i0][:, gg * 128:(gg + 1) * 128], ABAT[:, gl, :, :].rearrange("p i m -> p (i m)"),
                       identb, True, True, r=[B_AB, B_K], w=[PS[i0]])
                g0 = fb * 32 + q4 * 4
                cp(evac_eng(), UY[:, g0:g0 + 4, :], psb[i0][:, :].rearrange("p (g n) -> p g n", n=128), r=[PS[i0]],
                   w=[B_UY[g0 // 8]])
            for fcl in range(4):
                fc = fb * 4 + fcl
                tb = tbrr[0] % 2
                tbrr[0] += 1
                dma("sp", WVs[tb], wv_d[:, fc * 8:(fc + 1) * 8, :], "tbl%d" % tb, r=[B_SCR],
                    w=[B_TB[tb]])
                ire = nextps()
                iim = nextps()
                for gg in range(8):
                    g = fc * 8 + gg
                    p0 = (g % 2) * 64
                    pr = gg // 2
                    for (ii, c0) in ((ire, 0), (iim, 64)):
                        mm(psb[ii][p0:p0 + 64, pr * 128:(pr + 1) * 128], WVs[tb][:, gg, c0:c0 + 64], UY[:, g, :],
                           True, True, r=[B_TB[tb], B_UY[g // 8]], w=[PS[ii]])
                cp(evac_eng(), VS[:, 0, fc * 4:(fc + 1) * 4, :], psb[ire][:, :].rearrange("p (a n) -> p a n", n=128),
                   r=[PS[ire]], w=[B_VS])
                cp(evac_eng(), VS[:, 1, fc * 4:(fc + 1) * 4, :], psb[iim][:, :].rearrange("p (a n) -> p a n", n=128),
                   r=[PS[iim]], w=[B_VS])
        cur = 0
        for n in range(128):
            So = S3[cur]
            Sn = S3[1 - cur]
            Bo = B_S3[cur]
            Bn = B_S3[1 - cur]
            Bv = B_VSn[n % 2]
            tt("dve", M1, So[:, 0:2, :], AR2, ALU.mult, r=[Bo, B_AR], w=[B_M1])
            tt("dve", M2, So[:, 1::-1, :], AI2, ALU.mult, r=[Bo, B_AR], w=[B_M2])
            tt("dve", M1, M1, VS[:, :, :, n], ALU.add, r=[B_M1, B_VS, Bv], w=[B_M1])
            tt("dve", Sn[:, 0:2, :], M1, M2, ALU.add, r=[B_M1, B_M2], w=[Bn])
            cp("act", VS[:, :, :, n], So[:, 0:2, :], r=[Bo], w=[Bv])
            cur = 1 - cur
        for fc in range(16):
            tb = tbrr[0] % 2
            tbrr[0] += 1
            dma("sp", T0s[tb], t0_d[:, fc * 8:(fc + 1) * 8, :], "tbl%d" % tb, r=[B_SCR],
                w=[B_TB[tb]])
            for g2 in range(2):
                dma("sp", WCt2[tb][g2 * 64:(g2 + 1) * 64, :, :, :], wc_d[g2, :, fc * 4:(fc + 1) * 4, :, :],
                    "tbl%d" % tb, r=[B_SCR], w=[B_TB[tb]])
            banks = []
            for hb in range(2):
                i0 = nextps()
                banks.append(i0)
                for gg in range(4):
                    gl = hb * 4 + gg
                    g = fc * 8 + gl
                    p0 = (g % 2) * 64
                    pr = g // 2
                    o = psb[i0][:, gg * 128:(gg + 1) * 128]
                    RV = [B_VS, B_VSn[0], B_VSn[1], B_TB[tb]]
                    mm(o, UY[:, g, :], T0s[tb][:, gl, :], True, False, r=[B_UY[fc], B_TB[tb]], w=[PS[i0]])
                    mm(o, VS[p0:p0 + 64, 0, pr, :], WCt2[tb][p0:p0 + 64, gl // 2, 0, :], False, False, r=RV,
                       w=[PS[i0]])
                    mm(o, VS[p0:p0 + 64, 1, pr, :], WCt2[tb][p0:p0 + 64, gl // 2, 1, :], False, True, r=RV,
                       w=[PS[i0]])
            yav = UY[:, fc * 8:(fc + 1) * 8, :].rearrange("p g c -> p (g c)").rearrange(
                "p (j g m) -> p g j m", j=8, g=8, m=16)
            for hb in range(2):
                i0 = banks[hb]
                act(yav[:, hb * 4:(hb + 1) * 4, :, :],
                    psb[i0][:, :].rearrange("p (g j m) -> p g j m", g=4, j=8, m=16), AF.Gelu, r=[PS[i0]],
                    w=[B_UY[fc]])
        for b in B_YFM:
            P.alias(b, [B_VS, B_VSn[0], B_VSn[1]])
        for fc in range(16):
            for jh in range(2):
                i0 = nextps()
                for jj in range(4):
                    j_ = jh * 4 + jj
                    mm(psb[i0][:, jj * 128:(jj + 1) * 128], UY[:, fc * 8 + j_, :], identb,
                       True, True, r=[B_UY[fc], B_K], w=[PS[i0]])
                cp(evac_eng(), YFM[:, fc, :].rearrange("p (n j) -> p j n", j=8)[:, jh * 4:(jh + 1) * 4, :],
                   psb[i0][:, :].rearrange("p (j n) -> p j n", n=128), r=[PS[i0]], w=[B_YFM[fc]])
        for b in B_Y2:
            P.alias(b, B_UY)
        for fb in range(4):
            s = wslot()
            Wg = A.view(WOFF[s], [128, 16, 512], BF16)
            wload(Wg, wglu_d[:, fb * 512:(fb + 1) * 512].rearrange("(k p) c -> p k c", p=128), s)
            s2_ = wslot()
            Wz = A.view(WOFF[s2_], [128, 8, 512], BF16)
            wload(Wz, s5win_d[:, E + fb * 512:E + (fb + 1) * 512].rearrange("(k p) c -> p k c", p=128), s2_)
            for ftl in range(4):
                ft = fb * 4 + ftl
                for th in range(2):
                    tsl = slice(th * 512, (th + 1) * 512)
                    ig = nextps()
                    for k in range(16):
                        mm(psb[ig][:, :], Wg[:, k, ftl * 128:(ftl + 1) * 128], YFM[:, k, tsl], k == 0, k == 15,
                           r=[B_W[s]] + B_YFM, w=[PS[ig]])
                    iz = nextps()
                    for k in range(8):
                        mm(psb[iz][:, :], Wz[:, k, ftl * 128:(ftl + 1) * 128], HT[:, k, tsl], k == 0, k == 7,
                           r=[B_W[s2_]] + B_HT, w=[PS[iz]])
                    act(TMP2, psb[ig][:, :], AF.Tanh, r=[PS[ig]], w=[B_T2], scale=0.5)
                    act(TMP3, psb[iz][:, :], AF.Tanh, r=[PS[iz]], w=[B_T3], scale=0.5)
                    stt(TMP2, TMP2, 1.0, YFM[:, ft, tsl], ALU.add, ALU.mult, r=[B_T2, B_YFM[ft]], w=[B_T2])
                    stt(TMP3, TMP3, 1.0, psb[iz][:, :], ALU.add, ALU.mult, r=[B_T3, PS[iz]], w=[B_T3])
                    tt("dve", Y2[:, ft, tsl], TMP2, TMP3, ALU.mult, r=[B_T2, B_T3], w=[B_Y2[ft]])
        out_proj(s5wo_d, 0)

    def out_proj(w_d, layer):
        for dh in range(2):
            s = wslot()
            Wo = A.view(WOFF[s], [128, 16, 512], BF16)
            wload(Wo, w_d[:, dh * 512:(dh + 1) * 512].rearrange("(k p) c -> p k c", p=128), s)
            for tti in range(NT):
                i0 = nextps()
                for k in range(16):
                    mm(psb[i0][:, :], Y2[:, k, tti * 128:(tti + 1) * 128], Wo[:, k, :], k == 0, k == 15,
                       r=B_Y2 + [B_W[s]], w=[PS[i0]])
                tt("dve", TMP1, psb[i0][:, :], GATEB[:, layer, dh * 512:(dh + 1) * 512], ALU.mult,
                   r=[PS[i0], B_MOD], w=[B_T1])
                tt("dve", XS[:, tti, dh * 512:(dh + 1) * 512], XS[:, tti, dh * 512:(dh + 1) * 512], TMP1, ALU.add,
                   r=[B_T1, B_X[tti]], w=[B_X[tti]])

    def final_norm(ch):
        t0 = ch * T
        for tti in range(NT):
            xt = XS[:, tti, :]
            s = tti % 2
            act(XN[s], xt, AF.Square, r=[B_X[tti]], w=[B_XN[s], B_ST], accum=STAT[:, tti:tti + 1])
            act(STAT[:, 8 + tti:9 + tti], STAT[:, tti:tti + 1], AF.Ln, r=[B_ST], w=[B_ST], scale=1.0 / D, bias=EPS)
            act(STAT[:, 16 + tti:17 + tti], STAT[:, 8 + tti:9 + tti], AF.Exp, r=[B_ST], w=[B_ST], scale=-0.5)
            stt(xt, xt, STAT[:, 16 + tti:17 + tti], FNWB, ALU.mult, ALU.mult, r=[B_X[tti], B_ST, B_MOD],
                w=[B_X[tti]])
            dma("sp", out_d[t0 + tti * 128:t0 + (tti + 1) * 128, :], XS[:, tti, :], "o%d" % tti, r=[B_X[tti]],
                is_out=True)
            if ch + 1 < NCH:
                t1_ = t0 + T
                dma("sp", XS[:, tti, :], x_d[t1_ + tti * 128:t1_ + (tti + 1) * 128, :], "x%d" % tti, w=[B_X[tti]])


    TH = 256
    NTH = TH // 128
    NQ = T // TH

    def mkstream(i):
        rgs = Reg(BIG2 + i * 4096, 16384)
        rss = Reg(SMALL + i * 1280, 5120)
        d = {}
        d["QN"] = rgs.new([128, TH], BF16)
        d["KN"] = rgs.new([128, TH], BF16)
        d["VT"] = rgs.new([128, NTH, 256], BF16)
        d["KD"] = rgs.new([128, NTH, 128], BF16)
        d["MT"] = rgs.new([128, NTH, 128], BF16)
        d["QKM"] = rgs.new([128, NTH, 128], BF16)
        d["SZ"] = rgs.new([128, 2, TH], BF16)
        d["VA"] = rgs.new([128, 2, TH], BF16)
        d["DG"] = rgs.new([128, NTH, 128], F32)
        d["BM"] = rgs.new([128, NTH, 128], BF16)
        ov = rgs.base + rgs.off
        rd_ = Reg(ov, 8192)
        for nm in ("ET", "NP", "NT", "P0", "P1", "PT0", "PT1", "RF"):
            d[nm] = rd_.new([128, NTH, 128], F32)
        ra_ = Reg(ov, 8192)
        for j in range(2):
            d["PRE%d" % j] = ra_.new([128, TH + 8], BF16)
            d["CT%d" % j] = ra_.new([128, TH], F32)
            d["TA%d" % j] = ra_.new([128, TH], F32)
            d["SQ%d" % j] = ra_.new([128, TH], BF16)
        d["DW"] = A.view(SMALL + 2560 + i * 1024, [128, 16, 128], BF16)
        d["RM"] = rss.new([128, 256], BF16)
        d["VN"] = rss.new([128, 256], BF16)
        d["OV"] = rss.new([128, 256], F32)
        d["OT0"] = rss.new([128, 256], F32)
        d["OT1"] = rss.new([128, 256], F32)
        d["ON"] = rss.new([128, 256], BF16)
        d["SB"] = rss.new([128, 256], BF16)
        for k_ in list(d.keys()):
            d["B_" + k_] = Buf("%s_%d" % (k_, i))
        d["RA_B"] = [d["B_" + n] for n in ("PRE0", "CT0", "TA0", "SQ0", "PRE1", "CT1", "TA1", "SQ1")]
        d["RD_B"] = [d["B_" + n] for n in ("ET", "NP", "NT", "P0", "P1", "PT0", "PT1", "RF")]
        d["sid"] = i
        return d

    GST = [mkstream(0), mkstream(1)]
    B_WS = [[Buf("ws%d_%d" % (i, j)) for j in range(4)] for i in range(2)]

    def gsm(i):
        return GSM[:, i, :].rearrange("p (a h) -> p a h", h=8)

    BETA, GG, GC, GLT, EGC, NEGEGC, EKD, EGL0, EGL1, GTMP = [gsm(i) for i in range(10)]
    ABR = GSM[:, 10:12, :].rearrange("p a c -> p (a c)").rearrange("p (t c) -> p t c", c=16)

    def vn(ap):
        return ap.rearrange("p (a i) -> p a i", i=128)

    idn = ident.unsqueeze(1).to_broadcast([128, NTH, 128])
    smn = CST[:, C_SMASK:C_SMASK + 128].unsqueeze(1).to_broadcast([128, NTH, 128])
    ngn = CST[:, C_NEGM:C_NEGM + 128].unsqueeze(1).to_broadcast([128, NTH, 128])
    WH = {}
    W2 = NTH * 128

    def gdn_gates():
        i0 = nextps()
        for tti in range(NT):
            for k in range(8):
                mm(psb[i0][:, tti * 16:(tti + 1) * 16], HT[:, k, tti * 128:(tti + 1) * 128], WAB[:, k, :], k == 0,
                   k == 7, r=[B_HT[tti], B_WAB], w=[PS[i0]])
        G = [B_GSM]
        cp("dve", ABR, psb[i0][:, 0:128].rearrange("p (t c) -> p t c", c=16), r=[PS[i0]], w=G)
        act(BETA, ABR[:, :, 0:8], AF.Tanh, r=G, w=G, scale=0.5)
        ts("dve", BETA, BETA, 0.5, ALU.mult, r=G, w=G, s2=0.5, op1=ALU.add)
        tt("dve", GTMP, ABR[:, :, 8:16], DTBB.unsqueeze(1).to_broadcast([128, 8, 8]), ALU.add, r=G + [B_GC], w=G)
        act(GTMP, GTMP, AF.Exp, r=G, w=G)
        act(GTMP, GTMP, AF.Ln, r=G, w=G, bias=1.0)
        tt("dve", GG, GTMP, NEXPA.unsqueeze(1).to_broadcast([128, 8, 8]), ALU.mult, r=G + [B_GC], w=G)
        i0 = nextps()
        grhs = GSM[:, 1, :]
        mm(psb[i0][:, 0:64], CST[:, C_TRI:C_TRI + 128], grhs, True, True, r=G + [B_CST], w=[PS[i0]])
        mm(psb[i0][0:64, 64:128], CST[:, C_CH0:C_CH0 + 64], grhs, True, True, r=G + [B_CST], w=[PS[i0]])
        mm(psb[i0][64:128, 64:128], CST[:, C_CH1 + 64:C_CH1 + 128], grhs, True, True, r=G + [B_CST], w=[PS[i0]])
        mm(psb[i0][:, 128:192], CST[:, C_CH0:C_CH0 + 128], grhs, True, True, r=G + [B_CST], w=[PS[i0]])
        mm(psb[i0][:, 192:256], CST[:, C_CH1:C_CH1 + 128], grhs, True, True, r=G + [B_CST], w=[PS[i0]])
        cp("dve", GSM[:, 2, :], psb[i0][:, 0:64], r=[PS[i0]], w=G)
        act(GSM[:, 4, :], psb[i0][:, 0:64], AF.Exp, r=[PS[i0]], w=G)
        ts("dve", GSM[:, 5, :], GSM[:, 4, :], -1.0, ALU.mult, r=G, w=G)
        tt("dve", GSM[:, 3, :], psb[i0][:, 64:128], GSM[:, 2, :], ALU.subtract, r=[PS[i0]] + G, w=G)
        act(GSM[:, 6, :], GSM[:, 3, :], AF.Exp, r=G, w=G)
        act(GSM[:, 7, :], psb[i0][:, 128:192], AF.Exp, r=[PS[i0]], w=G)
        act(GSM[:, 8, :], psb[i0][:, 192:256], AF.Exp, r=[PS[i0]], w=G)

    def gdn_prep(h, qi, S):
        sid = S["sid"]
        tb0 = qi * NTH
        hs = slice(qi * TH, (qi + 1) * TH)
        if qi == 0:
            s = sid
            Wh = A.view(WOFF[s], [128, 8, 768], BF16)
            c0 = 0
            for j, (src0, n_) in enumerate(((h * 128, 128), (1024 + h * 128, 128), (2048 + h * 256, 256),
                                           (4096 + h * 256, 256))):
                dma("pool", Wh[:, :, c0:c0 + n_], gwin_d[:, src0:src0 + n_].rearrange("(k p) c -> p k c", p=128),
                    "ws%d_%d" % (s, j), w=[B_WS[s][j]])
                c0 += n_
            WH[h] = (Wh, s)
            yield
            for jt, tidx_ in enumerate((h, 8 + h, 16 + 2 * h, 17 + 2 * h)):
                for tap in range(4):
                    ts("pool", S["DW"][:, jt * 4 + tap, :], ident, CW[:, tap, tidx_:tidx_ + 1], ALU.mult,
                       r=[B_CST, B_GC], w=[S["B_DW"]])
                yield
        Wh, s = WH[h]
        for b in S["RA_B"]:
            P.alias(b, S["RD_B"])
        pairs = ((("q", 0, h, 0, 0), ("k", 128, 8 + h, 0, 1)),
                 (("v", 256, 16 + 2 * h, 0, 2), ("v", 384, 17 + 2 * h, 1, 2)))
        for pair in pairs:
            pi = []
            for j, (name, wc0, tidx, sub, seg) in enumerate(pair):
                i0 = nextps()
                for k in range(8):
                    mm(psb[i0][:, 0:TH], Wh[:, k, wc0:wc0 + 128], HT[:, k, hs], k == 0, k == 7,
                       r=[B_WS[s][seg]] + B_HT, w=[PS[i0]])
                pi.append(i0)
                yield
            pcs = []
            for j, (name, wc0, tidx, sub, seg) in enumerate(pair):
                PRE, B_PRE = S["PRE%d" % j], S["B_PRE%d" % j]
                cp("act", PRE[:, 3:3 + TH], psb[pi[j]][:, 0:TH], r=[PS[pi[j]]], w=[B_PRE])
                cp("pool", PRE[:, 0:3], HIST[:, tidx, :], r=[B_HIST], w=[B_PRE])
                yield
            for j, (name, wc0, tidx, sub, seg) in enumerate(pair):
                PRE, B_PRE = S["PRE%d" % j], S["B_PRE%d" % j]
                jt = (0 if name == "q" else 1) if name != "v" else 2 + sub
                ic = nextps()
                for tap in range(4):
                    mm(psb[ic][:, 0:TH], S["DW"][:, jt * 4 + tap, :], PRE[:, tap:tap + TH], tap == 0, tap == 3,
                       r=[S["B_DW"], B_PRE], w=[PS[ic]])
                cp("pool", HIST[:, tidx, :], PRE[:, TH:TH + 3], r=[B_PRE], w=[B_HIST])
                pcs.append(ic)
                yield
            for j, (name, wc0, tidx, sub, seg) in enumerate(pair):
                CT, B_CT = S["CT%d" % j], S["B_CT%d" % j]
                TA, B_TA = S["TA%d" % j], S["B_TA%d" % j]
                act(TA, psb[pcs[j]][:, 0:TH], AF.Tanh, r=[PS[pcs[j]]], w=[B_TA], scale=0.5)
                yield
                if name == "v":
                    stt(S["VA"][:, sub, :], TA, 1.0, psb[pcs[j]][:, 0:TH], ALU.add, ALU.mult,
                        r=[B_TA, PS[pcs[j]]], w=[S["B_VA"]])
                else:
                    stt(CT, TA, 1.0, psb[pcs[j]][:, 0:TH], ALU.add, ALU.mult, r=[B_TA, PS[pcs[j]]], w=[B_CT])
                yield
            if pair[0][0] == "v":
                continue
            ips = []
            for j in range(2):
                CT, B_CT = S["CT%d" % j], S["B_CT%d" % j]
                SQ, B_SQ = S["SQ%d" % j], S["B_SQ%d" % j]
                act(SQ, CT, AF.Square, r=[B_CT], w=[B_SQ])
                i0 = nextps()
                mm(psb[i0][:, 0:TH], onesb, SQ, True, True, r=[B_SQ, B_K], w=[PS[i0]])
                ips.append(i0)
                yield
            for j in range(2):
                CT, B_CT = S["CT%d" % j], S["B_CT%d" % j]
                TA, B_TA = S["TA%d" % j], S["B_TA%d" % j]
                act(TA, psb[ips[j]][:, 0:TH], AF.Ln, r=[PS[ips[j]]], w=[B_TA], bias=4.0 * EPS)
                act(TA, TA, AF.Exp, r=[B_TA], w=[B_TA], scale=-0.5)
                if j == 0:
                    stt(S["QN"], CT, 128.0 ** -0.5, TA, ALU.mult, ALU.mult, r=[B_CT, B_TA], w=[S["B_QN"]])
                else:
                    stt(S["KN"], CT, 1.0, TA, ALU.mult, ALU.mult, r=[B_CT, B_TA], w=[S["B_KN"]])
                yield
        for sub in range(2):
            wc0 = 512 + sub * 128
            i0 = nextps()
            for k in range(8):
                mm(psb[i0][:, 0:TH], Wh[:, k, wc0:wc0 + 128], HT[:, k, hs], k == 0, k == 7,
                   r=[B_WS[s][3]] + B_HT, w=[PS[i0]])
            TA, B_TA = S["TA%d" % sub], S["B_TA%d" % sub]
            act(TA, psb[i0][:, 0:TH], AF.Tanh, r=[PS[i0]], w=[B_TA], scale=0.5)
            stt(S["SZ"][:, sub, :], TA, 1.0, psb[i0][:, 0:TH], ALU.add, ALU.mult, r=[B_TA, PS[i0]], w=[S["B_SZ"]])
            yield
        KN_ = S["KN"]
        QN_ = S["QN"]
        i0 = nextps()
        for tl in range(NTH):
            mm(psb[i0][:, tl * 128:(tl + 1) * 128], KN_[:, tl * 128:(tl + 1) * 128], identb, True, True,
               r=[S["B_KN"], B_K], w=[PS[i0]])
        tt("dve", S["KD"], vn(psb[i0][:, 0:W2]), EKD[:, tb0:tb0 + NTH, h:h + 1].to_broadcast([128, NTH, 128]),
           ALU.mult, r=[PS[i0], B_GSM], w=[S["B_KD"]])
        yield
        i0 = nextps()
        for tl in range(NTH):
            for vt in range(2):
                o = tl * 256 + vt * 128
                mm(psb[i0][:, o:o + 128], S["VA"][:, vt, tl * 128:(tl + 1) * 128], identb, True, True,
                   r=[S["B_VA"], B_K], w=[PS[i0]])
        act(S["VT"], psb[i0][:, :].rearrange("p (a e) -> p a e", e=256), AF.Identity, r=[PS[i0]], w=[S["B_VT"]],
            scale=0.5)
        yield
        for b in S["RD_B"]:
            P.alias(b, S["RA_B"])
        DG, BM = S["DG"], S["BM"]
        tt("dve", DG, idn, GC[:, tb0:tb0 + NTH, h:h + 1].to_broadcast([128, NTH, 128]), ALU.mult,
           r=[B_CST, B_GSM], w=[S["B_DG"]])
        tt("dve", BM, smn, BETA[:, tb0:tb0 + NTH, h:h + 1].to_broadcast([128, NTH, 128]), ALU.mult,
           r=[B_CST, B_GSM], w=[S["B_BM"]])
        yield
        ikk = nextps()
        iqk = nextps()
        igd = nextps()
        for tl in range(NTH):
            cs = slice(tl * 128, (tl + 1) * 128)
            mm(psb[ikk][:, cs], KN_[:, cs], KN_[:, cs], True, True, r=[S["B_KN"]], w=[PS[ikk]])
            mm(psb[iqk][:, cs], KN_[:, cs], QN_[:, cs], True, True, r=[S["B_KN"], S["B_QN"]], w=[PS[iqk]])
            mm(psb[igd][:, cs], onesf, DG[:, tl, :], True, False, r=[B_K, S["B_DG"]], w=[PS[igd]])
            mm(psb[igd][:, cs], DG[:, tl, :], NEGONES, False, False, r=[S["B_DG"], B_GC], w=[PS[igd]])
            mm(psb[igd][:, cs], ident, CST[:, C_NEGM:C_NEGM + 128], False, True, r=[B_CST], w=[PS[igd]])
        yield
        ET, NP_, NT_, RF = S["ET"], S["NP"], S["NT"], S["RF"]
        Pb = [S["P0"], S["P1"]]
        PTb = [S["PT0"], S["PT1"]]
        B_PB = [S["B_P0"], S["B_P1"]]
        B_PTB = [S["B_PT0"], S["B_PT1"]]
        act(ET, vn(psb[igd][:, 0:W2]), AF.Exp, r=[PS[igd]], w=[S["B_ET"]])
        yield
        tt("dve", S["QKM"], vn(psb[iqk][:, 0:W2]), ET, ALU.mult, r=[PS[iqk], S["B_ET"]], w=[S["B_QKM"]])
        tt("dve", NP_, vn(psb[ikk][:, 0:W2]), ET, ALU.mult, r=[PS[ikk], S["B_ET"]], w=[S["B_NP"]])
        yield
        tt("dve", NP_, NP_, BM, ALU.mult, r=[S["B_NP"], S["B_BM"]], w=[S["B_NP"]])
        yield
        i0 = nextps()
        for tl in range(NTH):
            o = slice(tl * 128, (tl + 1) * 128)
            mm(psb[i0][:, o], NP_[:, tl, :], ident, True, True, r=[S["B_NP"], B_CST], w=[PS[i0]])
        cp("act", NT_, vn(psb[i0][:, 0:W2]), r=[PS[i0]], w=[S["B_NT"]])
        tt("dve", RF, idn, NP_, ALU.subtract, r=[B_CST, S["B_NP"]], w=[S["B_RF"]])
        yield
        ip = nextps()
        ipt = nextps()
        for tl in range(NTH):
            o = slice(tl * 128, (tl + 1) * 128)
            mm(psb[ip][:, o], NT_[:, tl, :], NP_[:, tl, :], True, True, r=[S["B_NT"], S["B_NP"]], w=[PS[ip]])
            mm(psb[ipt][:, o], NP_[:, tl, :], NT_[:, tl, :], True, True, r=[S["B_NT"], S["B_NP"]], w=[PS[ipt]])
        cur = 0
        cp("act", Pb[0], vn(psb[ip][:, 0:W2]), r=[PS[ip]], w=[B_PB[0]])
        cp("dve", PTb[0], vn(psb[ipt][:, 0:W2]), r=[PS[ipt]], w=[B_PTB[0]])
        yield
        for k in range(1, 6):
            iq = nextps()
            for tl in range(NTH):
                o = slice(tl * 128, (tl + 1) * 128)
                mm(psb[iq][:, o], PTb[cur][:, tl, :], RF[:, tl, :], True, True, r=[B_PTB[cur], S["B_RF"]],
                   w=[PS[iq]])
            if k < 5:
                ip = nextps()
                ipt = nextps()
                for tl in range(NTH):
                    o = slice(tl * 128, (tl + 1) * 128)
                    mm(psb[ip][:, o], PTb[cur][:, tl, :], Pb[cur][:, tl, :], True, True,
                       r=[B_PTB[cur], B_PB[cur]], w=[PS[ip]])
                    mm(psb[ipt][:, o], Pb[cur][:, tl, :], PTb[cur][:, tl, :], True, True,
                       r=[B_PTB[cur], B_PB[cur]], w=[PS[ipt]])
            yield
            tt("dve", RF, RF, vn(psb[iq][:, 0:W2]), ALU.add, r=[S["B_RF"], PS[iq]], w=[S["B_RF"]])
            if k < 5:
                cp("act", Pb[1 - cur], vn(psb[ip][:, 0:W2]), r=[PS[ip]], w=[B_PB[1 - cur]])
                cp("dve", PTb[1 - cur], vn(psb[ipt][:, 0:W2]), r=[PS[ipt]], w=[B_PTB[1 - cur]])
                cur = 1 - cur
            else:
                cp("act", S["MT"], RF, r=[S["B_RF"]], w=[S["B_MT"]])
            yield

    def gdn_loop(h, qi, S):
        sid = S["sid"]
        tb0 = qi * NTH
        KN_, QN_, VT_, KD_, MT_, QKM_, SZ_ = S["KN"], S["QN"], S["VT"], S["KD"], S["MT"], S["QKM"], S["SZ"]
        RM, VN, OV, ON, SB = S["RM"], S["VN"], S["OV"], S["ON"], S["SB"]
        B_RM, B_VN, B_OV, B_ON, B_SB = S["B_RM"], S["B_VN"], S["B_OV"], S["B_ON"], S["B_SB"]
        st0 = 32 + 4 * sid
        cp("act", SB, GS[:, h, :], r=[B_GS[h]], w=[B_SB])
        for cidx in range(2 * NTH):
            tl = cidx // 2
            t_ = tb0 + tl
            hf = cidx % 2
            p0 = hf * 64
            cs = slice(tl * 128 + p0, tl * 128 + p0 + 64)
            pp = slice(p0, p0 + 64)
            iks = nextps()
            mm(psb[iks][pp, 0:256], KN_[:, cs], SB, True, True, r=[S["B_KN"], B_SB], w=[PS[iks]])
            iqs = nextps()
            mm(psb[iqs][pp, 0:256], QN_[:, cs], SB, True, True, r=[S["B_QN"], B_SB], w=[PS[iqs]])
            yield
            stt(RM[pp, :], psb[iks][pp, 0:256], NEGEGC[pp, t_, h:h + 1], VT_[pp, tl, :], ALU.mult, ALU.add,
                r=[PS[iks], B_GSM, S["B_VT"]], w=[B_RM])
            yield
            ivn = nextps()
            mm(psb[ivn][pp, 0:256], MT_[pp, tl, p0:p0 + 64], RM[pp, :], True, True, r=[S["B_MT"], B_RM],
               w=[PS[ivn]])
            yield
            act(VN[pp, :], psb[ivn][pp, 0:256], AF.Identity, r=[PS[ivn], B_GSM], w=[B_VN],
                scale=BETA[pp, t_, h:h + 1])
            yield
            isu = nextps()
            mm(psb[isu][:, 0:256], KD_[pp, tl, :], VN[pp, :], True, True, r=[S["B_KD"], B_VN], w=[PS[isu]])
            iqv = nextps()
            mm(psb[iqv][pp, 0:256], QKM_[pp, tl, p0:p0 + 64], VN[pp, :], True, True, r=[S["B_QKM"], B_VN],
               w=[PS[iqv]])
            yield
            egl = EGL0 if hf == 0 else EGL1
            stt(GS[:, h, :], GS[:, h, :], egl[:, t_, h:h + 1], psb[isu][:, 0:256], ALU.mult, ALU.add,
                r=[B_GS[h], B_GSM, PS[isu]], w=[B_GS[h]])
            cp("act", OV[pp, :], psb[iqv][pp, 0:256], r=[PS[iqv]], w=[B_OV])
            yield
            cp("act", SB, GS[:, h, :], r=[B_GS[h]], w=[B_SB])
            Ot = S["OT%d" % (tl % 2)]
            B_Ot = S["B_OT%d" % (tl % 2)]
            stt(Ot[pp, :], psb[iqs][pp, 0:256], EGC[pp, t_, h:h + 1], OV[pp, :], ALU.mult, ALU.add,
                r=[PS[iqs], B_GSM, B_OV], w=[B_Ot])
            yield
            if hf == 1:
                act(ON, Ot, AF.Square, r=[B_Ot], w=[B_ON, B_ST], accum=STAT[:, st0:st0 + 1])
                yield
                act(STAT[:, st0 + 1:st0 + 2], STAT[:, st0:st0 + 1], AF.Ln, r=[B_ST], w=[B_ST], scale=1.0 / 256,
                    bias=EPS)
                act(STAT[:, st0 + 2:st0 + 3], STAT[:, st0 + 1:st0 + 2], AF.Exp, r=[B_ST], w=[B_ST], scale=-0.5)
                yield
                stt(ON, Ot, STAT[:, st0 + 2:st0 + 3], GNWB, ALU.mult, ALU.mult, r=[B_Ot, B_ST, B_GC], w=[B_ON])
                yield
                i0 = nextps()
                for et in range(2):
                    mm(psb[i0][:, et * 128:(et + 1) * 128], ON[:, et * 128:(et + 1) * 128], identb, True, True,
                       r=[B_ON, B_K], w=[PS[i0]])
                yield
                tt("dve", Y2[:, 2 * h:2 * h + 2, t_ * 128:(t_ + 1) * 128],
                   psb[i0][:, 0:256].rearrange("p (a n) -> p a n", n=128), SZ_[:, :, tl * 128:(tl + 1) * 128],
                   ALU.mult, r=[PS[i0], S["B_SZ"]], w=[B_Y2[2 * h], B_Y2[2 * h + 1]])
                yield

    def head_gen(h, S):
        for qi in range(NQ):
            yield from gdn_prep(h, qi, S)
            yield from gdn_loop(h, qi, S)

    def gdn_layer(ch):
        P.barrier()
        norm_transpose(1)
        gdn_gates()
        for hp in range(4):
            gens = [head_gen(2 * hp, GST[0]), head_gen(2 * hp + 1, GST[1])]
            alive = [True, True]
            pspool[0] = [0, 1, 2, 3]
            for _ in range(OFFSET):
                try:
                    next(gens[0])
                except StopIteration:
                    alive[0] = False
                    break
            if LOCKSTEP == 0:
                for g_ in gens:
                    for _ in g_:
                        pass
                alive = [False, False]
            while alive[0] or alive[1]:
                for gi in range(2):
                    pspool[0] = [0, 1, 2, 3] if gi == 0 else [4, 5, 6, 7]
                    for _rep in range(max(1, LOCKSTEP)):
                        if alive[gi]:
                            try:
                                next(gens[gi])
                            except StopIteration:
                                alive[gi] = False
            pspool[0] = None
        P.barrier()
        out_proj(gwo_d, 1)
        P.barrier()

    for ch in range(NCH):
        t0 = ch * T
        if ch == 0 or stage < 2:
            for tti in range(NT):
                dma("sp", XS[:, tti, :], x_d[t0 + tti * 128:t0 + (tti + 1) * 128, :], "x%d" % tti, w=[B_X[tti]])
        s5_layer(ch)
        if stage >= 2:
            gdn_layer(ch)
            final_norm(ch)
        else:
            for tti in range(NT):
                dma("sp", out_d[t0 + tti * 128:t0 + (tti + 1) * 128, :], XS[:, tti, :], "o%d" % tti,
                    r=[B_X[tti]], is_out=True)

    P.emit(nc, es)
    es.close()
    return nc


def make_in_maps(inputs, seq, ncores):
    f = lambda a: np.ascontiguousarray(np.asarray(a, dtype=np.float32))
    shared = {
        "ada_w": f(inputs["ada_w"]),
        "ada_b": f(inputs["ada_b"]).reshape(1, 6 * D),
        "norm_w": f(inputs["norm_w"]).reshape(16, 128),
        "s5_w_in": f(inputs["s5_w_in"])[0],
        "s5_lambda_re": f(inputs["s5_lambda_re"])[0],
        "s5_lambda_im": f(inputs["s5_lambda_im"])[0],
        "s5_log_dt": f(inputs["s5_log_dt"])[0].reshape(128, 1),
        "s5_b_re": f(inputs["s5_b_re"])[0].reshape(128, 1024),
        "s5_b_im": f(inputs["s5_b_im"])[0].reshape(128, 1024),
        "s5_c_re": f(inputs["s5_c_re"])[0].reshape(128, 1024),
        "s5_c_im": f(inputs["s5_c_im"])[0].reshape(128, 1024),
        "s5_d": f(inputs["s5_d"])[0].reshape(128, 16),
        "s5_w_glu": f(inputs["s5_w_glu"])[0],
        "s5_w_out": f(inputs["s5_w_out"])[0],
        "gdn_w_in": f(inputs["gdn_w_in"])[0],
        "gdn_conv_w": f(inputs["gdn_conv_w"])[0].reshape(128, 128),
        "gdn_a_log": f(inputs["gdn_a_log"]).reshape(1, 8),
        "gdn_dt_bias": f(inputs["gdn_dt_bias"]).reshape(1, 8),
        "gdn_norm_w": f(inputs["gdn_norm_w"]).reshape(1, 256),
        "gdn_w_out": f(inputs["gdn_w_out"])[0],
        "final_norm_w": f(inputs["final_norm_w"]).reshape(1, D),
        "cst": make_consts(),
    }
    x = f(inputs["x"])
    c = f(inputs["c"])
    maps = []
    for b in range(ncores):
        m = dict(shared)
        m["x"] = np.ascontiguousarray(x[b, :seq])
        m["c"] = np.ascontiguousarray(c[b].reshape(8, 128))
        maps.append(m)
    return maps


_NC_CACHE = {}


def kernel(**inputs):
    x = np.asarray(inputs["x"])
    nb, seq, _ = x.shape
    key = (seq, 2)
    if key not in _NC_CACHE:
        _NC_CACHE[key] = build(seq, 2)
    nc = _NC_CACHE[key]
    maps = make_in_maps(inputs, seq, nb)
    res = run_bass_kernel_spmd(nc, maps, core_ids=list(range(nb)))
    out = np.stack([np.asarray(r["out"], dtype=np.float32) for r in res.results], axis=0)
    return out
```
